# Optimizing a Trainium2 kernel written in Bass

```python
import math
import jax, jax.numpy as jnp
from jax import lax
import numpy as np

D_MODEL = 2048
BATCH = 1
SEQ = 8192
DEPTH = 4

HEAD_DIM = 128
ROPE_THETA = 10000.0
NORM_EPS = 1e-6
D_FF = 4 * D_MODEL
N_MIX_HEADS = D_MODEL // HEAD_DIM
MOBA_HEADS = N_MIX_HEADS // 2
MOBA_BLOCK = 256
MOBA_TOPK = 3
SWA_HEADS = N_MIX_HEADS - MOBA_HEADS
SWA_KV_HEADS = 2
SWA_WINDOW = 128
FOX_HEADS = N_MIX_HEADS // 2
FOX_Q_BLOCK = 128
NSA_HEADS = N_MIX_HEADS - FOX_HEADS
NSA_KV_HEADS = 2
NSA_CMP_LEN = 32
NSA_CMP_STRIDE = 16
NSA_SLC_BLOCK = 64
NSA_SLC_TOPK = 16
NSA_WINDOW = 512
BAND_BLOCK = 128
Q_CHUNK = 64
NEG_INF = -1e30
N_EVEN = (DEPTH + 1) // 2
N_ODD = DEPTH // 2

EVEN_WIDTHS = (MOBA_HEADS * HEAD_DIM,) * 3 + (SWA_HEADS * HEAD_DIM, SWA_KV_HEADS * HEAD_DIM, SWA_KV_HEADS * HEAD_DIM)
ODD_WIDTHS = ((FOX_HEADS * HEAD_DIM,) * 3 + (FOX_HEADS, NSA_HEADS * HEAD_DIM)
              + (NSA_KV_HEADS * HEAD_DIM,) * 6 + (NSA_HEADS * 3,))
EVEN_IN = sum(EVEN_WIDTHS)
ODD_IN = sum(ODD_WIDTHS)
MIX_OUT = N_MIX_HEADS * HEAD_DIM

kernel_name = 'hybrid_moba_swa_fox_nsa_trunk'


def rms_norm(x, g):
    xf = x.astype(jnp.float32)
    y = xf * lax.rsqrt(jnp.mean(xf * xf, axis=-1, keepdims=True) + NORM_EPS)
    return (y * g.astype(jnp.float32)).astype(x.dtype)


def rope_tables(seq, dim):
    inv = 1.0 / (ROPE_THETA ** (jnp.arange(0, dim, 2, dtype=jnp.float32) / dim))
    ang = jnp.arange(seq, dtype=jnp.float32)[:, None] * inv[None, :]
    return jnp.cos(ang), jnp.sin(ang)


def apply_rope(x, cos, sin):
    x1, x2 = jnp.split(x.astype(jnp.float32), 2, axis=-1)
    c = cos[None, :, None, :]
    s = sin[None, :, None, :]
    return jnp.concatenate([x1 * c - x2 * s, x1 * s + x2 * c], axis=-1).astype(x.dtype)


def masked_softmax(s, mask):
    s = jnp.where(mask, s, NEG_INF)
    m = jnp.max(s, axis=-1, keepdims=True)
    p = jnp.where(mask, jnp.exp(s - m), 0.0)
    return p / jnp.maximum(jnp.sum(p, axis=-1, keepdims=True), 1e-30)


def split_cols(a, widths):
    cuts = [int(v) for v in np.cumsum(widths)[:-1]]
    return jnp.split(a, cuts, axis=-1)


def heads(a, n):
    return a.reshape(a.shape[0], a.shape[1], n, HEAD_DIM)


def banded_attention(q, k, v, window, sinks):
    B, S, Hq, hd = q.shape
    Hkv = k.shape[2]
    G = Hq // Hkv
    nb = S // BAND_BLOCK
    n_prev = -(-(window - 1) // BAND_BLOCK)
    pad = n_prev * BAND_BLOCK

    def band(a):
        ap = jnp.pad(a, ((0, 0), (pad, 0), (0, 0), (0, 0))).reshape(B, nb + n_prev, BAND_BLOCK, Hkv, hd)
        return jnp.concatenate([ap[:, i:i + nb] for i in range(n_prev + 1)], axis=2)

    kb, vb = band(k), band(v)
    qb = q.reshape(B, nb, BAND_BLOCK, Hkv, G, hd)
    s = jnp.einsum('bnqhgd,bnkhd->bnhgqk', qb, kb, preferred_element_type=jnp.float32) * (hd ** -0.5)
    t = jnp.arange(nb)[:, None] * BAND_BLOCK + jnp.arange(BAND_BLOCK)[None, :]
    spos = (jnp.arange(nb)[:, None] - n_prev) * BAND_BLOCK + jnp.arange((n_prev + 1) * BAND_BLOCK)[None, :]
    mask = (spos[:, None, :] <= t[:, :, None]) & (spos[:, None, :] > t[:, :, None] - window) & (spos[:, None, :] >= 0)
    mask = mask[None, :, None, None]
    s = jnp.where(mask, s, NEG_INF)
    m = jnp.max(s, axis=-1, keepdims=True)
    if sinks is not None:
        sk = sinks.astype(jnp.float32).reshape(1, 1, Hkv, G, 1, 1)
        m = jnp.maximum(m, sk)
    p = jnp.where(mask, jnp.exp(s - m), 0.0)
    den = jnp.sum(p, axis=-1, keepdims=True)
    if sinks is not None:
        den = den + jnp.exp(sk - m)
    p = p / den
    o = jnp.einsum('bnhgqk,bnkhd->bnqhgd', p.astype(v.dtype), vb)
    return o.reshape(B, S, Hq, hd)


def moba_attention(q, k, v):
    B, S, H, hd = q.shape
    nb = -(-S // MOBA_BLOCK)
    sp = nb * MOBA_BLOCK
    kpad = jnp.pad(k, ((0, 0), (0, sp - S), (0, 0), (0, 0)))
    vpad = jnp.pad(v, ((0, 0), (0, sp - S), (0, 0), (0, 0)))
    kbt = kpad.reshape(B, nb, MOBA_BLOCK, H, hd).transpose(0, 3, 1, 2, 4)
    vbt = vpad.reshape(B, nb, MOBA_BLOCK, H, hd).transpose(0, 3, 1, 2, 4)
    kmean = jnp.mean(kbt.astype(jnp.float32), axis=3)
    topk = min(MOBA_TOPK, nb)
    scale = hd ** -0.5
    n_chunks = S // Q_CHUNK
    qc = q.reshape(B, n_chunks, Q_CHUNK, H, hd).transpose(1, 0, 2, 3, 4)
    bi = jnp.arange(B)[:, None, None, None]
    hi = jnp.arange(H)[None, None, :, None]
    blk = jnp.arange(nb)

    def step(args):
        ci, qi = args
        t = ci * Q_CHUNK + jnp.arange(Q_CHUNK)
        own = (ci * Q_CHUNK) // MOBA_BLOCK
        gate = jnp.einsum('bqhd,bhnd->bqhn', qi.astype(jnp.float32), kmean)
        gate = jnp.where(blk < own, gate, NEG_INF)
        _, idx = lax.top_k(gate, topk)
        valid = idx < own
        ks = kbt[bi, hi, idx]
        vs = vbt[bi, hi, idx]
        s_sel = jnp.einsum('bqhd,bqhnkd->bqhnk', qi, ks, preferred_element_type=jnp.float32) * scale
        s_sel = jnp.where(valid[..., None], s_sel, NEG_INF).reshape(B, Q_CHUNK, H, topk * MOBA_BLOCK)
        ko = lax.dynamic_slice_in_dim(kpad, own * MOBA_BLOCK, MOBA_BLOCK, axis=1)
        vo = lax.dynamic_slice_in_dim(vpad, own * MOBA_BLOCK, MOBA_BLOCK, axis=1)
        s_own = jnp.einsum('bqhd,bkhd->bqhk', qi, ko, preferred_element_type=jnp.float32) * scale
        kpos = own * MOBA_BLOCK + jnp.arange(MOBA_BLOCK)
        s_own = jnp.where((kpos[None, :] <= t[:, None])[None, :, None, :], s_own, NEG_INF)
        p = jax.nn.softmax(jnp.concatenate([s_sel, s_own], axis=-1), axis=-1)
        p_sel = p[..., :topk * MOBA_BLOCK].reshape(B, Q_CHUNK, H, topk, MOBA_BLOCK).astype(v.dtype)
        p_own = p[..., topk * MOBA_BLOCK:].astype(v.dtype)
        return jnp.einsum('bqhnk,bqhnkd->bqhd', p_sel, vs) + jnp.einsum('bqhk,bkhd->bqhd', p_own, vo)

    o = lax.map(step, (jnp.arange(n_chunks), qc))
    return o.transpose(1, 0, 2, 3, 4).reshape(B, S, H, hd)


def forgetting_attention(q, k, v, log_f):
    B, S, H, hd = q.shape
    scale = hd ** -0.5
    cum = jnp.cumsum(log_f, axis=1).transpose(0, 2, 1)
    n_blk = S // FOX_Q_BLOCK
    qb = q.reshape(B, n_blk, FOX_Q_BLOCK, H, hd).transpose(1, 0, 2, 3, 4)
    kpos = jnp.arange(S)

    def step(args):
        bi, qi = args
        t = bi * FOX_Q_BLOCK + jnp.arange(FOX_Q_BLOCK)
        cq = lax.dynamic_slice_in_dim(cum, bi * FOX_Q_BLOCK, FOX_Q_BLOCK, axis=2)
        s = jnp.einsum('bqhd,bkhd->bhqk', qi, k, preferred_element_type=jnp.float32) * scale
        s = s + cq[..., :, None] - cum[..., None, :]
        p = masked_softmax(s, kpos[None, :] <= t[:, None])
        return jnp.einsum('bhqk,bkhd->bqhd', p.astype(v.dtype), v)

    o = lax.map(step, (jnp.arange(n_blk), qb))
    return o.transpose(1, 0, 2, 3, 4).reshape(B, S, H, hd)


def nsa_compress(a, pos, w1, w2):
    B, S, H, hd = a.shape
    n_cmp = (S - NSA_CMP_LEN) // NSA_CMP_STRIDE + 1
    idx = jnp.arange(n_cmp)[:, None] * NSA_CMP_STRIDE + jnp.arange(NSA_CMP_LEN)[None, :]
    blocks = a[:, idx] + pos[:, None, :]
    flat = blocks.transpose(0, 1, 3, 2, 4).reshape(B, n_cmp, H, NSA_CMP_LEN * hd)
    return jax.nn.gelu(flat @ w1) @ w2


def nsa_compressed_selected(q, k_cmp, v_cmp, k_slc, v_slc):
    B, S, Hq, hd = q.shape
    Hkv = k_slc.shape[2]
    G = Hq // Hkv
    n_cmp = k_cmp.shape[1]
    n_slc = S // NSA_SLC_BLOCK
    topk = min(NSA_SLC_TOPK, n_slc)
    scale = hd ** -0.5
    cmp_end = jnp.arange(n_cmp) * NSA_CMP_STRIDE + NSA_CMP_LEN - 1
    c_start = jnp.arange(n_cmp)[:, None] * NSA_CMP_STRIDE
    s_start = jnp.arange(n_slc)[None, :] * NSA_SLC_BLOCK
    overlap = ((c_start < s_start + NSA_SLC_BLOCK) & (c_start + NSA_CMP_LEN > s_start)).astype(jnp.float32)
    ksb = k_slc.reshape(B, n_slc, NSA_SLC_BLOCK, Hkv, hd).transpose(0, 3, 1, 2, 4)
    vsb = v_slc.reshape(B, n_slc, NSA_SLC_BLOCK, Hkv, hd).transpose(0, 3, 1, 2, 4)
    bi = jnp.arange(B)[:, None, None, None]
    hi = jnp.arange(Hkv)[None, :, None, None]
    blk = jnp.arange(n_slc)
    n_chunks = S // Q_CHUNK
    qc = q.reshape(B, n_chunks, Q_CHUNK, Hkv, G, hd).transpose(1, 0, 2, 3, 4, 5)

    def step(args):
        ci, qi = args
        t = ci * Q_CHUNK + jnp.arange(Q_CHUNK)
        s_c = jnp.einsum('bqhgd,bnhd->bhgqn', qi, k_cmp, preferred_element_type=jnp.float32) * scale
        p_c = masked_softmax(s_c, cmp_end[None, :] <= t[:, None])
        o_c = jnp.einsum('bhgqn,bnhd->bqhgd', p_c.astype(v_cmp.dtype), v_cmp)
        imp = jnp.einsum('bhgqn,nj->bhqj', p_c, overlap)
        cur = (t // NSA_SLC_BLOCK)[:, None]
        forced = (blk == 0) | (blk == cur) | (blk == cur - 1)
        imp = jnp.where(forced, jnp.inf, imp)
        imp = jnp.where(blk > cur, -jnp.inf, imp)
        _, sel = lax.top_k(imp, topk)
        ks = ksb[bi, hi, sel]
        vs = vsb[bi, hi, sel]
        s_s = jnp.einsum('bqhgd,bhqnkd->bhgqnk', qi, ks, preferred_element_type=jnp.float32) * scale
        kpos = sel[..., None] * NSA_SLC_BLOCK + jnp.arange(NSA_SLC_BLOCK)
        m_s = (kpos <= t[:, None, None])[:, :, None]
        p_s = masked_softmax(s_s.reshape(B, Hkv, G, Q_CHUNK, topk * NSA_SLC_BLOCK),
                             m_s.reshape(B, Hkv, 1, Q_CHUNK, topk * NSA_SLC_BLOCK)).reshape(s_s.shape)
        o_s = jnp.einsum('bhgqnk,bhqnkd->bqhgd', p_s.astype(vs.dtype), vs)
        return o_c, o_s

    o_c, o_s = lax.map(step, (jnp.arange(n_chunks), qc))
    back = lambda o: o.transpose(1, 0, 2, 3, 4, 5).reshape(B, S, Hq, hd)
    return back(o_c), back(o_s)


def even_mixer(h, w_in, sinks, w_out, cos, sin):
    B, S, _ = h.shape
    qa, ka, va, qb, kb, vb = split_cols(h @ w_in, EVEN_WIDTHS)
    qa = apply_rope(heads(qa, MOBA_HEADS), cos, sin)
    ka = apply_rope(heads(ka, MOBA_HEADS), cos, sin)
    qb = apply_rope(heads(qb, SWA_HEADS), cos, sin)
    kb = apply_rope(heads(kb, SWA_KV_HEADS), cos, sin)
    oa = moba_attention(qa, ka, heads(va, MOBA_HEADS))
    ob = banded_attention(qb, kb, heads(vb, SWA_KV_HEADS), SWA_WINDOW, sinks)
    o = jnp.concatenate([oa.reshape(B, S, -1), ob.reshape(B, S, -1)], axis=-1)
    return o @ w_out


def odd_mixer(h, w_in, forget_b, k_pos, k_w1, k_w2, v_pos, v_w1, v_w2, w_out, cos, sin):
    B, S, _ = h.shape
    qc, kc, vc, fc, qd, kcmp, vcmp, kslc, vslc, kwin, vwin, gd = split_cols(h @ w_in, ODD_WIDTHS)
    log_f = jax.nn.log_sigmoid(fc.astype(jnp.float32) + forget_b.astype(jnp.float32))
    oc = forgetting_attention(heads(qc, FOX_HEADS), heads(kc, FOX_HEADS), heads(vc, FOX_HEADS), log_f)
    qd = apply_rope(heads(qd, NSA_HEADS), cos, sin)
    k_c = nsa_compress(apply_rope(heads(kcmp, NSA_KV_HEADS), cos, sin), k_pos, k_w1, k_w2)
    v_c = nsa_compress(heads(vcmp, NSA_KV_HEADS), v_pos, v_w1, v_w2)
    o_cmp, o_slc = nsa_compressed_selected(qd, k_c, v_c,
                                           apply_rope(heads(kslc, NSA_KV_HEADS), cos, sin),
                                           heads(vslc, NSA_KV_HEADS))
    o_win = banded_attention(qd, apply_rope(heads(kwin, NSA_KV_HEADS), cos, sin),
                             heads(vwin, NSA_KV_HEADS), NSA_WINDOW, None)
    g = jax.nn.sigmoid(gd.reshape(B, S, NSA_HEADS, 3).astype(jnp.float32)).astype(qd.dtype)
    od = g[..., 0:1] * o_cmp + g[..., 1:2] * o_slc + g[..., 2:3] * o_win
    o = jnp.concatenate([oc.reshape(B, S, -1), od.reshape(B, S, -1)], axis=-1)
    return o @ w_out


def setup_inputs(seed: int = 0) -> dict:
    key = jax.random.key(seed)
    ks = jax.random.split(key, 21)
    nrm = lambda k, shape, scale: jax.random.normal(k, shape, jnp.float32) * scale
    L, hd = NSA_CMP_LEN, HEAD_DIM
    return {
        'x': nrm(ks[0], (BATCH, SEQ, D_MODEL), 1.0),
        'c': nrm(ks[1], (BATCH, D_MODEL), 1.0),
        'norm_mix_g': 1.0 + nrm(ks[2], (DEPTH, D_MODEL), 0.02),
        'norm_mlp_g': 1.0 + nrm(ks[3], (DEPTH, D_MODEL), 0.02),
        'ada_w': nrm(ks[4], (DEPTH, D_MODEL, 6 * D_MODEL), 0.5 * D_MODEL ** -0.5),
        'ada_b': nrm(ks[5], (DEPTH, 6 * D_MODEL), 0.02),
        'mlp_up': nrm(ks[6], (DEPTH, D_MODEL, D_FF), D_MODEL ** -0.5),
        'mlp_down': nrm(ks[7], (DEPTH, D_FF, D_MODEL), D_FF ** -0.5),
        'even_w_in': nrm(ks[8], (N_EVEN, D_MODEL, EVEN_IN), D_MODEL ** -0.5),
        'even_sinks': nrm(ks[9], (N_EVEN, SWA_HEADS), 0.5),
        'even_w_out': nrm(ks[10], (N_EVEN, MIX_OUT, D_MODEL), MIX_OUT ** -0.5),
        'odd_w_in': nrm(ks[11], (N_ODD, D_MODEL, ODD_IN), D_MODEL ** -0.5),
        'fox_forget_b': 3.0 + nrm(ks[12], (N_ODD, FOX_HEADS), 0.5),
        'nsa_k_pos': nrm(ks[13], (N_ODD, L, hd), 0.1),
        'nsa_k_w1': nrm(ks[14], (N_ODD, L * hd, hd), (L * hd) ** -0.5),
        'nsa_k_w2': nrm(ks[15], (N_ODD, hd, hd), hd ** -0.5),
        'nsa_v_pos': nrm(ks[16], (N_ODD, L, hd), 0.1),
        'nsa_v_w1': nrm(ks[17], (N_ODD, L * hd, hd), (L * hd) ** -0.5),
        'nsa_v_w2': nrm(ks[18], (N_ODD, hd, hd), hd ** -0.5),
        'odd_w_out': nrm(ks[19], (N_ODD, MIX_OUT, D_MODEL), MIX_OUT ** -0.5),
        'final_norm_g': 1.0 + nrm(ks[20], (D_MODEL,), 0.02),
    }


def reference(x, c, norm_mix_g, norm_mlp_g, ada_w, ada_b, mlp_up, mlp_down, even_w_in, even_sinks,
              even_w_out, odd_w_in, fox_forget_b, nsa_k_pos, nsa_k_w1, nsa_k_w2, nsa_v_pos, nsa_v_w1,
              nsa_v_w2, odd_w_out, final_norm_g):
    S = x.shape[1]
    cos, sin = rope_tables(S, HEAD_DIM)
    cond = jax.nn.silu(c)
    for i in range(DEPTH):
        mod = cond @ ada_w[i] + ada_b[i]
        sh1, sc1, g1, sh2, sc2, g2 = [m[:, None, :] for m in jnp.split(mod, 6, axis=-1)]
        h = rms_norm(x, norm_mix_g[i]) * (1.0 + sc1) + sh1
        if i % 2 == 0:
            j = i // 2
            y = even_mixer(h, even_w_in[j], even_sinks[j], even_w_out[j], cos, sin)
        else:
            j = i // 2
            y = odd_mixer(h, odd_w_in[j], fox_forget_b[j], nsa_k_pos[j], nsa_k_w1[j], nsa_k_w2[j],
                          nsa_v_pos[j], nsa_v_w1[j], nsa_v_w2[j], odd_w_out[j], cos, sin)
        x = x + g1 * y
        h = rms_norm(x, norm_mlp_g[i]) * (1.0 + sc2) + sh2
        x = x + g2 * (jnp.square(jax.nn.relu(h @ mlp_up[i])) @ mlp_down[i])
    return rms_norm(x, final_norm_g)
```

```python
import contextlib
import numpy as np
import ml_dtypes
import concourse.bass as bass
import concourse.mybir as mybir
from concourse.bass_utils import run_bass_kernel_spmd

F32 = mybir.dt.float32
BF16 = mybir.dt.bfloat16
AF = mybir.ActivationFunctionType
ALU = mybir.AluOpType
AX = mybir.AxisListType

NCORE = 8
D = 2048
SEQ = 8192
TL = 1024
NCH = 16
DFF = 8192
DEPTH = 4
SCALE = 128.0 ** -0.5
NEG = -30000.0
EPS = 1e-6
ENGS = ("tensor", "vector", "scalar", "gpsimd", "sync")


class Sched:
    def __init__(self, same_engine_sync=True):
        self.ops = []
        self.buf = {}
        self.same_engine_sync = same_engine_sync
        self.ext_keys = set()
        self.ext_ops = []

    def op(self, eng, fn, reads=(), writes=(), kind="c", dsem=None):
        deps = set()
        for k in reads:
            st = self.buf.setdefault(k, [None, []])
            if st[0] is not None:
                deps.add(st[0])
        for k in writes:
            st = self.buf.setdefault(k, [None, []])
            if st[0] is not None:
                deps.add(st[0])
            deps.update(st[1])
        i = len(self.ops)
        self.ops.append(dict(eng=eng, fn=fn, deps=deps, kind=kind, dsem=dsem))
        if kind == "d" and any(k in self.ext_keys for k in writes):
            self.ext_ops.append(i)
        for k in reads:
            self.buf[k][1].append(i)
        for k in writes:
            self.buf[k] = [i, []]
        return i

    def pe(self, fn, reads=(), writes=()):
        return self.op("tensor", fn, reads, writes)

    def dve(self, fn, reads=(), writes=()):
        return self.op("vector", fn, reads, writes)

    def act(self, fn, reads=(), writes=()):
        return self.op("scalar", fn, reads, writes)

    def pool(self, fn, reads=(), writes=()):
        return self.op("gpsimd", fn, reads, writes)

    def dma(self, eng, fn, dsem, reads=(), writes=()):
        return self.op(eng, fn, reads, writes, kind="d", dsem=dsem)

    def cc(self, fn, dsem, reads=(), writes=()):
        return self.op("gpsimd", fn, reads, writes, kind="cc", dsem=dsem)

    def dsem_names(self):
        return sorted({o["dsem"] for o in self.ops if o["dsem"] is not None})

    def emit(self, block, sems, final_wait_ops=()):
        ops = self.ops
        n = len(ops)
        needed = [False] * n
        for o in ops:
            for d in o["deps"]:
                needed[d] = True
        for d in final_wait_ops:
            needed[d] = True
        cnt = {}
        semval = [None] * n
        for i, o in enumerate(ops):
            if o["kind"] == "c":
                if needed[i]:
                    key = "e_" + o["eng"]
                    cnt[key] = cnt.get(key, 0) + 1
                    semval[i] = (key, cnt[key], 1)
            elif o["kind"] == "d":
                key = o["dsem"]
                cnt[key] = cnt.get(key, 0) + 16
                semval[i] = (key, cnt[key], 16)
            else:
                key = o["dsem"]
                cnt[key] = cnt.get(key, 0) + 1
                semval[i] = (key, cnt[key], 1)
        per_eng = {e: [] for e in ENGS}
        for i, o in enumerate(ops):
            per_eng[o["eng"]].append(i)
        self.stats = {e: len(v) for e, v in per_eng.items()}
        self.stats["sems"] = dict(cnt)
        ses = self.same_engine_sync

        def make(engname):
            def body(eng):
                waited = {}
                for i in per_eng[engname]:
                    o = ops[i]
                    need = {}
                    for d in o["deps"]:
                        od = ops[d]
                        if od["eng"] == engname and od["kind"] == "c":
                            if engname == "tensor" or not ses:
                                continue
                        k, v, _ = semval[d]
                        if v > need.get(k, 0):
                            need[k] = v
                    for k, v in need.items():
                        if waited.get(k, 0) < v:
                            eng.wait_ge(sems[k], v)
                            waited[k] = v
                    ins = o["fn"](eng)
                    if semval[i] is not None:
                        k, v, inc = semval[i]
                        ins.then_inc(sems[k], inc)
                if engname == "sync":
                    for d in final_wait_ops:
                        k, v, _ = semval[d]
                        eng.wait_ge(sems[k], v)
            return body

        block.tensor(make("tensor"))
        block.vector(make("vector"))
        block.scalar(make("scalar"))
        block.gpsimd(make("gpsimd"))
        block.sync(make("sync"))


def even_spec():
    fm = []
    for h in range(8):
        fm.append((h * 128, True, ("q", h)))
    for h in range(8):
        fm.append((3072 + h * 128, True, ("q", 8 + h)))
    for h in range(8):
        fm.append((1024 + h * 128, True, ("k", h)))
    for g in range(2):
        fm.append((4096 + g * 128, True, ("k", 8 + g)))
    tm = []
    for h in range(8):
        tm.append((2048 + h * 128, 10 + h))
    for g in range(2):
        tm.append((4352 + g * 128, 18 + g))
    return dict(fm=fm, tm=tm, nitems=20, nk=10, small=None)


def odd_spec():
    fm = []
    for h in range(8):
        fm.append((h * 128, False, ("q", h)))
    for h in range(8):
        fm.append((3080 + h * 128, True, ("q", 8 + h)))
    for h in range(8):
        fm.append((1024 + h * 128, False, ("k", h)))
    for g in range(2):
        fm.append((4616 + g * 128, True, ("k", 8 + g)))
    for g in range(2):
        fm.append((5128 + g * 128, True, ("k", 10 + g)))
    for g in range(2):
        fm.append((4104 + g * 128, True, ("cmp", g)))
    for g in range(2):
        fm.append((4360 + g * 128, False, ("cmp", 2 + g)))
    tm = []
    for h in range(8):
        tm.append((2048 + h * 128, 12 + h))
    for g in range(2):
        tm.append((4872 + g * 128, 20 + g))
    for g in range(2):
        tm.append((5384 + g * 128, 22 + g))
    return dict(fm=fm, tm=tm, nitems=24, nk=12, small=(3072, 5640))


def host_w_layout(w_in, spec):
    fm = np.concatenate([w_in[:, o:o + 128] for (o, _, _) in spec["fm"]], axis=1)
    tm = np.concatenate([w_in[:, o:o + 128] for (o, _) in spec["tm"]], axis=1)
    return np.ascontiguousarray(fm), np.ascontiguousarray(tm)


class H:
    def __init__(self, ap):
        self._ap = ap

    def ap(self):
        return self._ap

    def __getitem__(self, k):
        return self._ap[k]


class Builder:
    def __init__(self, nlayers=DEPTH, stop_after_mix=False, final_norm=True, phase=None):
        self.phase = phase
        self.nlayers = nlayers
        self.stop_after_mix = stop_after_mix
        self.final_norm = final_norm
        self.nc = bass.Bass("TRN2", target_bir_lowering=False)
        self.S = Sched()
        self.psrr = 0
        self.uid = 0
        self.slot_rr = 0
        self.pt_rr = 0
        self.tq_rr = 0
        self._ins = {}
        self.out_ops = []
        self.pending = None

    def dram_in(self, name, shape, dt=F32):
        ap = self.nc.dram_tensor(name, list(shape), dt, kind="ExternalInput").ap()
        self._ins[name] = ap
        return ap

    def sb(self, name, shape, dt):
        return self.es.enter_context(self.nc.sbuf_tensor(name, list(shape), dt))

    def next_ps(self, lo=0, hi=7):
        b = lo + (self.psrr % (hi - lo))
        self.psrr += 1
        return b

    def build(self):
        nc = self.nc
        S = self.S
        NL = self.nlayers
        with contextlib.ExitStack() as es:
            self.es = es
            ph = self.phase
            if ph is None:
                need_A = set(range(NL))
                need_B = set(range(NL))
                mlp_layers = set(l for l in range(NL) if not self.stop_after_mix or l < NL - 1)
            elif ph == "M":
                need_A, need_B, mlp_layers = set(), set(), set()
            else:
                need_A = {ph} if ph < DEPTH else set()
                need_B = {ph - 1} if ph >= 1 else set()
                mlp_layers = set(need_B)
            self.need_A, self.need_B = need_A, need_B
            if ph in (None, 0):
                self.dram_in("xT", [16, 128, TL])
            elif ph != "M":
                self.dram_in("xstate_in", [16, 128, TL])
                self.dram_in("qstate_in", [128, 16 * TL], BF16)
                self.dram_in("gstate_in", [128, 192], BF16)
                self.dram_in("pstate_in", [128, 2])
            if ph is None or ph == DEPTH:
                self._outT = nc.dram_tensor("outT", [16, 128, TL], F32, kind="ExternalOutput").ap()
                S.ext_keys.add("outT")
            elif ph != "M":
                self._xso = nc.dram_tensor("xstate_out", [16, 128, TL], F32, kind="ExternalOutput").ap()
                self._qso = nc.dram_tensor("qstate_out", [128, 16 * TL], BF16, kind="ExternalOutput").ap()
                self._gso = nc.dram_tensor("gstate_out", [128, 192], BF16, kind="ExternalOutput").ap()
                self._pso = nc.dram_tensor("pstate_out", [128, 2], F32, kind="ExternalOutput").ap()
                S.ext_keys.update(["xso", "qso", "gso", "pso"])
            self.dram_in("ropecos", [128, TL])
            self.dram_in("ropesin", [128, TL])
            self.dram_in("cmask", [128, 8 * 128], BF16)
            self.dram_in("constf", [128, 4 * 128])
            self.dram_in("constb", [128, 2 * 128], BF16)
            self.dram_in("cT", [128, 16])
            if ph in (None, "M"):
                self.dram_in("adaw", [DEPTH, D, 1536])
            self.dram_in("adabT", [128, 48])
            self.dram_in("gvec", [128, 9 * 16])
            W = {}
            for l in sorted(need_A | need_B):
                W[l] = {}
                if l in need_A:
                    if l % 2 == 0:
                        W[l].update(fm=self.dram_in(f"wfm{l}", [D, 26 * 128]), tm=self.dram_in(f"wtm{l}", [D, 10 * 128]))
                    else:
                        W[l].update(fm=self.dram_in(f"wfm{l}", [D, 32 * 128]), tm=self.dram_in(f"wtm{l}", [D, 12 * 128]),
                                    sm=self.dram_in(f"wsm{l}", [D, 32]))
                if l in need_B:
                    W[l]["out"] = self.dram_in(f"wout{l}", [D, D])
                    if l in mlp_layers:
                        W[l]["up"] = self.dram_in(f"wup{l}", [D, DFF])
                        W[l]["down"] = self.dram_in(f"wdown{l}", [DFF, D])
            self.W = W
            ev = dict(
                wm=self.dram_in("wm", [32, 4096], BF16),
                swamask=self.dram_in("swamask", [128, 9 * 128], BF16),
                gsel=self.dram_in("gsel", [128, 3 * 256]),
                sinks=self.dram_in("sinksb", [128, 2 * 8]),
            )
            self.ev = ev
            any_odd = any(l % 2 == 1 for l in (need_A | need_B))
            od = dict(
                wh=self.dram_in("wh", [128, 4096], BF16),
                winmask=self.dram_in("winmask", [128, 12 * 128], BF16),
                cmpmask=self.dram_in("cmpmask", [128, 3 * 128], BF16),
                ovl=self.dram_in("ovl", [128, 4 * 128], BF16),
                selbase=self.dram_in("selbase", [128, 240]),
                onehot=self.dram_in("onehot", [128, 8]),
                fbias=self.dram_in("fbias", [128, 16]),
                posT=self.dram_in("posT", [128, 4 * 32]),
                w1=[[self.dram_in(f"w1_{j2}_{t}", [4096, 128]) for t in range(2)] if (2 * j2 + 1) in need_A else None for j2 in range(2)],
                w2=[self.dram_in(f"w2_{j2}", [128, 256]) if (2 * j2 + 1) in need_B else None for j2 in range(2)],
            ) if any_odd else None
            self.od = od
            self.send, self.gath, self.ssend, self.sgath = {}, {}, {}, {}
            if ph is None:
                self.modsend = nc.dram_tensor("modsend", [128, 48], F32)
                self.modgath = nc.dram_tensor("modgath", [NCORE * 128, 48], F32)
            elif ph == "M":
                self.modsend = H(nc.dram_tensor("modsend_out", [128, 48], F32, kind="ExternalOutput").ap())
                S.ext_keys.add("modsend")
            else:
                self.modgath = H(self.dram_in("modgath_in", [NCORE * 128, 48]))
            for l in sorted(need_A | need_B):
                ni = 20 if l % 2 == 0 else 24
                nsm = 64 if l % 2 == 0 else 576
                if ph is None:
                    self.send[l] = nc.dram_tensor(f"send{l}", [ni * 128, TL], BF16)
                    self.gath[l] = nc.dram_tensor(f"gath{l}", [NCORE * ni * 128, TL], BF16)
                    self.ssend[l] = nc.dram_tensor(f"ssend{l}", [128, nsm], F32)
                    self.sgath[l] = nc.dram_tensor(f"sgath{l}", [NCORE * 128, nsm], F32)
                else:
                    if l in need_A:
                        self.send[l] = H(nc.dram_tensor(f"send{l}_out", [ni * 128, TL], BF16, kind="ExternalOutput").ap())
                        self.ssend[l] = H(nc.dram_tensor(f"ssend{l}_out", [128, nsm], F32, kind="ExternalOutput").ap())
                        S.ext_keys.update([("send", l), ("ssend", l)])
                    if l in need_B:
                        self.gath[l] = H(self.dram_in(f"gath{l}_in", [NCORE * ni * 128, TL], BF16))
                        self.sgath[l] = H(self.dram_in(f"sgath{l}_in", [NCORE * 128, nsm]))

            self.xT = self.sb("xT_sb", [128, 16, TL], F32)
            self.A = self.sb("A_sb", [128, 16 * TL], BF16)
            self.QT = self.sb("QT_sb", [128, 16, TL], BF16)
            self.SL = [self.sb(f"slot{i}", [128, 4096], BF16) for i in range(3)]
            self.cosT = self.sb("cosT", [128, TL], F32)
            self.sinT = self.sb("sinT", [128, TL], F32)
            self.cmask = self.sb("cmask_sb", [128, 8, 128], BF16)
            self.cf = self.sb("constf_sb", [128, 4, 128], F32)
            self.cb = self.sb("constb_sb", [128, 2, 128], BF16)
            self.XF = self.sb("XF", [128, 2560], F32)
            self.vec = self.sb("vec", [128, 1024], F32)
            self.tq = [self.sb(f"tq{i}", [128, 512], F32) for i in range(3)]
            self.PT = [self.sb(f"PT{i}", [128, 512], BF16) for i in range(3)]
            self.kst = [self.sb(f"kst{i}", [128, TL], BF16) for i in range(2)]
            self.vst = [self.sb(f"vst{i}", [128, 2, 8, 128], BF16) for i in range(2)]
            self.selb = self.sb("selb", [128, 256], BF16)
            self.gsb = self.sb("gsb", [128, 8, 24], BF16)
            self.w2sb = self.sb("w2sb", [128, 2, 128], BF16)
            self.posT = self.sb("posT_sb", [128, 2, 32], BF16)
            self.PC = self.sb("PC", [128, 4, 512], BF16)
            self.PS = [es.enter_context(nc.psum_tensor(f"ps{i}", [128, 512], F32)) for i in range(7)]
            self.PSB = es.enter_context(nc.psum_tensor("psb", [128, 1024], BF16))
            self.identf = self.cf[:, 0, :]
            self.Rm = self.cf[:, 1, :]
            self.onesf = self.cf[:, 2, :]
            self.tri = self.cf[:, 3, :]
            self.identb = self.cb[:, 0, :]
            self.onesb = self.cb[:, 1, :]
            self.V_COND = 0
            self.V_MOD = 16
            self.V_G = 400
            self.V_ADAB = 544
            self.V_DER = 592
            self.V_SINK = 700
            self.V_POSB = 720

            if ph is None:
                self.load_consts("xT")
                self.compute_mods_local()
                self.mods_gather()
                for l in range(NL):
                    self.layer_A(l)
                    self.exchange(l)
                    self.layer_B(l)
                if self.final_norm:
                    self.final()
                else:
                    self.store_x()
            elif ph == "M":
                self.load_consts(None)
                self.compute_mods_local()
            else:
                self.load_consts("xT" if ph == 0 else "xstate_in")
                self.mods_load()
                if ph >= 1:
                    self.restore_state()
                    self.derive(ph - 1)
                    self.layer_B(ph - 1)
                if ph < DEPTH:
                    self.layer_A(ph)
                    self.save_state()
                else:
                    self.final()

            sems = {}
            for e in ENGS:
                sems["e_" + e] = es.enter_context(nc.semaphore("e_" + e))
            for nme in S.dsem_names():
                sems[nme] = es.enter_context(nc.semaphore(nme))
            block = es.enter_context(nc.Block())
            S.emit(block, sems, final_wait_ops=sorted(set(self.out_ops) | set(S.ext_ops)))
        return nc

    def load_consts(self, xname):
        S = self.S
        if xname is not None:
            xT_d = self.nc_in(xname)
            S.dma("sync", lambda e: e.dma_start(out=self.xT[:], in_=xT_d.rearrange("c p t -> p c t")), "d_x",
                  writes=[("x", c, h) for c in range(16) for h in range(2)])
        for (nm, dst, key) in [("ropecos", self.cosT, "cos"), ("ropesin", self.sinT, "sin"), ("cT", self.vec[:, 0:16], "v_cond"),
                               ("adabT", self.vec[:, self.V_ADAB:self.V_ADAB + 48], "v_adab"),
                               ("gvec", self.vec[:, self.V_G:self.V_G + 144], "v_g")]:
            src = self.nc_in(nm)
            d = dst if nm not in ("ropecos", "ropesin") else dst[:]
            S.dma("sync", (lambda d, src: (lambda e: e.dma_start(out=d, in_=src)))(d, src), "d_c_" + key, writes=[key])
        S.dma("sync", lambda e: e.dma_start(out=self.cmask[:], in_=self.nc_in("cmask").rearrange("p (a b) -> p a b", a=8)), "d_c_cmask", writes=["cmask"])
        S.dma("sync", lambda e: e.dma_start(out=self.cf[:], in_=self.nc_in("constf").rearrange("p (a b) -> p a b", a=4)), "d_c_cf", writes=["cf"])
        S.dma("sync", lambda e: e.dma_start(out=self.cb[:], in_=self.nc_in("constb").rearrange("p (a b) -> p a b", a=2)), "d_c_cb", writes=["cb"])

    def nc_in(self, name):
        return self._ins[name]

    def compute_mods_local(self):
        S = self.S
        vec = self.vec
        cond = vec[:, 0:16]
        S.act(lambda e: e.activation(out=cond, in_=cond, func=AF.Silu), reads=["v_cond"], writes=["v_cond"])
        adaw = self.nc_in("adaw")
        modrow = self.tq[0]
        for l in range(DEPTH):
            banks = [0, 1, 2]
            for kc in range(16):
                for nb in range(3):
                    st = self.tq[nb]
                    src = adaw[l, kc * 128:(kc + 1) * 128, nb * 512:(nb + 1) * 512]
                    S.dma("sync", (lambda st, src: (lambda e: e.dma_start(out=st[:], in_=src)))(st, src), f"d_tq{nb}",
                          writes=[("tq", nb)])
                    S.pe((lambda st, kc, nb: (lambda e: e.matmul(self.PS[nb][0:1, :], lhsT=cond[:, kc:kc + 1], rhs=st[:],
                                                                  start=(kc == 0), stop=(kc == 15))))(st, kc, nb),
                         reads=[("tq", nb), "v_cond"], writes=[("ps", nb)])
            for nb in range(3):
                S.act((lambda nb: (lambda e: e.copy(out=self.XF[0:1, nb * 512:(nb + 1) * 512], in_=self.PS[nb][0:1, :])))(nb),
                      reads=[("ps", nb)], writes=[("xfrow", nb)])
            for k in range(12):
                nb = k // 4
                S.pe((lambda k: (lambda e: e.matmul(self.PS[3][:, k:k + 1], lhsT=self.XF[0:1, k * 128:(k + 1) * 128],
                                                    rhs=self.onesf[0:1, 0:1], start=True, stop=True)))(k),
                     reads=[("xfrow", nb), "cf"], writes=[("ps", 3)])
            S.dve((lambda l: (lambda e: e.tensor_tensor(out=self.XF[:, 2432 + l * 12:2432 + (l + 1) * 12], in0=self.PS[3][:, 0:12],
                                                        in1=vec[:, self.V_ADAB + l * 12:self.V_ADAB + (l + 1) * 12], op=ALU.add)))(l),
                  reads=[("ps", 3), "v_adab"], writes=["modloc"])
        S.dma("sync", lambda e: e.dma_start(out=self.modsend[:, :], in_=self.XF[:, 2432:2480]), "d_modsend", reads=["modloc"], writes=["modsend"])

    def mods_gather(self):
        S = self.S
        S.cc(lambda e: e.collective_compute("AllGather", ALU.bypass, replica_groups=[list(range(NCORE))],
                                            ins=[self.modsend.ap().opt()], outs=[self.modgath.ap().opt()]),
             "cc_mod", reads=["modsend"], writes=["modgath"])
        self.mods_load()

    def mods_load(self):
        S = self.S
        vec = self.vec
        S.dma("sync", lambda e: e.dma_start(out=vec[:, self.V_MOD:self.V_MOD + 384].rearrange("p (r f) -> p r f", r=8),
                                            in_=self.modgath.ap().rearrange("(r p) f -> p r f", p=128)),
              "d_modload", reads=["modgath"], writes=["v_mod"])

    def modcol(self, l, part, ch):
        gch = part * 16 + ch
        r, k = gch // 12, gch % 12
        o = self.V_MOD + r * 48 + l * 12 + k
        return self.vec[:, o:o + 1]

    def derive(self, l):
        S = self.S
        vec = self.vec
        for which, part, goff in ((0, 1, l * 16), (1, 4, 64 + l * 16)):
            for ch in range(16):
                dst = vec[:, self.V_DER + which * 16 + ch:self.V_DER + which * 16 + ch + 1]
                S.dve((lambda dst, part, ch, goff: (lambda e: e.scalar_tensor_tensor(
                    out=dst, in0=self.modcol(l, part, ch), scalar=1.0, in1=vec[:, self.V_G + goff + ch:self.V_G + goff + ch + 1],
                    op0=ALU.add, op1=ALU.mult)))(dst, part, ch, goff),
                    reads=["v_mod", "v_g"], writes=[("v_der", which)])

    def norm_to_A(self, gmul_of, gadd_of, derkey):
        S = self.S
        A3 = self.A[:].rearrange("p (c t) -> p c t", c=16)
        for half in range(2):
            cs = slice(half * 512, (half + 1) * 512)
            bank = 4 + half
            for ch in range(16):
                sq = self.tq[ch % 2]
                S.act((lambda sq, ch, cs: (lambda e: e.activation(out=sq[:], in_=self.xT[:, ch, cs], func=AF.Square)))(sq, ch, cs),
                      reads=[("x", ch, half)], writes=[("tq", ch % 2)])
                S.pe((lambda sq, ch, bank: (lambda e: e.matmul(self.PS[bank][:], lhsT=self.onesf, rhs=sq[:], start=(ch == 0), stop=(ch == 15))))(sq, ch, bank),
                     reads=[("tq", ch % 2), "cf"], writes=[("ps", bank)])
            rs = self.XF[:, 1024 + half * 512:1024 + (half + 1) * 512]
            S.act((lambda bank, rs: (lambda e: e.activation(out=rs, in_=self.PS[bank][:], func=AF.Sqrt, scale=1.0 / D, bias=EPS)))(bank, rs),
                  reads=[("ps", bank)], writes=[("rstd", half)])
            S.dve((lambda rs: (lambda e: e.reciprocal(out=rs, in_=rs)))(rs), reads=[("rstd", half)], writes=[("rstd", half)])
            for ch in range(16):
                t = self.tq[2]
                S.dve((lambda t, ch, cs, rs: (lambda e: e.scalar_tensor_tensor(out=t[:], in0=self.xT[:, ch, cs], scalar=gmul_of(ch), in1=rs,
                                                                                op0=ALU.mult, op1=ALU.mult)))(t, ch, cs, rs),
                      reads=[("x", ch, half), ("rstd", half), derkey, "v_mod"], writes=[("tq", 2)])
                S.act((lambda t, ch, cs: (lambda e: e.activation(out=A3[:, ch, cs], in_=t[:], func=AF.Identity, bias=gadd_of(ch), scale=1.0)))(t, ch, cs),
                      reads=[("tq", 2), "v_mod"], writes=[("A", ch, half)])

    def wslot_load(self, wd, row0, col0, ncols, key):
        S = self.S
        s = self.slot_rr % 3
        self.slot_rr += 1
        slot = self.SL[s]
        src = wd[row0:row0 + 2048, col0:col0 + ncols].rearrange("(k p) n -> p k n", p=128)
        dst = slot[:, 0:16 * ncols].rearrange("p (k n) -> p k n", k=16)
        S.dma("gpsimd", lambda e: e.dma_start(out=dst, in_=src), f"d_slot{s}", writes=[("slot", s), ("slotv", s), ("slotw", s), ("slotx", s)])
        return s, dst

    def projections(self, l, spec):
        S = self.S
        W = self.W[l]
        A3 = self.A[:].rearrange("p (c t) -> p c t", c=16)
        nfm = len(spec["fm"])
        send = self.send[l]
        self.send_ops = []
        for g in range(nfm // 2):
            s, wv = self.wslot_load(W["fm"], 0, g * 256, 256, None)
            for ci in range(2):
                (_, rope, dest) = spec["fm"][g * 2 + ci]
                if dest[0] == "q":
                    dst_full = self.QT[:, dest[1], :]
                    dkeys = [("Q", dest[1], 0), ("Q", dest[1], 1)]
                elif dest[0] == "k":
                    kb = dest[1] % 2
                    dst_full = self.kst[kb][:]
                    dkeys = [("kst", kb), ("kst", kb)]
                else:
                    kb = dest[1] % 2
                    dst_full = self.kst[kb][:]
                    dkeys = [("kst", kb), ("kst", kb)]
                for half in range(2):
                    cs = slice(half * 512, (half + 1) * 512)
                    b = self.next_ps(0, 4)
                    for kc in range(16):
                        S.pe((lambda b, wv, kc, ci, cs: (lambda e: e.matmul(self.PS[b][:], lhsT=wv[:, kc, ci * 128:(ci + 1) * 128], rhs=A3[:, kc, cs],
                                                                            start=(kc == 0), stop=(kc == 15))))(b, wv, kc, ci, cs),
                             reads=[("slot", s), ("A", kc, half)], writes=[("ps", b)])
                    dst = dst_full[:, cs]
                    if not rope:
                        S.act((lambda b, dst: (lambda e: e.copy(out=dst, in_=self.PS[b][:])))(b, dst), reads=[("ps", b)], writes=[dkeys[half]])
                    else:
                        q32, t1, t2 = self.tq
                        rb = 4 + (self.uid % 2)
                        self.uid += 1
                        S.act((lambda b: (lambda e: e.copy(out=q32[:], in_=self.PS[b][:])))(b), reads=[("ps", b)], writes=[("tq", 0)])
                        S.pe((lambda rb: (lambda e: e.matmul(self.PS[rb][:], lhsT=self.Rm, rhs=q32[:], start=True, stop=True)))(rb),
                             reads=[("tq", 0), "cf"], writes=[("ps", rb)])
                        S.dve((lambda cs: (lambda e: e.tensor_tensor(out=t1[:], in0=q32[:], in1=self.cosT[:, cs], op=ALU.mult)))(cs),
                              reads=[("tq", 0), "cos"], writes=[("tq", 1)])
                        S.dve((lambda rb, cs: (lambda e: e.tensor_tensor(out=t2[:], in0=self.PS[rb][:], in1=self.sinT[:, cs], op=ALU.mult)))(rb, cs),
                              reads=[("ps", rb), "sin"], writes=[("tq", 2)])
                        S.dve((lambda dst: (lambda e: e.tensor_tensor(out=dst, in0=t1[:], in1=t2[:], op=ALU.add)))(dst),
                              reads=[("tq", 1), ("tq", 2)], writes=[dkeys[half]])
                if dest[0] == "k":
                    item = dest[1]
                    kb = item % 2
                    if l % 2 == 0 and item < 8:
                        S.dve((lambda kb, item: (lambda e: e.tensor_reduce(out=self.XF[:, 2368 + item * 8:2368 + item * 8 + 8],
                                                                            in_=self.kst[kb][:].rearrange("p (j t) -> p j t", j=8), axis=AX.X, op=ALU.add)))(kb, item),
                              reads=[("kst", kb)], writes=["ksumT"])
                    o = S.dma("sync", (lambda kb, item: (lambda e: e.dma_start(out=send[item * 128:(item + 1) * 128, :], in_=self.kst[kb][:])))(kb, item),
                              f"d_kst{kb}", reads=[("kst", kb)], writes=[("send", l)])
                elif dest[0] == "cmp":
                    self.cmp_partials(l, dest[1])
        if l % 2 == 1:
            self.odd_small(l)
        ntm = len(spec["tm"])
        for g in range(ntm // 2):
            s, wv = self.wslot_load(W["tm"], 0, g * 256, 256, None)
            vb = g % 2
            for j in range(8):
                b = self.next_ps(0, 4)
                for kc in range(16):
                    S.pe((lambda b, wv, kc, j: (lambda e: e.matmul(self.PS[b][:, 0:256], lhsT=A3[:, kc, j * 128:(j + 1) * 128], rhs=wv[:, kc, :],
                                                                   start=(kc == 0), stop=(kc == 15))))(b, wv, kc, j),
                         reads=[("slot", s), ("A", kc, j // 4)], writes=[("ps", b)])
                S.act((lambda b, vb, j: (lambda e: e.copy(out=self.vst[vb][:, :, j, :], in_=self.PS[b][:, 0:256].rearrange("p (h d) -> p h d", h=2))))(b, vb, j),
                      reads=[("ps", b)], writes=[("vst", vb)])
            for ci in range(2):
                item = spec["tm"][g * 2 + ci][1]
                S.dma("sync", (lambda vb, ci, item: (lambda e: e.dma_start(out=send[item * 128:(item + 1) * 128, :],
                                                                           in_=self.vst[vb][:, ci, :, :].rearrange("p j d -> p (j d)"))))(vb, ci, item),
                      f"d_vst{vb}", reads=[("vst", vb)], writes=[("send", l)])

    def exchange(self, l):
        S = self.S
        send, gath = self.send[l], self.gath[l]
        S.cc(lambda e: e.collective_compute("AllGather", ALU.bypass, replica_groups=[list(range(NCORE))],
                                            ins=[send.ap().opt()], outs=[gath.ap().opt()]),
             f"cc_big{l}", reads=[("send", l)], writes=[("gath", l)])
        ss, sg = self.ssend[l], self.sgath[l]
        S.cc(lambda e: e.collective_compute("AllGather", ALU.bypass, replica_groups=[list(range(NCORE))],
                                            ins=[ss.ap().opt()], outs=[sg.ap().opt()]),
             f"cc_small{l}", reads=[("ssend", l)], writes=[("sgath", l)])

    def kv_load(self, l, kitem, vitem, cp, ni):
        S = self.S
        s = self.slot_rr % 3
        self.slot_rr += 1
        slot = self.SL[s]
        g3 = self.gath[l].ap().rearrange("(c r) t -> r c t", c=NCORE)
        ksrc = g3[kitem * 128:(kitem + 1) * 128, 2 * cp:2 * cp + 2, :]
        vsrc = g3[vitem * 128:(vitem + 1) * 128, 2 * cp:2 * cp + 2, :]
        kd = slot[:, 0:2048].rearrange("p (c t) -> p c t", c=2)
        vd = slot[:, 2048:4096].rearrange("p (c t) -> p c t", c=2)
        S.dma("sync", lambda e: e.dma_start(out=kd, in_=ksrc), f"d_slot{s}", reads=[("gath", l)], writes=[("slot", s), ("slotw", s)])
        S.dma("sync", lambda e: e.dma_start(out=vd, in_=vsrc), f"d_slotv{s}", reads=[("gath", l)], writes=[("slotv", s), ("slotx", s)])
        return s, kd, slot[:, 2048:4096].rearrange("p (c j d) -> p c j d", c=2, j=8)

    def attn_unit(self, sbank, kT, vT, q_ap, n, extra, exp_bias, obank, dbank, ocols, first, last, skeys, pkeys):
        S = self.S
        pt = self.PT[self.pt_rr % 3]
        ptk = ("PT", self.pt_rr % 3)
        self.pt_rr += 1
        nex = len(extra)
        S.pe(lambda e: e.matmul(self.PS[sbank][:, 0:n], lhsT=kT, rhs=q_ap, start=True, stop=(nex == 0)),
             reads=skeys, writes=[("ps", sbank)])
        for xi, (lhs, rhs, c0, c1, xkeys) in enumerate(extra):
            S.pe((lambda lhs, rhs, c0, c1, xi: (lambda e: e.matmul(self.PS[sbank][:, c0:c1], lhsT=lhs, rhs=rhs, start=False, stop=(xi == nex - 1))))(lhs, rhs, c0, c1, xi),
                 reads=xkeys, writes=[("ps", sbank)])
        if exp_bias is None:
            S.act(lambda e: e.activation(out=pt[:, 0:n], in_=self.PS[sbank][:, 0:n], func=AF.Exp, scale=SCALE), reads=[("ps", sbank)], writes=[ptk])
        else:
            for (c0, c1, bap, bkeys) in exp_bias:
                S.act((lambda c0, c1, bap: (lambda e: e.activation(out=pt[:, c0:c1], in_=self.PS[sbank][:, c0:c1], func=AF.Exp, scale=SCALE, bias=bap)))(c0, c1, bap),
                      reads=[("ps", sbank)] + bkeys, writes=[ptk])
        def pv():
            S.pe(lambda e: e.matmul(self.PS[obank][:, ocols], lhsT=vT, rhs=pt[:, 0:n], start=first, stop=last, skip_group_check=True),
                 reads=[ptk] + pkeys, writes=[("ps", obank)])
            S.pe(lambda e: e.matmul(self.PS[dbank][:, ocols], lhsT=self.onesb, rhs=pt[:, 0:n], start=first, stop=last, skip_group_check=True),
                 reads=[ptk, "cb"], writes=[("ps", dbank)])
        prev = self.pending
        self.pending = pv
        if prev is not None:
            prev()

    def attn_flush(self):
        if self.pending is not None:
            p = self.pending
            self.pending = None
            p()

    def finalize_head(self, obank, dbank, dst, dkeys, ncols=512, sink_cols=None):
        S = self.S
        self.attn_flush()
        rd = self.tq[self.tq_rr % 3]
        rk = ("tq", self.tq_rr % 3)
        self.tq_rr += 1
        if sink_cols is None:
            S.dve(lambda e: e.tensor_scalar(out=rd[:, 0:ncols], in0=self.PS[dbank][:, 0:ncols], scalar1=1e-30, scalar2=None, op0=ALU.max),
                  reads=[("ps", dbank)], writes=[rk])
        else:
            for hh, sc in enumerate(sink_cols):
                S.dve((lambda hh, sc: (lambda e: e.tensor_scalar(out=rd[:, hh * 128:(hh + 1) * 128], in0=self.PS[dbank][:, hh * 128:(hh + 1) * 128],
                                                                  scalar1=sc, scalar2=None, op0=ALU.add)))(hh, sc),
                      reads=[("ps", dbank), "v_sink"], writes=[rk])
        S.dve(lambda e: e.reciprocal(out=rd[:, 0:ncols], in_=rd[:, 0:ncols]), reads=[rk], writes=[rk])
        S.dve(lambda e: e.tensor_tensor(out=dst, in0=self.PS[obank][:, 0:ncols] if len(dst.shape) == 2 else self.PS[obank][:, 0:ncols].rearrange("p (h t) -> p h t", h=4),
                                        in1=rd[:, 0:ncols] if len(dst.shape) == 2 else rd[:, 0:ncols].rearrange("p (h t) -> p h t", h=4), op=ALU.mult),
              reads=[("ps", obank), rk], writes=dkeys)

    def even_attention(self, l):
        S = self.S
        A = self.A
        XF = self.XF
        j2 = l // 2
        ev = self.ev
        wm = A[0:32, 0:4096]
        swam = A[:, 4096:4096 + 1152].rearrange("p (s q) -> p s q", s=9)
        selbT = [A[0:32, 5248:6272], A[0:32, 6272:7296]]
        kmT = A[:, 7296:7552]
        akeys = [("A", c, h) for c in range(16) for h in range(2)]
        S.dma("sync", lambda e: e.dma_start(out=wm, in_=ev["wm"]), "d_ac0", writes=akeys + ["wm"])
        S.dma("sync", lambda e: e.dma_start(out=A[:, 4096:4096 + 1152], in_=ev["swamask"]), "d_ac1", writes=["swam"])
        S.dma("sync", lambda e: e.dma_start(out=XF[:, 0:768], in_=ev["gsel"]), "d_ac2", writes=["gsel"])
        S.dma("sync", lambda e: e.dma_start(out=self.vec[:, self.V_SINK:self.V_SINK + 8], in_=ev["sinks"][:, j2 * 8:(j2 + 1) * 8]), "d_ac3", writes=["v_sink"])
        S.act(lambda e: e.activation(out=self.vec[:, self.V_SINK:self.V_SINK + 8], in_=self.vec[:, self.V_SINK:self.V_SINK + 8], func=AF.Exp),
              reads=["v_sink"], writes=["v_sink"])
        kg = self.tq[0][:]
        S.dma("sync", lambda e: e.dma_start(out=kg.rearrange("p (c f) -> p c f", c=8), in_=self.sgath[l].ap().rearrange("(c p) f -> p c f", p=128)),
              "d_ac4", reads=[("sgath", l)], writes=[("tq", 0)])
        kg5 = kg.rearrange("p (c2 par h j) -> p c2 par h j", c2=4, par=2, h=8)
        S.dve(lambda e: e.tensor_tensor(out=kmT.rearrange("p (h j c2) -> p h j c2", h=8, j=8),
                                        in0=kg5[:, :, 0, :, :].transpose([0, 2, 3, 1]), in1=kg5[:, :, 1, :, :].transpose([0, 2, 3, 1]), op=ALU.add),
              reads=[("tq", 0)], writes=["kmT"])
        gbias, past01, own01 = XF[:, 0:256], XF[:, 256:512], XF[:, 512:768]
        gm, sel, m8 = XF[:, 768:1024], XF[:, 2048:2304], XF[:, 2304:2368]
        ni = 20
        for h in range(8):
            sb_ = selbT[h % 2]
            sbk = ("selbT", h % 2)
            for j in range(8):
                S.pe((lambda h, j: (lambda e: e.matmul(self.PS[6][:, j * 32:(j + 1) * 32], lhsT=self.QT[:, h, j * 128:(j + 1) * 128],
                                                       rhs=kmT[:, h * 32:(h + 1) * 32], start=True, stop=True)))(h, j),
                     reads=[("Q", h, j // 4), "kmT"], writes=[("ps", 6)])
            S.dve(lambda e: e.tensor_tensor(out=gm, in0=self.PS[6][:, 0:256], in1=gbias, op=ALU.add), reads=[("ps", 6), "gsel"], writes=["gm"])
            for j in range(8):
                S.dve((lambda j: (lambda e: e.max(out=m8[:, j * 8:(j + 1) * 8], in_=gm[:, j * 32:(j + 1) * 32])))(j), reads=["gm"], writes=["m8"])
                S.dve((lambda j: (lambda e: e.tensor_scalar(out=sel[:, j * 32:(j + 1) * 32], in0=gm[:, j * 32:(j + 1) * 32],
                                                            scalar1=m8[:, j * 8 + 2:j * 8 + 3], scalar2=None, op0=ALU.is_ge)))(j),
                      reads=["gm", "m8"], writes=["sel"])
            S.dve(lambda e: e.tensor_tensor(out=sel, in0=sel, in1=past01, op=ALU.mult), reads=["sel", "gsel"], writes=["sel"])
            S.dve(lambda e: e.tensor_tensor(out=sel, in0=sel, in1=own01, op=ALU.add), reads=["sel", "gsel"], writes=["sel"])
            S.dve(lambda e: e.tensor_scalar(out=self.selb[:], in0=sel, scalar1=-1.0, scalar2=-NEG, op0=ALU.add, op1=ALU.mult), reads=["sel"], writes=["selb"])
            for j in range(8):
                S.pe((lambda j: (lambda e: e.transpose(self.PSB[0:32, j * 128:(j + 1) * 128], self.selb[:, j * 32:(j + 1) * 32], self.identb)))(j),
                     reads=["selb", "cb"], writes=["psb"])
            S.act((lambda sb_: (lambda e: e.copy(out=sb_, in_=self.PSB[0:32, :])))(sb_), reads=["psb"], writes=[sbk])
            first = True
            for cp in range(4):
                s, kd, vd = self.kv_load(l, h, 10 + h, cp, ni)
                for cc in range(2):
                    ck = 2 * cp + cc
                    for jk in range(8):
                        b = (8 * jk + ck) // 2
                        groups = []
                        if jk < 4:
                            groups.append((0, jk * 128, 512))
                        groups.append((1, max(512, jk * 128), 1024))
                        for (grp, c0, c1) in groups:
                            n = c1 - c0
                            sbank = self.next_ps(0, 3)
                            extra = []
                            if c0 == jk * 128:
                                extra.append((self.identb, self.cmask[:, ck, :], 0, 128, ["cb", "cmask"]))
                            extra.append((wm[:, b * 128:(b + 1) * 128], sb_[:, c0:c1], 0, n, ["wm", sbk]))
                            last = (ck == 7 and jk == (3 if grp == 0 else 7))
                            fst = (ck == 0 and jk == 0)
                            hq = [("Q", h, grp)]
                            self.attn_unit(sbank, kd[:, cc, jk * 128:(jk + 1) * 128], vd[:, cc, jk, :], self.QT[:, h, c0:c1], n, extra, None,
                                           3 + grp, 5 + grp if False else (5 if grp == 0 else 6), slice(c0 - grp * 512, c1 - grp * 512), fst, last,
                                           [("slot", s)] + hq, [("slotv", s)])
            for grp in range(2):
                self.finalize_head(3 + grp, 5 if grp == 0 else 6, self.QT[:, h, grp * 512:(grp + 1) * 512], [("Q", h, grp)])
        g3 = self.gath[l].ap().rearrange("(c r) t -> r c t", c=NCORE)
        for g in range(2):
            kitem, vitem = 8 + g, 18 + g
            for j in range(8):
                s = self.slot_rr % 3
                self.slot_rr += 1
                slot = self.SL[s]
                kw = slot[:, 0:1152].rearrange("p (s t) -> p s t", s=9)
                vw = slot[:, 2048:2048 + 1152].rearrange("p (s t) -> p s t", s=9)
                rows_k = slice(kitem * 128, (kitem + 1) * 128)
                rows_v = slice(vitem * 128, (vitem + 1) * 128)
                S.dma("sync", (lambda kw, j, rows_k: (lambda e: e.dma_start(out=kw[:, 1:9, :], in_=g3[rows_k, :, j * 128:(j + 1) * 128])))(kw, j, rows_k),
                      f"d_slot{s}", reads=[("gath", l)], writes=[("slot", s)])
                S.dma("sync", (lambda vw, j, rows_v: (lambda e: e.dma_start(out=vw[:, 1:9, :], in_=g3[rows_v, :, j * 128:(j + 1) * 128])))(vw, j, rows_v),
                      f"d_slotv{s}", reads=[("gath", l)], writes=[("slotv", s)])
                if j > 0:
                    S.dma("sync", (lambda kw, j, rows_k: (lambda e: e.dma_start(out=kw[:, 0, :], in_=g3[rows_k, 7, (j - 1) * 128:j * 128])))(kw, j, rows_k),
                          f"d_slotw{s}", reads=[("gath", l)], writes=[("slotw", s)])
                    S.dma("sync", (lambda vw, j, rows_v: (lambda e: e.dma_start(out=vw[:, 0, :], in_=g3[rows_v, 7, (j - 1) * 128:j * 128])))(vw, j, rows_v),
                          f"d_slotx{s}", reads=[("gath", l)], writes=[("slotx", s)])
                cands = list(range(0 if j > 0 else 1, 9))
                qap = self.QT[:, 8 + 4 * g:12 + 4 * g, j * 128:(j + 1) * 128]
                qkeys = [("Q", 8 + 4 * g + hh, j // 4) for hh in range(4)]
                ob, db = 3 + (j % 2), 5 + (j % 2)
                for si, sidx in enumerate(cands):
                    sbank = self.next_ps(0, 3)
                    extra = [(self.identb, swam[:, sidx, :], hh * 128, (hh + 1) * 128, ["cb", "swam"]) for hh in range(4)]
                    kk = [("slot", s), ("slotw", s)] + qkeys
                    self.attn_unit(sbank, kw[:, sidx, :], vw[:, sidx, :], qap, 512, extra, None, ob, db, slice(0, 512),
                                   si == 0, si == len(cands) - 1, kk, [("slotv", s), ("slotx", s)])
                sinks = [self.vec[:, self.V_SINK + 4 * g + hh:self.V_SINK + 4 * g + hh + 1] for hh in range(4)]
                self.finalize_head(ob, db, qap, qkeys, 512, sink_cols=sinks)

    def out_proj(self, l):
        S = self.S
        W = self.W[l]
        for g in range(8):
            s, wv = self.wslot_load(W["out"], 0, g * 256, 256, None)
            for ci in range(2):
                fo = g * 2 + ci
                for half in range(2):
                    cs = slice(half * 512, (half + 1) * 512)
                    b = self.next_ps(0, 4)
                    for hc in range(16):
                        S.pe((lambda b, wv, hc, ci, cs: (lambda e: e.matmul(self.PS[b][:], lhsT=wv[:, hc, ci * 128:(ci + 1) * 128], rhs=self.QT[:, hc, cs],
                                                                            start=(hc == 0), stop=(hc == 15))))(b, wv, hc, ci, cs),
                             reads=[("slot", s), ("Q", hc, half)], writes=[("ps", b)])
                    S.dve((lambda b, fo, cs: (lambda e: e.scalar_tensor_tensor(out=self.xT[:, fo, cs], in0=self.PS[b][:], scalar=self.modcol(l, 2, fo),
                                                                                in1=self.xT[:, fo, cs], op0=ALU.mult, op1=ALU.add)))(b, fo, cs),
                          reads=[("ps", b), ("x", fo, half), "v_mod"], writes=[("x", fo, half)])

    def mlp(self, l):
        S = self.S
        W = self.W[l]
        A3 = self.A[:].rearrange("p (c t) -> p c t", c=16)
        for qd in range(4):
            for g in range(8):
                s, wv = self.wslot_load(W["up"], 0, qd * 2048 + g * 256, 256, None)
                for ci in range(2):
                    fc = g * 2 + ci
                    for half in range(2):
                        cs = slice(half * 512, (half + 1) * 512)
                        b = self.next_ps(0, 4)
                        for kc in range(16):
                            S.pe((lambda b, wv, kc, ci, cs: (lambda e: e.matmul(self.PS[b][:], lhsT=wv[:, kc, ci * 128:(ci + 1) * 128], rhs=A3[:, kc, cs],
                                                                                start=(kc == 0), stop=(kc == 15))))(b, wv, kc, ci, cs),
                                 reads=[("slot", s), ("A", kc, half)], writes=[("ps", b)])
                        t = self.tq[self.tq_rr % 3]
                        tk = ("tq", self.tq_rr % 3)
                        self.tq_rr += 1
                        S.act((lambda b, t: (lambda e: e.activation(out=t[:], in_=self.PS[b][:], func=AF.Relu)))(b, t), reads=[("ps", b)], writes=[tk])
                        S.dve((lambda t, fc, cs: (lambda e: e.tensor_tensor(out=self.QT[:, fc, cs], in0=t[:], in1=t[:], op=ALU.mult)))(t, fc, cs),
                              reads=[tk], writes=[("Q", fc, half)])
            for g in range(8):
                s, wv = self.wslot_load(W["down"], qd * 2048, g * 256, 256, None)
                for ci in range(2):
                    fo = g * 2 + ci
                    for half in range(2):
                        cs = slice(half * 512, (half + 1) * 512)
                        b = self.next_ps(0, 4)
                        for fc in range(16):
                            S.pe((lambda b, wv, fc, ci, cs: (lambda e: e.matmul(self.PS[b][:], lhsT=wv[:, fc, ci * 128:(ci + 1) * 128], rhs=self.QT[:, fc, cs],
                                                                                start=(fc == 0), stop=(fc == 15))))(b, wv, fc, ci, cs),
                                 reads=[("slot", s), ("Q", fc, half)], writes=[("ps", b)])
                        S.dve((lambda b, fo, cs: (lambda e: e.scalar_tensor_tensor(out=self.xT[:, fo, cs], in0=self.PS[b][:], scalar=self.modcol(l, 5, fo),
                                                                                    in1=self.xT[:, fo, cs], op0=ALU.mult, op1=ALU.add)))(b, fo, cs),
                              reads=[("ps", b), ("x", fo, half), "v_mod"], writes=[("x", fo, half)])

    def layer_A(self, l):
        S = self.S
        vec = self.vec
        self.derive(l)
        self.norm_to_A(lambda ch: vec[:, self.V_DER + ch:self.V_DER + ch + 1], lambda ch: self.modcol(l, 0, ch), ("v_der", 0))
        spec = even_spec() if l % 2 == 0 else odd_spec()
        self.projections(l, spec)
        if l % 2 == 0:
            S.dma("sync", lambda e: e.dma_start(out=self.ssend[l][:, :], in_=self.XF[:, 2368:2432]), f"d_ss{l}", reads=["ksumT"], writes=[("ssend", l)])

    def layer_B(self, l):
        vec = self.vec
        if l % 2 == 0:
            self.even_attention(l)
        else:
            self.odd_attention(l)
        self.out_proj(l)
        if self.stop_after_mix and l == self.nlayers - 1:
            return
        self.norm_to_A(lambda ch: vec[:, self.V_DER + 16 + ch:self.V_DER + 16 + ch + 1], lambda ch: self.modcol(l, 3, ch), ("v_der", 1))
        self.mlp(l)

    def save_state(self):
        S = self.S
        allx = [("x", c, h) for c in range(16) for h in range(2)]
        allq = [("Q", c, h) for c in range(16) for h in range(2)]
        S.dma("sync", lambda e: e.dma_start(out=self._xso.rearrange("c p t -> p c t"), in_=self.xT[:]), "d_so0", reads=allx, writes=["xso"])
        S.dma("sync", lambda e: e.dma_start(out=self._qso, in_=self.QT[:].rearrange("p c t -> p (c t)")), "d_so1", reads=allq, writes=["qso"])
        S.dma("sync", lambda e: e.dma_start(out=self._gso, in_=self.gsb[:].rearrange("p j n -> p (j n)")), "d_so2", reads=["gsb"], writes=["gso"])
        S.dma("sync", lambda e: e.dma_start(out=self._pso, in_=self.vec[:, self.V_POSB:self.V_POSB + 2]), "d_so3", reads=["v_posb"], writes=["pso"])

    def restore_state(self):
        S = self.S
        allq = [("Q", c, h) for c in range(16) for h in range(2)]
        S.dma("sync", lambda e: e.dma_start(out=self.QT[:].rearrange("p c t -> p (c t)"), in_=self.nc_in("qstate_in")), "d_si1", writes=allq)
        S.dma("sync", lambda e: e.dma_start(out=self.gsb[:].rearrange("p j n -> p (j n)"), in_=self.nc_in("gstate_in")), "d_si2", writes=["gsb"])
        S.dma("sync", lambda e: e.dma_start(out=self.vec[:, self.V_POSB:self.V_POSB + 2], in_=self.nc_in("pstate_in")), "d_si3", writes=["v_posb"])

    def store_x(self):
        S = self.S
        outT = self._outT
        self.out_ops = [S.dma("sync", lambda e: e.dma_start(out=outT.rearrange("c p t -> p c t"), in_=self.xT[:]), "d_out",
                              reads=[("x", c, h) for c in range(16) for h in range(2)], writes=["outT"])]

    def final(self):
        S = self.S
        vec = self.vec
        A3 = self.A[:].rearrange("p (c t) -> p c t", c=16)
        for half in range(2):
            cs = slice(half * 512, (half + 1) * 512)
            bank = 4 + half
            for ch in range(16):
                sq = self.tq[ch % 2]
                S.act((lambda sq, ch, cs: (lambda e: e.activation(out=sq[:], in_=self.xT[:, ch, cs], func=AF.Square)))(sq, ch, cs),
                      reads=[("x", ch, half)], writes=[("tq", ch % 2)])
                S.pe((lambda sq, ch, bank: (lambda e: e.matmul(self.PS[bank][:], lhsT=self.onesf, rhs=sq[:], start=(ch == 0), stop=(ch == 15))))(sq, ch, bank),
                     reads=[("tq", ch % 2), "cf"], writes=[("ps", bank)])
            rs = self.XF[:, 1024 + half * 512:1024 + (half + 1) * 512]
            S.act((lambda bank, rs: (lambda e: e.activation(out=rs, in_=self.PS[bank][:], func=AF.Sqrt, scale=1.0 / D, bias=EPS)))(bank, rs),
                  reads=[("ps", bank)], writes=[("rstd", half)])
            S.dve((lambda rs: (lambda e: e.reciprocal(out=rs, in_=rs)))(rs), reads=[("rstd", half)], writes=[("rstd", half)])
            for ch in range(16):
                S.dve((lambda ch, cs, rs: (lambda e: e.scalar_tensor_tensor(out=self.xT[:, ch, cs], in0=self.xT[:, ch, cs],
                                                                            scalar=vec[:, self.V_G + 128 + ch:self.V_G + 128 + ch + 1], in1=rs,
                                                                            op0=ALU.mult, op1=ALU.mult)))(ch, cs, rs),
                      reads=[("x", ch, half), ("rstd", half), "v_g"], writes=[("x", ch, half)])
        self.store_x()

    def odd_small(self, l):
        S = self.S
        XF = self.XF
        j2 = l // 2
        A3 = self.A[:].rearrange("p (c t) -> p c t", c=16)
        s = self.slot_rr % 3
        self.slot_rr += 1
        wv = self.SL[s][:, 0:512].rearrange("p (k n) -> p k n", k=16)
        wsm = self.W[l]["sm"]
        S.dma("gpsimd", lambda e: e.dma_start(out=wv, in_=wsm.rearrange("(k p) n -> p k n", p=128)), f"d_slot{s}",
              writes=[("slot", s), ("slotv", s), ("slotw", s), ("slotx", s)])
        S.dma("sync", lambda e: e.dma_start(out=XF[:, 2072:2080], in_=self.od["fbias"][:, j2 * 8:(j2 + 1) * 8]), "d_fb", writes=["fbias"])
        for j in range(8):
            for kc in range(16):
                S.pe((lambda kc, j: (lambda e: e.matmul(self.PS[6][:, j * 32:(j + 1) * 32], lhsT=A3[:, kc, j * 128:(j + 1) * 128], rhs=wv[:, kc, :],
                                                        start=(kc == 0), stop=(kc == 15))))(kc, j),
                     reads=[("slot", s), ("A", kc, j // 4)], writes=[("ps", 6)])
        ps3 = self.PS[6][:, 0:256].rearrange("p (j n) -> p j n", j=8)
        z = XF[:, 1536:1600].rearrange("p (j h) -> p j h", j=8)
        S.dve(lambda e: e.tensor_tensor(out=z, in0=ps3[:, :, 0:8], in1=XF[:, 2072:2080].unsqueeze(1).broadcast_to([128, 8, 8]), op=ALU.add),
              reads=[("ps", 6), "fbias"], writes=[("rstd", 1)])
        S.act(lambda e: e.activation(out=XF[:, 1536:1600], in_=XF[:, 1536:1600], func=AF.Exp, scale=-1.0), reads=[("rstd", 1)], writes=[("rstd", 1)])
        S.act(lambda e: e.activation(out=XF[:, 1536:1600], in_=XF[:, 1536:1600], func=AF.Ln, bias=1.0), reads=[("rstd", 1)], writes=[("rstd", 1)])
        S.act(lambda e: e.activation(out=self.gsb[:], in_=ps3[:, :, 8:32], func=AF.Sigmoid), reads=[("ps", 6)], writes=["gsb"])
        S.dma("sync", lambda e: e.dma_start(out=self.ssend[l][:, 0:64], in_=XF[:, 1536:1600]), f"d_ss{l}", reads=[("rstd", 1)], writes=[("ssend", l)])

    def cmp_partials(self, l, idx):
        S = self.S
        XF = self.XF
        j2 = l // 2
        typ = idx // 2
        kb = idx % 2
        if idx % 2 == 0:
            s = self.slot_rr % 3
            self.slot_rr += 1
            self.w1slot = s
            w1v = self.SL[s][:, 0:4096].rearrange("p (l m) -> p l m", l=32)
            self.w1v = w1v
            src = self.od["w1"][j2][typ].rearrange("(l d) m -> d l m", d=128)
            S.dma("gpsimd", lambda e: e.dma_start(out=w1v, in_=src), f"d_slot{s}", writes=[("slot", s), ("slotv", s), ("slotw", s), ("slotx", s)])
            if typ == 0:
                S.dma("gpsimd", lambda e: e.dma_start(out=self.posT[:], in_=self.od["posT"][:, j2 * 64:(j2 + 1) * 64].rearrange("p (t l) -> p t l", t=2)),
                      "d_pos", writes=["posT"])
            for li in range(32):
                S.pe((lambda li, w1v, typ: (lambda e: e.matmul(self.PS[6][:, 300:301], lhsT=w1v[:, li, :], rhs=self.posT[:, typ, li:li + 1],
                                                               start=(li == 0), stop=(li == 31))))(li, w1v, typ),
                     reads=[("slot", s), "posT"], writes=[("ps", 6)])
            S.act((lambda typ: (lambda e: e.copy(out=self.vec[:, self.V_POSB + typ:self.V_POSB + typ + 1], in_=self.PS[6][:, 300:301])))(typ),
                  reads=[("ps", 6)], writes=["v_posb"])
        s = self.w1slot
        w1v = self.w1v
        kv = self.kst[kb][:].rearrange("p (n r) -> p n r", r=16)
        for ab in range(2):
            b = self.next_ps(0, 4)
            for li in range(16):
                S.pe((lambda b, li, ab, w1v, kv: (lambda e: e.matmul(self.PS[b][:, 0:64], lhsT=w1v[:, ab * 16 + li, :], rhs=kv[:, :, li],
                                                                      start=(li == 0), stop=(li == 15))))(b, li, ab, w1v, kv),
                     reads=[("slot", s), ("kst", kb)], writes=[("ps", b)])
            o = 1024 + idx * 128 + ab * 64
            S.act((lambda b, o: (lambda e: e.copy(out=XF[:, o:o + 64], in_=self.PS[b][:, 0:64])))(b, o), reads=[("ps", b)], writes=[("rstd", 0)])
        if idx == 3:
            S.dma("sync", lambda e: e.dma_start(out=self.ssend[l][:, 64:576], in_=XF[:, 1024:1536]), f"d_ss{l}", reads=[("rstd", 0)], writes=[("ssend", l)])

    def gate_bcast(self, bank, c0, n, head, br, tcols):
        S = self.S
        A = self.A
        r = head * 3 + br
        wh = A[0:24, r * 64:(r + 1) * 64]
        gT = A[0:24, 10624:11648]
        for half in range(2):
            S.pe((lambda half: (lambda e: e.matmul(self.PS[bank][half * 64:(half + 1) * 64, c0:c0 + n], lhsT=wh, rhs=gT[:, tcols], start=True, stop=True)))(half),
                 reads=["wh", "gT"], writes=[("ps", bank)])

    def odd_attention(self, l):
        S = self.S
        A = self.A
        XF = self.XF
        od = self.od
        j2 = l // 2
        g3 = self.gath[l].ap().rearrange("(c r) t -> r c t", c=NCORE)
        sg = self.sgath[l].ap().rearrange("(c p) f -> p c f", p=128)
        wh = A[:, 0:4096]
        winm = A[:, 4096:5632].rearrange("p (s q) -> p s q", s=12)
        cmpm = A[:, 5632:6016].rearrange("p (s q) -> p s q", s=3)
        selbTn = [A[:, 6016:7040], A[:, 7040:8064]]
        ovl = A[:, 8064:8576].rearrange("p (n b) -> p n b", n=4)
        kcT = [A[:, 8576:9088], A[:, 9088:9600]]
        vcv = [A[:, 9600:10112].rearrange("p (n d) -> p n d", n=4), A[:, 10112:10624].rearrange("p (n d) -> p n d", n=4)]
        gT = A[0:24, 10624:11648]
        odacc = A[:, 11648:15744].rearrange("p (h t) -> p h t", h=4)
        hid = A[:, 15744:16256]
        akeys = [("A", c, h) for c in range(16) for h in range(2)]
        S.dma("sync", lambda e: e.dma_start(out=wh, in_=od["wh"]), "d_ac0", writes=akeys + ["wh"])
        S.dma("sync", lambda e: e.dma_start(out=A[:, 4096:5632], in_=od["winmask"]), "d_ac1", writes=["winm"])
        S.dma("sync", lambda e: e.dma_start(out=A[:, 5632:6016], in_=od["cmpmask"]), "d_ac2", writes=["cmpm"])
        S.dma("sync", lambda e: e.dma_start(out=A[:, 8064:8576], in_=od["ovl"]), "d_ac3", writes=["ovl"])
        S.dma("sync", lambda e: e.dma_start(out=XF[:, 576:816], in_=od["selbase"]), "d_ac4", writes=["selbase"])
        S.dma("sync", lambda e: e.dma_start(out=XF[:, 2064:2072], in_=od["onehot"]), "d_ac5", writes=["onehot"])
        S.dma("gpsimd", lambda e: e.dma_start(out=self.w2sb[:], in_=od["w2"][j2].rearrange("p (t m) -> p t m", t=2)), "d_w2", writes=["w2sb"])
        for j in range(8):
            S.pe((lambda j: (lambda e: e.transpose(self.PSB[0:24, j * 128:(j + 1) * 128], self.gsb[:, j, :], self.identb)))(j),
                 reads=["gsb", "cb"], writes=["psb"])
        S.act(lambda e: e.copy(out=gT, in_=self.PSB[0:24, :]), reads=["psb"], writes=["gT"])
        for idx in range(4):
            typ, g = idx // 2, idx % 2
            for ab in range(2):
                S.dma("sync", (lambda idx, ab: (lambda e: e.dma_start(out=self.tq[ab][:].rearrange("p (c f) -> p c f", c=8),
                                                                       in_=sg[:, :, 64 + idx * 128 + ab * 64:64 + idx * 128 + (ab + 1) * 64])))(idx, ab),
                      f"d_tq{ab}", reads=[("sgath", l)], writes=[("tq", ab)])
            t2 = self.tq[2]
            pbg = XF[:, 1024:1536]
            S.dve(lambda e: e.tensor_copy(out=t2[:].rearrange("p (j c m) -> p j c m", j=8, c=8),
                                          in_=self.tq[0][:].rearrange("p (c j m) -> p c j m", c=8, j=8).transpose([0, 2, 1, 3])),
                  reads=[("tq", 0)], writes=[("tq", 2)])
            S.dve(lambda e: e.tensor_copy(out=pbg.rearrange("p (j c m) -> p j c m", j=8, c=8),
                                          in_=self.tq[1][:].rearrange("p (c j m) -> p c j m", c=8, j=8).transpose([0, 2, 1, 3])),
                  reads=[("tq", 1)], writes=[("rstd", 0)])
            S.dve(lambda e: e.tensor_tensor(out=t2[:, 0:511], in0=t2[:, 0:511], in1=pbg[:, 1:512], op=ALU.add), reads=[("tq", 2), ("rstd", 0)], writes=[("tq", 2)])
            pb = self.vec[:, self.V_POSB + typ:self.V_POSB + typ + 1]
            S.dve((lambda pb: (lambda e: e.tensor_scalar(out=t2[:], in0=t2[:], scalar1=pb, scalar2=None, op0=ALU.add)))(pb), reads=[("tq", 2), "v_posb"], writes=[("tq", 2)])
            x2 = XF[:, 1536:2048]
            S.dve(lambda e: e.tensor_tensor(out=x2, in0=t2[:], in1=t2[:], op=ALU.mult), reads=[("tq", 2)], writes=[("rstd", 1)])
            S.dve(lambda e: e.tensor_scalar(out=x2, in0=x2, scalar1=0.044715, scalar2=1.0, op0=ALU.mult, op1=ALU.add), reads=[("rstd", 1)], writes=[("rstd", 1)])
            S.dve(lambda e: e.tensor_tensor(out=x2, in0=x2, in1=t2[:], op=ALU.mult), reads=[("rstd", 1), ("tq", 2)], writes=[("rstd", 1)])
            S.act(lambda e: e.activation(out=x2, in_=x2, func=AF.Tanh, scale=0.7978845608028654), reads=[("rstd", 1)], writes=[("rstd", 1)])
            S.dve(lambda e: e.tensor_scalar(out=x2, in0=x2, scalar1=1.0, scalar2=0.5, op0=ALU.add, op1=ALU.mult), reads=[("rstd", 1)], writes=[("rstd", 1)])
            S.dve(lambda e: e.tensor_tensor(out=hid, in0=x2, in1=t2[:], op=ALU.mult), reads=[("rstd", 1), ("tq", 2)], writes=["hid"])
            if typ == 0:
                S.pe(lambda e: e.matmul(self.PS[6][:], lhsT=self.w2sb[:, 0, :], rhs=hid, start=True, stop=True), reads=["hid", "w2sb"], writes=[("ps", 6)])
                S.act((lambda g: (lambda e: e.copy(out=kcT[g], in_=self.PS[6][:])))(g), reads=[("ps", 6)], writes=[("kcT", g)])
            else:
                for nt in range(4):
                    S.pe((lambda nt: (lambda e: e.matmul(self.PS[6][:, nt * 128:(nt + 1) * 128], lhsT=hid[:, nt * 128:(nt + 1) * 128], rhs=self.w2sb[:, 1, :],
                                                         start=True, stop=True)))(nt),
                         reads=["hid", "w2sb"], writes=[("ps", 6)])
                S.act((lambda g: (lambda e: e.copy(out=vcv[g], in_=self.PS[6][:].rearrange("p (n d) -> p n d", n=4))))(g), reads=[("ps", 6)], writes=[("vcv", g)])
        lfg = self.tq[0]
        S.dma("sync", lambda e: e.dma_start(out=lfg[:].rearrange("p (c f) -> p c f", c=8), in_=sg[:, :, 0:64]), "d_tq0", reads=[("sgath", l)], writes=[("tq", 0)])
        S.pe(lambda e: e.matmul(self.PS[4][:], lhsT=self.tri, rhs=lfg[:], start=True, stop=True), reads=[("tq", 0), "cf"], writes=[("ps", 4)])
        S.pe(lambda e: e.matmul(self.PS[5][:], lhsT=self.onesf, rhs=lfg[:], start=True, stop=True), reads=[("tq", 0), "cf"], writes=[("ps", 5)])
        ta, tb = self.tq[1], self.tq[2]
        perm = lambda ap: ap.rearrange("p (c j h) -> p c j h", c=8, j=8).transpose([0, 2, 1, 3])
        glob = lambda ap: ap.rearrange("p (j c h) -> p j c h", j=8, c=8)
        S.dve(lambda e: e.tensor_copy(out=glob(ta[:]), in_=perm(self.PS[5][:])), reads=[("ps", 5)], writes=[("tq", 1)])
        S.dve(lambda e: e.memset(tb[:, 0:8], 0.0), writes=[("tq", 2)])
        S.dve(lambda e: e.tensor_copy(out=tb[:, 8:512], in_=ta[:, 0:504]), reads=[("tq", 1)], writes=[("tq", 2)])
        cur, nxt, ck_, nk_ = tb, ta, ("tq", 2), ("tq", 1)
        for sh in (1, 2, 4, 8, 16, 32):
            w = sh * 8
            S.dve((lambda cur, nxt, w: (lambda e: e.tensor_tensor(out=nxt[:, w:512], in0=cur[:, w:512], in1=cur[:, 0:512 - w], op=ALU.add)))(cur, nxt, w),
                  reads=[ck_], writes=[nk_])
            S.dve((lambda cur, nxt, w: (lambda e: e.tensor_copy(out=nxt[:, 0:w], in_=cur[:, 0:w])))(cur, nxt, w), reads=[ck_], writes=[nk_])
            cur, nxt, ck_, nk_ = nxt, cur, nk_, ck_
        off = cur
        offk = ck_
        NCf = XF[:, 0:512]
        S.dve(lambda e: e.tensor_tensor(out=glob(NCf), in0=perm(self.PS[4][:]), in1=glob(off[:]), op=ALU.add), reads=[("ps", 4), offk], writes=["NCf"])
        tmp = nxt
        S.dve(lambda e: e.tensor_tensor(out=tmp[:].rearrange("p (j h c) -> p j h c", j=8, h=8),
                                        in0=glob(off[:]).transpose([0, 1, 3, 2]),
                                        in1=XF[:, 2064:2072].unsqueeze(1).unsqueeze(1).broadcast_to([128, 8, 8, 8]), op=ALU.mult),
              reads=[offk, "onehot"], writes=[nk_])
        offq = XF[:, 512:576]
        S.dve(lambda e: e.tensor_reduce(out=offq, in_=tmp[:].rearrange("p (a c) -> p a c", c=8), axis=AX.X, op=ALU.add), reads=[nk_], writes=["offq"])
        NC3 = NCf.rearrange("p (b h) -> p b h", h=8)
        oq3 = offq.rearrange("p (j h) -> p j h", h=8)
        ni = 24
        for h in range(8):
            fbt = XF[:, 1024 + (h % 2) * 512:1024 + (h % 2 + 1) * 512]
            fbk = ("rstd", h % 2)
            fb3 = fbt.rearrange("p (b j) -> p b j", j=8)
            S.dve((lambda h, fb3: (lambda e: e.tensor_tensor(out=fb3, in0=NC3[:, :, h].unsqueeze(2).broadcast_to([128, 64, 8]),
                                                             in1=oq3[:, :, h].unsqueeze(1).broadcast_to([128, 64, 8]), op=ALU.subtract)))(h, fb3),
                  reads=["NCf", "offq"], writes=[fbk])
            for cp in range(4):
                s, kd, vd = self.kv_load(l, h, 12 + h, cp, ni)
                for cc in range(2):
                    ck = 2 * cp + cc
                    for jk in range(8):
                        b = 8 * jk + ck
                        groups = []
                        if jk < 4:
                            groups.append((0, jk * 128, 512))
                        groups.append((1, max(512, jk * 128), 1024))
                        for (grp, c0, c1) in groups:
                            n = c1 - c0
                            sbank = self.next_ps(0, 3)
                            extra = []
                            if c0 == jk * 128:
                                extra.append((self.identb, self.cmask[:, ck, :], 0, 128, ["cb", "cmask"]))
                            ebias = [((jq * 128 - c0), (jq * 128 - c0) + 128, fb3[:, b, jq:jq + 1], [fbk]) for jq in range(c0 // 128, c1 // 128)]
                            last = (ck == 7 and jk == (3 if grp == 0 else 7))
                            fst = (ck == 0 and jk == 0)
                            self.attn_unit(sbank, kd[:, cc, jk * 128:(jk + 1) * 128], vd[:, cc, jk, :], self.QT[:, h, c0:c1], n, extra, ebias,
                                           3 + grp, 5 + grp, slice(c0 - grp * 512, c1 - grp * 512), fst, last,
                                           [("slot", s), ("Q", h, grp)], [("slotv", s)])
            for grp in range(2):
                self.finalize_head(3 + grp, 5 + grp, self.QT[:, h, grp * 512:(grp + 1) * 512], [("Q", h, grp)])
        impa, tmp2, m1, m2 = XF[:, 816:944], XF[:, 944:1072], XF[:, 2048:2056], XF[:, 2056:2064]
        def nsa_group(g):
            qh0 = 8 + 4 * g
            for j in range(8):
                ntb = j // 2
                qap = self.QT[:, qh0:qh0 + 4, j * 128:(j + 1) * 128]
                qkeys = [("Q", qh0 + hh, j // 4) for hh in range(4)]
                for nt in range(ntb + 1):
                    sbank = self.next_ps(0, 3)
                    slot_m = None
                    if nt == ntb:
                        slot_m = 0 if j % 2 == 0 else 1
                    elif nt == ntb - 1 and j % 2 == 0:
                        slot_m = 2
                    S.pe((lambda sbank, nt, qap, slot_m: (lambda e: e.matmul(self.PS[sbank][:], lhsT=kcT[g][:, nt * 128:(nt + 1) * 128], rhs=qap,
                                                                             start=True, stop=(slot_m is None))))(sbank, nt, qap, slot_m),
                         reads=[("kcT", g)] + qkeys, writes=[("ps", sbank)])
                    if slot_m is not None:
                        for hh in range(4):
                            S.pe((lambda sbank, hh, slot_m: (lambda e: e.matmul(self.PS[sbank][:, hh * 128:(hh + 1) * 128], lhsT=self.identb, rhs=cmpm[:, slot_m, :],
                                                                                start=False, stop=(hh == 3))))(sbank, hh, slot_m),
                                 reads=["cb", "cmpm"], writes=[("ps", sbank)])
                    S.act((lambda sbank, nt: (lambda e: e.activation(out=self.PC[:, nt, :], in_=self.PS[sbank][:], func=AF.Exp, scale=SCALE)))(sbank, nt),
                          reads=[("ps", sbank)], writes=[("PC", nt)])
                    S.pe((lambda nt: (lambda e: e.matmul(self.PS[3][:], lhsT=vcv[g][:, nt, :], rhs=self.PC[:, nt, :], start=(nt == 0), stop=(nt == ntb),
                                                         skip_group_check=True)))(nt),
                         reads=[("PC", nt), ("vcv", g)], writes=[("ps", 3)])
                    S.pe((lambda nt: (lambda e: e.matmul(self.PS[5][:], lhsT=self.onesb, rhs=self.PC[:, nt, :], start=(nt == 0), stop=(nt == ntb),
                                                         skip_group_check=True)))(nt),
                         reads=[("PC", nt), "cb"], writes=[("ps", 5)])
                rd = self.tq[self.tq_rr % 3]
                rk = ("tq", self.tq_rr % 3)
                self.tq_rr += 1
                S.dve((lambda rd: (lambda e: e.tensor_scalar(out=rd[:], in0=self.PS[5][:], scalar1=1e-30, scalar2=None, op0=ALU.max)))(rd), reads=[("ps", 5)], writes=[rk])
                S.dve((lambda rd: (lambda e: e.reciprocal(out=rd[:], in_=rd[:])))(rd), reads=[rk], writes=[rk])
                for nt in range(ntb + 1):
                    S.dve((lambda rd, nt: (lambda e: e.tensor_tensor(out=self.PC[:, nt, :], in0=self.PC[:, nt, :], in1=rd[:], op=ALU.mult)))(rd, nt),
                          reads=[("PC", nt), rk], writes=[("PC", nt)])
                nmm = 4 * (ntb + 1)
                k = 0
                for hh in range(4):
                    for nt in range(ntb + 1):
                        S.pe((lambda hh, nt, k: (lambda e: e.matmul(self.PS[6][:, 0:128], lhsT=self.PC[:, nt, hh * 128:(hh + 1) * 128], rhs=ovl[:, nt, :],
                                                                    start=(k == 0), stop=(k == nmm - 1))))(hh, nt, k),
                             reads=[("PC", nt), "ovl"], writes=[("ps", 6)])
                        k += 1
                w0 = 576 + 112 - 16 * j
                S.dve((lambda w0: (lambda e: e.tensor_tensor(out=impa, in0=self.PS[6][:, 0:128], in1=XF[:, w0:w0 + 128], op=ALU.add)))(w0),
                      reads=[("ps", 6), "selbase"], writes=["impa"])
                S.dve(lambda e: e.memset(impa[:, 0:1], 100.0), reads=["impa"], writes=["impa"])
                S.dve(lambda e: e.max(out=m1, in_=impa), reads=["impa"], writes=["m1"])
                S.dve(lambda e: e.match_replace(out=tmp2, in_to_replace=m1, in_values=impa, imm_value=-1e30), reads=["impa", "m1"], writes=["tmp2"])
                S.dve(lambda e: e.max(out=m2, in_=tmp2), reads=["tmp2"], writes=["m2"])
                S.dve(lambda e: e.tensor_scalar(out=tmp2, in0=impa, scalar1=m2[:, 7:8], scalar2=None, op0=ALU.is_ge), reads=["impa", "m2"], writes=["tmp2"])
                S.dve(lambda e: e.tensor_scalar(out=self.selb[:, 0:128], in0=tmp2, scalar1=-1.0, scalar2=-NEG, op0=ALU.add, op1=ALU.mult), reads=["tmp2"], writes=["selb"])
                S.pe(lambda e: e.transpose(self.PSB[:, 0:128], self.selb[:, 0:128], self.identb), reads=["selb", "cb"], writes=["psb"])
                S.act((lambda j: (lambda e: e.copy(out=selbTn[g][:, j * 128:(j + 1) * 128], in_=self.PSB[:, 0:128])))(j), reads=["psb"], writes=[("selbTn", g)])
                for hh in range(4):
                    self.gate_bcast(4, hh * 128, 128, 4 * g + hh, 0, slice(j * 128, (j + 1) * 128))
                S.dve((lambda rd: (lambda e: e.tensor_tensor(out=rd[:], in0=self.PS[4][:], in1=rd[:], op=ALU.mult)))(rd), reads=[("ps", 4), rk], writes=[rk])
                S.dve((lambda rd, j: (lambda e: e.tensor_tensor(out=odacc[:, :, j * 128:(j + 1) * 128], in0=self.PS[3][:].rearrange("p (h t) -> p h t", h=4),
                                                                in1=rd[:].rearrange("p (h t) -> p h t", h=4), op=ALU.mult)))(rd, j),
                      reads=[("ps", 3), rk], writes=[("odacc", j)])
            for hh in range(4):
                h = qh0 + hh
                for cp in range(4):
                    s, kd, vd = self.kv_load(l, 8 + g, 20 + g, cp, ni)
                    for cc in range(2):
                        ck = 2 * cp + cc
                        for jk in range(8):
                            gb = 8 * jk + ck
                            pbase = 64 * ((2 * gb) // 64)
                            r0 = (2 * gb) % 64
                            groups = []
                            if jk < 4:
                                groups.append((0, jk * 128, 512))
                            groups.append((1, max(512, jk * 128), 1024))
                            for (grp, c0, c1) in groups:
                                n = c1 - c0
                                sbank = self.next_ps(0, 3)
                                extra = []
                                if c0 == jk * 128:
                                    extra.append((self.identb, self.cmask[:, ck, :], 0, 128, ["cb", "cmask"]))
                                extra.append((wh[pbase:pbase + 64, r0 * 64:r0 * 64 + 128], selbTn[g][pbase:pbase + 64, c0:c1], 0, n, ["wh", ("selbTn", g)]))
                                last = (ck == 7 and jk == (3 if grp == 0 else 7))
                                fst = (ck == 0 and jk == 0)
                                self.attn_unit(sbank, kd[:, cc, jk * 128:(jk + 1) * 128], vd[:, cc, jk, :], self.QT[:, h, c0:c1], n, extra, None,
                                               3 + grp, 5 + grp, slice(c0 - grp * 512, c1 - grp * 512), fst, last,
                                               [("slot", s), ("Q", h, grp)], [("slotv", s)])
                self.attn_flush()
                for grp in range(2):
                    rd = self.tq[self.tq_rr % 3]
                    rk = ("tq", self.tq_rr % 3)
                    self.tq_rr += 1
                    t2_ = self.tq[self.tq_rr % 3]
                    tk2 = ("tq", self.tq_rr % 3)
                    self.tq_rr += 1
                    S.dve((lambda rd, grp: (lambda e: e.tensor_scalar(out=rd[:], in0=self.PS[5 + grp][:], scalar1=1e-30, scalar2=None, op0=ALU.max)))(rd, grp),
                          reads=[("ps", 5 + grp)], writes=[rk])
                    S.dve((lambda rd: (lambda e: e.reciprocal(out=rd[:], in_=rd[:])))(rd), reads=[rk], writes=[rk])
                    gbank = 5 + grp
                    self.gate_bcast(gbank, 0, 512, 4 * g + hh, 1, slice(grp * 512, (grp + 1) * 512))
                    S.dve((lambda rd, gbank: (lambda e: e.tensor_tensor(out=rd[:], in0=self.PS[gbank][:], in1=rd[:], op=ALU.mult)))(rd, gbank),
                          reads=[("ps", gbank), rk], writes=[rk])
                    S.dve((lambda rd, t2_, grp: (lambda e: e.tensor_tensor(out=t2_[:], in0=self.PS[3 + grp][:], in1=rd[:], op=ALU.mult)))(rd, t2_, grp),
                          reads=[("ps", 3 + grp), rk], writes=[tk2])
                    oa = odacc[:, hh, grp * 512:(grp + 1) * 512]
                    S.dve((lambda t2_, oa: (lambda e: e.tensor_tensor(out=oa, in0=oa, in1=t2_[:], op=ALU.add)))(t2_, oa),
                          reads=[tk2] + [("odacc", jq) for jq in range(grp * 4, grp * 4 + 4)], writes=[("odacc", jq) for jq in range(grp * 4, grp * 4 + 4)])
            kitem, vitem = 10 + g, 22 + g
            for j in range(8):
                s = self.slot_rr % 3
                self.slot_rr += 1
                slot = self.SL[s]
                kw = slot[:, 0:1536].rearrange("p (s t) -> p s t", s=12)
                vw = slot[:, 2048:2048 + 1536].rearrange("p (s t) -> p s t", s=12)
                rows_k = slice(kitem * 128, (kitem + 1) * 128)
                rows_v = slice(vitem * 128, (vitem + 1) * 128)
                S.dma("sync", (lambda kw, j, rows_k: (lambda e: e.dma_start(out=kw[:, 4:12, :], in_=g3[rows_k, :, j * 128:(j + 1) * 128])))(kw, j, rows_k),
                      f"d_slot{s}", reads=[("gath", l)], writes=[("slot", s)])
                S.dma("sync", (lambda vw, j, rows_v: (lambda e: e.dma_start(out=vw[:, 4:12, :], in_=g3[rows_v, :, j * 128:(j + 1) * 128])))(vw, j, rows_v),
                      f"d_slotv{s}", reads=[("gath", l)], writes=[("slotv", s)])
                if j > 0:
                    S.dma("sync", (lambda kw, j, rows_k: (lambda e: e.dma_start(out=kw[:, 0:4, :], in_=g3[rows_k, 4:8, (j - 1) * 128:j * 128])))(kw, j, rows_k),
                          f"d_slotw{s}", reads=[("gath", l)], writes=[("slotw", s)])
                    S.dma("sync", (lambda vw, j, rows_v: (lambda e: e.dma_start(out=vw[:, 0:4, :], in_=g3[rows_v, 4:8, (j - 1) * 128:j * 128])))(vw, j, rows_v),
                          f"d_slotx{s}", reads=[("gath", l)], writes=[("slotx", s)])
                cands = list(range(0 if j > 0 else 4, 12))
                qap = self.QT[:, qh0:qh0 + 4, j * 128:(j + 1) * 128]
                qkeys = [("Q", qh0 + hh, j // 4) for hh in range(4)]
                ob, db = 3 + (j % 2), 5 + (j % 2)
                for si, sidx in enumerate(cands):
                    sbank = self.next_ps(0, 3)
                    extra = [(self.identb, winm[:, sidx, :], hh * 128, (hh + 1) * 128, ["cb", "winm"]) for hh in range(4)]
                    self.attn_unit(sbank, kw[:, sidx, :], vw[:, sidx, :], qap, 512, extra, None, ob, db, slice(0, 512),
                                   si == 0, si == len(cands) - 1, [("slot", s), ("slotw", s)] + qkeys, [("slotv", s), ("slotx", s)])
                self.attn_flush()
                rd = self.tq[self.tq_rr % 3]
                rk = ("tq", self.tq_rr % 3)
                self.tq_rr += 1
                t2_ = self.tq[self.tq_rr % 3]
                tk2 = ("tq", self.tq_rr % 3)
                self.tq_rr += 1
                S.dve((lambda rd, db: (lambda e: e.tensor_scalar(out=rd[:], in0=self.PS[db][:], scalar1=1e-30, scalar2=None, op0=ALU.max)))(rd, db),
                      reads=[("ps", db)], writes=[rk])
                S.dve((lambda rd: (lambda e: e.reciprocal(out=rd[:], in_=rd[:])))(rd), reads=[rk], writes=[rk])
                for hh in range(4):
                    self.gate_bcast(db, hh * 128, 128, 4 * g + hh, 2, slice(j * 128, (j + 1) * 128))
                S.dve((lambda rd, db: (lambda e: e.tensor_tensor(out=rd[:], in0=self.PS[db][:], in1=rd[:], op=ALU.mult)))(rd, db), reads=[("ps", db), rk], writes=[rk])
                S.dve((lambda rd, t2_, ob: (lambda e: e.tensor_tensor(out=t2_[:], in0=self.PS[ob][:], in1=rd[:], op=ALU.mult)))(rd, t2_, ob),
                      reads=[("ps", ob), rk], writes=[tk2])
                S.dve((lambda t2_, j, qap: (lambda e: e.tensor_tensor(out=qap, in0=odacc[:, :, j * 128:(j + 1) * 128],
                                                                      in1=t2_[:].rearrange("p (h t) -> p h t", h=4), op=ALU.add)))(t2_, j, qap),
                      reads=[tk2, ("odacc", j)], writes=qkeys)

        for g in range(2):
            nsa_group(g)

def tok_index(c):
    j = np.arange(8)[:, None]
    i = np.arange(128)[None, :]
    return ((8 * j + c) * 128 + i).reshape(-1)


def bf(a):
    return np.ascontiguousarray(a.astype(ml_dtypes.bfloat16))


def host_consts(c):
    tok = tok_index(c)
    inv = 1.0 / (10000.0 ** (np.arange(0, 128, 2, dtype=np.float32) / 128.0))
    ang = tok.astype(np.float32)[:, None] * inv[None, :]
    cos = np.cos(ang).astype(np.float32).T
    sin = np.sin(ang).astype(np.float32).T
    ropecos = np.concatenate([cos, cos], 0)
    ropesin = np.concatenate([-sin, sin], 0)
    ii = np.arange(128)
    cm = np.zeros((128, 8, 128), np.float32)
    for ck in range(8):
        if ck > c:
            cm[:, ck, :] = NEG
        elif ck == c:
            cm[:, ck, :] = np.where(ii[:, None] <= ii[None, :], 0.0, NEG)
    identf = np.eye(128, dtype=np.float32)
    Rm = np.zeros((128, 128), np.float32)
    for m in range(128):
        Rm[(m + 64) % 128, m] = 1.0
    onesf = np.ones((128, 128), np.float32)
    tri = (ii[:, None] <= ii[None, :]).astype(np.float32)
    constf = np.concatenate([identf, Rm, onesf, tri], 1)
    constb = bf(np.concatenate([identf, onesf], 1))
    wm = np.zeros((32, 4096), np.float32)
    for k in range(32):
        wm[k, k * 128:(k + 1) * 128] = 1.0
    sw = np.full((128, 9, 128), NEG, np.float32)
    for s in range(9):
        diff = c + 1 - s
        if diff == 0:
            sw[:, s, :] = np.where(ii[:, None] <= ii[None, :], 0.0, NEG)
        elif diff == 1:
            sw[:, s, :] = np.where(ii[:, None] > ii[None, :], 0.0, NEG)
    gb = np.zeros((128, 8, 32), np.float32)
    past = np.zeros((128, 8, 32), np.float32)
    own = np.zeros((128, 8, 32), np.float32)
    for j in range(8):
        o = (8 * j + c) // 2
        gb[:, j, o:] = -1e30
        past[:, j, :o] = 1.0
        own[:, j, o] = 1.0
    gsel = np.concatenate([gb.reshape(128, 256), past.reshape(128, 256), own.reshape(128, 256)], 1)
    wh = np.zeros((128, 4096), np.float32)
    for k in range(128):
        wh[k, (k % 64) * 64:(k % 64 + 1) * 64] = 1.0
    wn = np.full((128, 12, 128), NEG, np.float32)
    for s in range(12):
        diff = c + 4 - s
        if diff == 0:
            wn[:, s, :] = np.where(ii[:, None] <= ii[None, :], 0.0, NEG)
        elif diff in (1, 2, 3):
            wn[:, s, :] = 0.0
        elif diff == 4:
            wn[:, s, :] = np.where(ii[:, None] > ii[None, :], 0.0, NEG)
    cmpm = np.zeros((128, 3, 128), np.float32)
    for slot, sh in enumerate((0, 64, 128)):
        rel = ii[:, None] - sh - 8 * c
        cmpm[:, slot, :] = np.where(16 * rel + 31 <= ii[None, :], 0.0, NEG)
    ovl = np.zeros((128, 4, 128), np.float32)
    for nt in range(4):
        cs_ = 16 * (128 * nt + ii)[:, None]
        ss_ = 64 * np.arange(128)[None, :]
        ovl[:, nt, :] = ((cs_ < ss_ + 64) & (cs_ + 32 > ss_)).astype(np.float32)
    hi = (ii >= 64).astype(np.int64)[:, None]
    dd = (np.arange(240)[None, :] - 112) - 2 * c
    selbase = np.where((dd == hi) | (dd == hi - 1), 100.0, np.where(dd > hi, -100.0, 0.0)).astype(np.float32)
    onehot = np.zeros((128, 8), np.float32)
    onehot[:, c] = 1.0
    return dict(ropecos=ropecos, ropesin=ropesin, cmask=bf(cm.reshape(128, 1024)), constf=constf, constb=constb,
                wm=bf(wm), swamask=bf(sw.reshape(128, 9 * 128)), gsel=gsel,
                wh=bf(wh), winmask=bf(wn.reshape(128, 12 * 128)), cmpmask=bf(cmpm.reshape(128, 3 * 128)), ovl=bf(ovl.reshape(128, 512)),
                selbase=np.ascontiguousarray(selbase), onehot=onehot)


def make_in_maps(inp, nlayers, need_mlp_last=True):
    x = np.asarray(inp["x"], np.float32)[0]
    shared = {}
    cvec = np.asarray(inp["c"], np.float32)[0]
    shared["cT"] = np.ascontiguousarray(cvec.reshape(16, 128).T)
    gv = np.concatenate([np.asarray(inp["norm_mix_g"], np.float32).reshape(4, 16, 128).transpose(2, 0, 1).reshape(128, 64),
                         np.asarray(inp["norm_mlp_g"], np.float32).reshape(4, 16, 128).transpose(2, 0, 1).reshape(128, 64),
                         np.asarray(inp["final_norm_g"], np.float32).reshape(16, 128).T], 1)
    shared["gvec"] = np.ascontiguousarray(gv)
    sinks = np.asarray(inp["even_sinks"], np.float32)
    shared["sinksb"] = np.ascontiguousarray(np.broadcast_to(sinks.reshape(1, 16), (128, 16)))
    shared["fbias"] = np.ascontiguousarray(np.broadcast_to(np.asarray(inp["fox_forget_b"], np.float32).reshape(1, 16), (128, 16)))
    pos = [np.asarray(inp["nsa_k_pos"], np.float32), np.asarray(inp["nsa_v_pos"], np.float32)]
    shared["posT"] = np.ascontiguousarray(np.concatenate([pos[t][j2].T for j2 in range(2) for t in range(2)], 1))
    for j2 in range(2):
        shared[f"w1_{j2}_0"] = np.asarray(inp["nsa_k_w1"][j2], np.float32)
        shared[f"w1_{j2}_1"] = np.asarray(inp["nsa_v_w1"][j2], np.float32)
        shared[f"w2_{j2}"] = np.ascontiguousarray(np.concatenate([np.asarray(inp["nsa_k_w2"][j2], np.float32), np.asarray(inp["nsa_v_w2"][j2], np.float32)], 1))
    for l in range(nlayers):
        j = l // 2
        if l % 2 == 0:
            fm, tm = host_w_layout(np.asarray(inp["even_w_in"][j], np.float32), even_spec())
            shared[f"wout{l}"] = np.asarray(inp["even_w_out"][j], np.float32)
        else:
            w = np.asarray(inp["odd_w_in"][j], np.float32)
            fm, tm = host_w_layout(w, odd_spec())
            shared[f"wsm{l}"] = np.ascontiguousarray(np.concatenate([w[:, 3072:3080], w[:, 5640:5664]], 1))
            shared[f"wout{l}"] = np.asarray(inp["odd_w_out"][j], np.float32)
        shared[f"wfm{l}"] = fm
        shared[f"wtm{l}"] = tm
        if need_mlp_last or l < nlayers - 1:
            shared[f"wup{l}"] = np.asarray(inp["mlp_up"][l], np.float32)
            shared[f"wdown{l}"] = np.asarray(inp["mlp_down"][l], np.float32)
    ada_w = np.asarray(inp["ada_w"], np.float32)
    ada_b = np.asarray(inp["ada_b"], np.float32)
    in_maps = []
    for c in range(NCORE):
        m = dict(shared)
        tok = tok_index(c)
        m["xT"] = np.ascontiguousarray(x[tok].T.reshape(16, 128, TL))
        m.update(host_consts(c))
        m["adaw"] = np.ascontiguousarray(ada_w[:, :, c * 1536:(c + 1) * 1536])
        m["adabT"] = np.ascontiguousarray(ada_b[:, c * 1536:(c + 1) * 1536].reshape(4, 12, 128).transpose(2, 0, 1).reshape(128, 48))
        in_maps.append(m)
    return in_maps


def _run_phase(ph, in_maps, extra):
    b = Builder(DEPTH, False, True, phase=ph)
    nc = b.build()
    names = set(b._ins.keys())
    maps = []
    for c in range(NCORE):
        m = dict(in_maps[c])
        m.update(extra[c])
        missing = names - set(m.keys())
        assert not missing, missing
        maps.append({k: m[k] for k in names})
    res = run_bass_kernel_spmd(nc, maps, core_ids=list(range(NCORE)))
    return res.results


def _assemble(results):
    out = np.zeros((SEQ, D), np.float32)
    for c in range(NCORE):
        oT = np.asarray(results[c]["outT"]).reshape(D, TL)
        out[tok_index(c)] = oT.T
    return out[None]


def kernel_multi(debug_cb=None, **inputs):
    in_maps = make_in_maps(inputs, DEPTH)
    r = _run_phase("M", in_maps, [{} for _ in range(NCORE)])
    modgath = np.concatenate([np.asarray(r[c]["modsend_out"]) for c in range(NCORE)], 0)
    extra = [{"modgath_in": modgath} for _ in range(NCORE)]
    for ph in range(DEPTH + 1):
        r = _run_phase(ph, in_maps, extra)
        if ph == DEPTH:
            return _assemble(r)
        gath = np.concatenate([np.asarray(r[c][f"send{ph}_out"]) for c in range(NCORE)], 0)
        sgath = np.concatenate([np.asarray(r[c][f"ssend{ph}_out"]) for c in range(NCORE)], 0)
        if debug_cb is not None:
            debug_cb(ph, r)
        extra = [{"modgath_in": modgath, "xstate_in": np.asarray(r[c]["xstate_out"]), "qstate_in": np.asarray(r[c]["qstate_out"]),
                  "gstate_in": np.asarray(r[c]["gstate_out"]), "pstate_in": np.asarray(r[c]["pstate_out"]),
                  f"gath{ph}_in": gath, f"sgath{ph}_in": sgath} for c in range(NCORE)]


def kernel_fused(**inputs):
    b = Builder(DEPTH, False, True)
    in_maps = make_in_maps(inputs, DEPTH)
    nc = b.build()
    names = set(b._ins.keys())
    in_maps = [{k: m[k] for k in names} for m in in_maps]
    res = run_bass_kernel_spmd(nc, in_maps, core_ids=list(range(NCORE)))
    return _assemble(res.results)


def kernel(**inputs):
    return kernel_multi(**inputs)
```

```python
import contextlib
import numpy as np
import ml_dtypes
import concourse.bass as bass
import concourse.mybir as mybir
from concourse.bass_utils import run_bass_kernel_spmd

F32 = mybir.dt.float32
BF16 = mybir.dt.bfloat16
AF = mybir.ActivationFunctionType
ALU = mybir.AluOpType
AX = mybir.AxisListType

NCORE = 8
D = 2048
SEQ = 8192
TL = 1024
NCH = 16
DFF = 8192
DEPTH = 4
SCALE = 128.0 ** -0.5
NEG = -30000.0
EPS = 1e-6
ENGS = ("tensor", "vector", "scalar", "gpsimd", "sync")


class Sched:
    def __init__(self, same_engine_sync=True):
        self.ops = []
        self.buf = {}
        self.same_engine_sync = same_engine_sync
        self.ext_keys = set()
        self.ext_ops = []

    def op(self, eng, fn, reads=(), writes=(), kind="c", dsem=None):
        deps = set()
        for k in reads:
            st = self.buf.setdefault(k, [None, []])
            if st[0] is not None:
                deps.add(st[0])
        for k in writes:
            st = self.buf.setdefault(k, [None, []])
            if st[0] is not None:
                deps.add(st[0])
            deps.update(st[1])
        i = len(self.ops)
        self.ops.append(dict(eng=eng, fn=fn, deps=deps, kind=kind, dsem=dsem))
        if kind == "d" and any(k in self.ext_keys for k in writes):
            self.ext_ops.append(i)
        for k in reads:
            self.buf[k][1].append(i)
        for k in writes:
            self.buf[k] = [i, []]
        return i

    def pe(self, fn, reads=(), writes=()):
        return self.op("tensor", fn, reads, writes)

    def dve(self, fn, reads=(), writes=()):
        return self.op("vector", fn, reads, writes)

    def act(self, fn, reads=(), writes=()):
        return self.op("scalar", fn, reads, writes)

    def pool(self, fn, reads=(), writes=()):
        return self.op("gpsimd", fn, reads, writes)

    def dma(self, eng, fn, dsem, reads=(), writes=()):
        return self.op(eng, fn, reads, writes, kind="d", dsem=dsem)

    def cc(self, fn, dsem, reads=(), writes=()):
        return self.op("gpsimd", fn, reads, writes, kind="cc", dsem=dsem)

    def dsem_names(self):
        return sorted({o["dsem"] for o in self.ops if o["dsem"] is not None})

    def emit(self, block, sems, final_wait_ops=()):
        ops = self.ops
        n = len(ops)
        needed = [False] * n
        for o in ops:
            for d in o["deps"]:
                needed[d] = True
        for d in final_wait_ops:
            needed[d] = True
        cnt = {}
        semval = [None] * n
        for i, o in enumerate(ops):
            if o["kind"] == "c":
                if needed[i]:
                    key = "e_" + o["eng"]
                    cnt[key] = cnt.get(key, 0) + 1
                    semval[i] = (key, cnt[key], 1)
            elif o["kind"] == "d":
                key = o["dsem"]
                cnt[key] = cnt.get(key, 0) + 16
                semval[i] = (key, cnt[key], 16)
            else:
                key = o["dsem"]
                cnt[key] = cnt.get(key, 0) + 1
                semval[i] = (key, cnt[key], 1)
        per_eng = {e: [] for e in ENGS}
        for i, o in enumerate(ops):
            per_eng[o["eng"]].append(i)
        self.stats = {e: len(v) for e, v in per_eng.items()}
        self.stats["sems"] = dict(cnt)
        ses = self.same_engine_sync

        def make(engname):
            def body(eng):
                waited = {}
                for i in per_eng[engname]:
                    o = ops[i]
                    need = {}
                    for d in o["deps"]:
                        od = ops[d]
                        if od["eng"] == engname and od["kind"] == "c":
                            if engname == "tensor" or not ses:
                                continue
                        k, v, _ = semval[d]
                        if v > need.get(k, 0):
                            need[k] = v
                    for k, v in need.items():
                        if waited.get(k, 0) < v:
                            eng.wait_ge(sems[k], v)
                            waited[k] = v
                    ins = o["fn"](eng)
                    if semval[i] is not None:
                        k, v, inc = semval[i]
                        ins.then_inc(sems[k], inc)
                if engname == "sync":
                    for d in final_wait_ops:
                        k, v, _ = semval[d]
                        eng.wait_ge(sems[k], v)
            return body

        block.tensor(make("tensor"))
        block.vector(make("vector"))
        block.scalar(make("scalar"))
        block.gpsimd(make("gpsimd"))
        block.sync(make("sync"))


def even_spec():
    fm = []
    for h in range(8):
        fm.append((h * 128, True, ("q", h)))
    for h in range(8):
        fm.append((3072 + h * 128, True, ("q", 8 + h)))
    for h in range(8):
        fm.append((1024 + h * 128, True, ("k", h)))
    for g in range(2):
        fm.append((4096 + g * 128, True, ("k", 8 + g)))
    tm = []
    for h in range(8):
        tm.append((2048 + h * 128, 10 + h))
    for g in range(2):
        tm.append((4352 + g * 128, 18 + g))
    return dict(fm=fm, tm=tm, nitems=20, nk=10, small=None)


def odd_spec():
    fm = []
    for h in range(8):
        fm.append((h * 128, False, ("q", h)))
    for h in range(8):
        fm.append((3080 + h * 128, True, ("q", 8 + h)))
    for h in range(8):
        fm.append((1024 + h * 128, False, ("k", h)))
    for g in range(2):
        fm.append((4616 + g * 128, True, ("k", 8 + g)))
    for g in range(2):
        fm.append((5128 + g * 128, True, ("k", 10 + g)))
    for g in range(2):
        fm.append((4104 + g * 128, True, ("cmp", g)))
    for g in range(2):
        fm.append((4360 + g * 128, False, ("cmp", 2 + g)))
    tm = []
    for h in range(8):
        tm.append((2048 + h * 128, 12 + h))
    for g in range(2):
        tm.append((4872 + g * 128, 20 + g))
    for g in range(2):
        tm.append((5384 + g * 128, 22 + g))
    return dict(fm=fm, tm=tm, nitems=24, nk=12, small=(3072, 5640))


def host_w_layout(w_in, spec):
    fm = np.concatenate([w_in[:, o:o + 128] for (o, _, _) in spec["fm"]], axis=1)
    tm = np.concatenate([w_in[:, o:o + 128] for (o, _) in spec["tm"]], axis=1)
    return np.ascontiguousarray(fm), np.ascontiguousarray(tm)


class H:
    def __init__(self, ap):
        self._ap = ap

    def ap(self):
        return self._ap

    def __getitem__(self, k):
        return self._ap[k]


class Builder:
    def __init__(self, nlayers=DEPTH, stop_after_mix=False, final_norm=True, phase=None):
        self.phase = phase
        self.nlayers = nlayers
        self.stop_after_mix = stop_after_mix
        self.final_norm = final_norm
        self.nc = bass.Bass("TRN2", target_bir_lowering=False)
        self.S = Sched()
        self.psrr = 0
        self.uid = 0
        self.slot_rr = 0
        self.pt_rr = 0
        self.tq_rr = 0
        self._ins = {}
        self.out_ops = []
        self.pending = None

    def dram_in(self, name, shape, dt=F32):
        ap = self.nc.dram_tensor(name, list(shape), dt, kind="ExternalInput").ap()
        self._ins[name] = ap
        return ap

    def sb(self, name, shape, dt):
        return self.es.enter_context(self.nc.sbuf_tensor(name, list(shape), dt))

    def next_ps(self, lo=0, hi=7):
        b = lo + (self.psrr % (hi - lo))
        self.psrr += 1
        return b

    def build(self):
        nc = self.nc
        S = self.S
        NL = self.nlayers
        with contextlib.ExitStack() as es:
            self.es = es
            ph = self.phase
            if ph is None:
                need_A = set(range(NL))
                need_B = set(range(NL))
                mlp_layers = set(l for l in range(NL) if not self.stop_after_mix or l < NL - 1)
            elif ph == "M":
                need_A, need_B, mlp_layers = set(), set(), set()
            else:
                need_A = {ph} if ph < DEPTH else set()
                need_B = {ph - 1} if ph >= 1 else set()
                mlp_layers = set(need_B)
            self.need_A, self.need_B = need_A, need_B
            if ph in (None, 0):
                self.dram_in("xT", [16, 128, TL])
            elif ph != "M":
                self.dram_in("xstate_in", [16, 128, TL])
                self.dram_in("qstate_in", [128, 16 * TL], BF16)
                self.dram_in("gstate_in", [128, 192], BF16)
                self.dram_in("pstate_in", [128, 2])
            if ph is None or ph == DEPTH:
                self._outT = nc.dram_tensor("outT", [16, 128, TL], F32, kind="ExternalOutput").ap()
                S.ext_keys.add("outT")
            elif ph != "M":
                self._xso = nc.dram_tensor("xstate_out", [16, 128, TL], F32, kind="ExternalOutput").ap()
                self._qso = nc.dram_tensor("qstate_out", [128, 16 * TL], BF16, kind="ExternalOutput").ap()
                self._gso = nc.dram_tensor("gstate_out", [128, 192], BF16, kind="ExternalOutput").ap()
                self._pso = nc.dram_tensor("pstate_out", [128, 2], F32, kind="ExternalOutput").ap()
                S.ext_keys.update(["xso", "qso", "gso", "pso"])
            self.dram_in("ropecos", [128, TL])
            self.dram_in("ropesin", [128, TL])
            self.dram_in("cmask", [128, 8 * 128], BF16)
            self.dram_in("constf", [128, 4 * 128])
            self.dram_in("constb", [128, 2 * 128], BF16)
            self.dram_in("cT", [128, 16])
            if ph in (None, "M"):
                self.dram_in("adaw", [DEPTH, D, 1536])
            self.dram_in("adabT", [128, 48])
            self.dram_in("gvec", [128, 9 * 16])
            W = {}
            for l in sorted(need_A | need_B):
                W[l] = {}
                if l in need_A:
                    if l % 2 == 0:
                        W[l].update(fm=self.dram_in(f"wfm{l}", [D, 26 * 128]), tm=self.dram_in(f"wtm{l}", [D, 10 * 128]))
                    else:
                        W[l].update(fm=self.dram_in(f"wfm{l}", [D, 32 * 128]), tm=self.dram_in(f"wtm{l}", [D, 12 * 128]),
                                    sm=self.dram_in(f"wsm{l}", [D, 32]))
                if l in need_B:
                    W[l]["out"] = self.dram_in(f"wout{l}", [D, D])
                    if l in mlp_layers:
                        W[l]["up"] = self.dram_in(f"wup{l}", [D, DFF])
                        W[l]["down"] = self.dram_in(f"wdown{l}", [DFF, D])
            self.W = W
            ev = dict(
                wm=self.dram_in("wm", [32, 4096], BF16),
                swamask=self.dram_in("swamask", [128, 9 * 128], BF16),
                gsel=self.dram_in("gsel", [128, 3 * 256]),
                sinks=self.dram_in("sinksb", [128, 2 * 8]),
            )
            self.ev = ev
            any_odd = any(l % 2 == 1 for l in (need_A | need_B))
            od = dict(
                wh=self.dram_in("wh", [128, 4096], BF16),
                winmask=self.dram_in("winmask", [128, 12 * 128], BF16),
                cmpmask=self.dram_in("cmpmask", [128, 3 * 128], BF16),
                ovl=self.dram_in("ovl", [128, 4 * 128], BF16),
                selbase=self.dram_in("selbase", [128, 240]),
                onehot=self.dram_in("onehot", [128, 8]),
                fbias=self.dram_in("fbias", [128, 16]),
                posT=self.dram_in("posT", [128, 4 * 32]),
                w1=[[self.dram_in(f"w1_{j2}_{t}", [4096, 128]) for t in range(2)] if (2 * j2 + 1) in need_A else None for j2 in range(2)],
                w2=[self.dram_in(f"w2_{j2}", [128, 256]) if (2 * j2 + 1) in need_B else None for j2 in range(2)],
            ) if any_odd else None
            self.od = od
            self.send, self.gath, self.ssend, self.sgath = {}, {}, {}, {}
            if ph is None:
                self.modsend = nc.dram_tensor("modsend", [128, 48], F32)
                self.modgath = nc.dram_tensor("modgath", [NCORE * 128, 48], F32)
            elif ph == "M":
                self.modsend = H(nc.dram_tensor("modsend_out", [128, 48], F32, kind="ExternalOutput").ap())
                S.ext_keys.add("modsend")
            else:
                self.modgath = H(self.dram_in("modgath_in", [NCORE * 128, 48]))
            for l in sorted(need_A | need_B):
                ni = 20 if l % 2 == 0 else 24
                nsm = 64 if l % 2 == 0 else 576
                if ph is None:
                    self.send[l] = nc.dram_tensor(f"send{l}", [ni * 128, TL], BF16)
                    self.gath[l] = nc.dram_tensor(f"gath{l}", [NCORE * ni * 128, TL], BF16)
                    self.ssend[l] = nc.dram_tensor(f"ssend{l}", [128, nsm], F32)
                    self.sgath[l] = nc.dram_tensor(f"sgath{l}", [NCORE * 128, nsm], F32)
                else:
                    if l in need_A:
                        self.send[l] = H(nc.dram_tensor(f"send{l}_out", [ni * 128, TL], BF16, kind="ExternalOutput").ap())
                        self.ssend[l] = H(nc.dram_tensor(f"ssend{l}_out", [128, nsm], F32, kind="ExternalOutput").ap())
                        S.ext_keys.update([("send", l), ("ssend", l)])
                    if l in need_B:
                        self.gath[l] = H(self.dram_in(f"gath{l}_in", [NCORE * ni * 128, TL], BF16))
                        self.sgath[l] = H(self.dram_in(f"sgath{l}_in", [NCORE * 128, nsm]))

            self.xT = self.sb("xT_sb", [128, 16, TL], F32)
            self.A = self.sb("A_sb", [128, 16 * TL], BF16)
            self.QT = self.sb("QT_sb", [128, 16, TL], BF16)
            self.SL = [self.sb(f"slot{i}", [128, 4096], BF16) for i in range(3)]
            self.cosT = self.sb("cosT", [128, TL], F32)
            self.sinT = self.sb("sinT", [128, TL], F32)
            self.cmask = self.sb("cmask_sb", [128, 8, 128], BF16)
            self.cf = self.sb("constf_sb", [128, 4, 128], F32)
            self.cb = self.sb("constb_sb", [128, 2, 128], BF16)
            self.XF = self.sb("XF", [128, 2560], F32)
            self.vec = self.sb("vec", [128, 1024], F32)
            self.tq = [self.sb(f"tq{i}", [128, 512], F32) for i in range(3)]
            self.PT = [self.sb(f"PT{i}", [128, 512], BF16) for i in range(3)]
            self.kst = [self.sb(f"kst{i}", [128, TL], BF16) for i in range(2)]
            self.vst = [self.sb(f"vst{i}", [128, 2, 8, 128], BF16) for i in range(2)]
            self.selb = self.sb("selb", [128, 256], BF16)
            self.gsb = self.sb("gsb", [128, 8, 24], BF16)
            self.w2sb = self.sb("w2sb", [128, 2, 128], BF16)
            self.posT = self.sb("posT_sb", [128, 2, 32], BF16)
            self.PC = self.sb("PC", [128, 4, 512], BF16)
            self.PS = [es.enter_context(nc.psum_tensor(f"ps{i}", [128, 512], F32)) for i in range(7)]
            self.PSB = es.enter_context(nc.psum_tensor("psb", [128, 1024], BF16))
            self.identf = self.cf[:, 0, :]
            self.Rm = self.cf[:, 1, :]
            self.onesf = self.cf[:, 2, :]
            self.tri = self.cf[:, 3, :]
            self.identb = self.cb[:, 0, :]
            self.onesb = self.cb[:, 1, :]
            self.V_COND = 0
            self.V_MOD = 16
            self.V_G = 400
            self.V_ADAB = 544
            self.V_DER = 592
            self.V_SINK = 700
            self.V_POSB = 720

            if ph is None:
                self.load_consts("xT")
                self.compute_mods_local()
                self.mods_gather()
                for l in range(NL):
                    self.layer_A(l)
                    self.exchange(l)
                    self.layer_B(l)
                if self.final_norm:
                    self.final()
                else:
                    self.store_x()
            elif ph == "M":
                self.load_consts(None)
                self.compute_mods_local()
            else:
                self.load_consts("xT" if ph == 0 else "xstate_in")
                self.mods_load()
                if ph >= 1:
                    self.restore_state()
                    self.derive(ph - 1)
                    self.layer_B(ph - 1)
                if ph < DEPTH:
                    self.layer_A(ph)
                    self.save_state()
                else:
                    self.final()

            sems = {}
            for e in ENGS:
                sems["e_" + e] = es.enter_context(nc.semaphore("e_" + e))
            for nme in S.dsem_names():
                sems[nme] = es.enter_context(nc.semaphore(nme))
            block = es.enter_context(nc.Block())
            S.emit(block, sems, final_wait_ops=sorted(set(self.out_ops) | set(S.ext_ops)))
        return nc

    def load_consts(self, xname):
        S = self.S
        if xname is not None:
            xT_d = self.nc_in(xname)
            S.dma("sync", lambda e: e.dma_start(out=self.xT[:], in_=xT_d.rearrange("c p t -> p c t")), "d_x",
                  writes=[("x", c, h) for c in range(16) for h in range(2)])
        for (nm, dst, key) in [("ropecos", self.cosT, "cos"), ("ropesin", self.sinT, "sin"), ("cT", self.vec[:, 0:16], "v_cond"),
                               ("adabT", self.vec[:, self.V_ADAB:self.V_ADAB + 48], "v_adab"),
                               ("gvec", self.vec[:, self.V_G:self.V_G + 144], "v_g")]:
            src = self.nc_in(nm)
            d = dst if nm not in ("ropecos", "ropesin") else dst[:]
            S.dma("sync", (lambda d, src: (lambda e: e.dma_start(out=d, in_=src)))(d, src), "d_c_" + key, writes=[key])
        S.dma("sync", lambda e: e.dma_start(out=self.cmask[:], in_=self.nc_in("cmask").rearrange("p (a b) -> p a b", a=8)), "d_c_cmask", writes=["cmask"])
        S.dma("sync", lambda e: e.dma_start(out=self.cf[:], in_=self.nc_in("constf").rearrange("p (a b) -> p a b", a=4)), "d_c_cf", writes=["cf"])
        S.dma("sync", lambda e: e.dma_start(out=self.cb[:], in_=self.nc_in("constb").rearrange("p (a b) -> p a b", a=2)), "d_c_cb", writes=["cb"])

    def nc_in(self, name):
        return self._ins[name]

    def compute_mods_local(self):
        S = self.S
        vec = self.vec
        cond = vec[:, 0:16]
        S.act(lambda e: e.activation(out=cond, in_=cond, func=AF.Silu), reads=["v_cond"], writes=["v_cond"])
        adaw = self.nc_in("adaw")
        modrow = self.tq[0]
        for l in range(DEPTH):
            banks = [0, 1, 2]
            for kc in range(16):
                for nb in range(3):
                    st = self.tq[nb]
                    src = adaw[l, kc * 128:(kc + 1) * 128, nb * 512:(nb + 1) * 512]
                    S.dma("sync", (lambda st, src: (lambda e: e.dma_start(out=st[:], in_=src)))(st, src), f"d_tq{nb}",
                          writes=[("tq", nb)])
                    S.pe((lambda st, kc, nb: (lambda e: e.matmul(self.PS[nb][0:1, :], lhsT=cond[:, kc:kc + 1], rhs=st[:],
                                                                  start=(kc == 0), stop=(kc == 15))))(st, kc, nb),
                         reads=[("tq", nb), "v_cond"], writes=[("ps", nb)])
            for nb in range(3):
                S.act((lambda nb: (lambda e: e.copy(out=self.XF[0:1, nb * 512:(nb + 1) * 512], in_=self.PS[nb][0:1, :])))(nb),
                      reads=[("ps", nb)], writes=[("xfrow", nb)])
            for k in range(12):
                nb = k // 4
                S.pe((lambda k: (lambda e: e.matmul(self.PS[3][:, k:k + 1], lhsT=self.XF[0:1, k * 128:(k + 1) * 128],
                                                    rhs=self.onesf[0:1, 0:1], start=True, stop=True)))(k),
                     reads=[("xfrow", nb), "cf"], writes=[("ps", 3)])
            S.dve((lambda l: (lambda e: e.tensor_tensor(out=self.XF[:, 2432 + l * 12:2432 + (l + 1) * 12], in0=self.PS[3][:, 0:12],
                                                        in1=vec[:, self.V_ADAB + l * 12:self.V_ADAB + (l + 1) * 12], op=ALU.add)))(l),
                  reads=[("ps", 3), "v_adab"], writes=["modloc"])
        S.dma("sync", lambda e: e.dma_start(out=self.modsend[:, :], in_=self.XF[:, 2432:2480]), "d_modsend", reads=["modloc"], writes=["modsend"])

    def mods_gather(self):
        S = self.S
        S.cc(lambda e: e.collective_compute("AllGather", ALU.bypass, replica_groups=[list(range(NCORE))],
                                            ins=[self.modsend.ap().opt()], outs=[self.modgath.ap().opt()]),
             "cc_mod", reads=["modsend"], writes=["modgath"])
        self.mods_load()

    def mods_load(self):
        S = self.S
        vec = self.vec
        S.dma("sync", lambda e: e.dma_start(out=vec[:, self.V_MOD:self.V_MOD + 384].rearrange("p (r f) -> p r f", r=8),
                                            in_=self.modgath.ap().rearrange("(r p) f -> p r f", p=128)),
              "d_modload", reads=["modgath"], writes=["v_mod"])

    def modcol(self, l, part, ch):
        gch = part * 16 + ch
        r, k = gch // 12, gch % 12
        o = self.V_MOD + r * 48 + l * 12 + k
        return self.vec[:, o:o + 1]

    def derive(self, l):
        S = self.S
        vec = self.vec
        for which, part, goff in ((0, 1, l * 16), (1, 4, 64 + l * 16)):
            for ch in range(16):
                dst = vec[:, self.V_DER + which * 16 + ch:self.V_DER + which * 16 + ch + 1]
                S.dve((lambda dst, part, ch, goff: (lambda e: e.scalar_tensor_tensor(
                    out=dst, in0=self.modcol(l, part, ch), scalar=1.0, in1=vec[:, self.V_G + goff + ch:self.V_G + goff + ch + 1],
                    op0=ALU.add, op1=ALU.mult)))(dst, part, ch, goff),
                    reads=["v_mod", "v_g"], writes=[("v_der", which)])

    def norm_to_A(self, gmul_of, gadd_of, derkey):
        S = self.S
        A3 = self.A[:].rearrange("p (c t) -> p c t", c=16)
        for half in range(2):
            cs = slice(half * 512, (half + 1) * 512)
            bank = 4 + half
            for ch in range(16):
                sq = self.tq[ch % 2]
                S.act((lambda sq, ch, cs: (lambda e: e.activation(out=sq[:], in_=self.xT[:, ch, cs], func=AF.Square)))(sq, ch, cs),
                      reads=[("x", ch, half)], writes=[("tq", ch % 2)])
                S.pe((lambda sq, ch, bank: (lambda e: e.matmul(self.PS[bank][:], lhsT=self.onesf, rhs=sq[:], start=(ch == 0), stop=(ch == 15))))(sq, ch, bank),
                     reads=[("tq", ch % 2), "cf"], writes=[("ps", bank)])
            rs = self.XF[:, 1024 + half * 512:1024 + (half + 1) * 512]
            S.act((lambda bank, rs: (lambda e: e.activation(out=rs, in_=self.PS[bank][:], func=AF.Sqrt, scale=1.0 / D, bias=EPS)))(bank, rs),
                  reads=[("ps", bank)], writes=[("rstd", half)])
            S.dve((lambda rs: (lambda e: e.reciprocal(out=rs, in_=rs)))(rs), reads=[("rstd", half)], writes=[("rstd", half)])
            for ch in range(16):
                t = self.tq[2]
                S.dve((lambda t, ch, cs, rs: (lambda e: e.scalar_tensor_tensor(out=t[:], in0=self.xT[:, ch, cs], scalar=gmul_of(ch), in1=rs,
                                                                                op0=ALU.mult, op1=ALU.mult)))(t, ch, cs, rs),
                      reads=[("x", ch, half), ("rstd", half), derkey, "v_mod"], writes=[("tq", 2)])
                S.act((lambda t, ch, cs: (lambda e: e.activation(out=A3[:, ch, cs], in_=t[:], func=AF.Identity, bias=gadd_of(ch), scale=1.0)))(t, ch, cs),
                      reads=[("tq", 2), "v_mod"], writes=[("A", ch, half)])

    def wslot_load(self, wd, row0, col0, ncols, key):
        S = self.S
        s = self.slot_rr % 3
        self.slot_rr += 1
        slot = self.SL[s]
        src = wd[row0:row0 + 2048, col0:col0 + ncols].rearrange("(k p) n -> p k n", p=128)
        dst = slot[:, 0:16 * ncols].rearrange("p (k n) -> p k n", k=16)
        S.dma("gpsimd", lambda e: e.dma_start(out=dst, in_=src), f"d_slot{s}", writes=[("slot", s), ("slotv", s), ("slotw", s), ("slotx", s)])
        return s, dst

    def projections(self, l, spec):
        S = self.S
        W = self.W[l]
        A3 = self.A[:].rearrange("p (c t) -> p c t", c=16)
        nfm = len(spec["fm"])
        send = self.send[l]
        self.send_ops = []
        for g in range(nfm // 2):
            s, wv = self.wslot_load(W["fm"], 0, g * 256, 256, None)
            for ci in range(2):
                (_, rope, dest) = spec["fm"][g * 2 + ci]
                if dest[0] == "q":
                    dst_full = self.QT[:, dest[1], :]
                    dkeys = [("Q", dest[1], 0), ("Q", dest[1], 1)]
                elif dest[0] == "k":
                    kb = dest[1] % 2
                    dst_full = self.kst[kb][:]
                    dkeys = [("kst", kb), ("kst", kb)]
                else:
                    kb = dest[1] % 2
                    dst_full = self.kst[kb][:]
                    dkeys = [("kst", kb), ("kst", kb)]
                for half in range(2):
                    cs = slice(half * 512, (half + 1) * 512)
                    b = self.next_ps(0, 4)
                    for kc in range(16):
                        S.pe((lambda b, wv, kc, ci, cs: (lambda e: e.matmul(self.PS[b][:], lhsT=wv[:, kc, ci * 128:(ci + 1) * 128], rhs=A3[:, kc, cs],
                                                                            start=(kc == 0), stop=(kc == 15))))(b, wv, kc, ci, cs),
                             reads=[("slot", s), ("A", kc, half)], writes=[("ps", b)])
                    dst = dst_full[:, cs]
                    if not rope:
                        S.act((lambda b, dst: (lambda e: e.copy(out=dst, in_=self.PS[b][:])))(b, dst), reads=[("ps", b)], writes=[dkeys[half]])
                    else:
                        q32, t1, t2 = self.tq
                        rb = 4 + (self.uid % 2)
                        self.uid += 1
                        S.act((lambda b: (lambda e: e.copy(out=q32[:], in_=self.PS[b][:])))(b), reads=[("ps", b)], writes=[("tq", 0)])
                        S.pe((lambda rb: (lambda e: e.matmul(self.PS[rb][:], lhsT=self.Rm, rhs=q32[:], start=True, stop=True)))(rb),
                             reads=[("tq", 0), "cf"], writes=[("ps", rb)])
                        S.dve((lambda cs: (lambda e: e.tensor_tensor(out=t1[:], in0=q32[:], in1=self.cosT[:, cs], op=ALU.mult)))(cs),
                              reads=[("tq", 0), "cos"], writes=[("tq", 1)])
                        S.dve((lambda rb, cs: (lambda e: e.tensor_tensor(out=t2[:], in0=self.PS[rb][:], in1=self.sinT[:, cs], op=ALU.mult)))(rb, cs),
                              reads=[("ps", rb), "sin"], writes=[("tq", 2)])
                        S.dve((lambda dst: (lambda e: e.tensor_tensor(out=dst, in0=t1[:], in1=t2[:], op=ALU.add)))(dst),
                              reads=[("tq", 1), ("tq", 2)], writes=[dkeys[half]])
                if dest[0] == "k":
                    item = dest[1]
                    kb = item % 2
                    if l % 2 == 0 and item < 8:
                        S.dve((lambda kb, item: (lambda e: e.tensor_reduce(out=self.XF[:, 2368 + item * 8:2368 + item * 8 + 8],
                                                                            in_=self.kst[kb][:].rearrange("p (j t) -> p j t", j=8), axis=AX.X, op=ALU.add)))(kb, item),
                              reads=[("kst", kb)], writes=["ksumT"])
                    o = S.dma("sync", (lambda kb, item: (lambda e: e.dma_start(out=send[item * 128:(item + 1) * 128, :], in_=self.kst[kb][:])))(kb, item),
                              f"d_kst{kb}", reads=[("kst", kb)], writes=[("send", l)])
                elif dest[0] == "cmp":
                    self.cmp_partials(l, dest[1])
        if l % 2 == 1:
            self.odd_small(l)
        ntm = len(spec["tm"])
        for g in range(ntm // 2):
            s, wv = self.wslot_load(W["tm"], 0, g * 256, 256, None)
            vb = g % 2
            for j in range(8):
                b = self.next_ps(0, 4)
                for kc in range(16):
                    S.pe((lambda b, wv, kc, j: (lambda e: e.matmul(self.PS[b][:, 0:256], lhsT=A3[:, kc, j * 128:(j + 1) * 128], rhs=wv[:, kc, :],
                                                                   start=(kc == 0), stop=(kc == 15))))(b, wv, kc, j),
                         reads=[("slot", s), ("A", kc, j // 4)], writes=[("ps", b)])
                S.act((lambda b, vb, j: (lambda e: e.copy(out=self.vst[vb][:, :, j, :], in_=self.PS[b][:, 0:256].rearrange("p (h d) -> p h d", h=2))))(b, vb, j),
                      reads=[("ps", b)], writes=[("vst", vb)])
            for ci in range(2):
                item = spec["tm"][g * 2 + ci][1]
                S.dma("sync", (lambda vb, ci, item: (lambda e: e.dma_start(out=send[item * 128:(item + 1) * 128, :],
                                                                           in_=self.vst[vb][:, ci, :, :].rearrange("p j d -> p (j d)"))))(vb, ci, item),
                      f"d_vst{vb}", reads=[("vst", vb)], writes=[("send", l)])

    def exchange(self, l):
        S = self.S
        send, gath = self.send[l], self.gath[l]
        S.cc(lambda e: e.collective_compute("AllGather", ALU.bypass, replica_groups=[list(range(NCORE))],
                                            ins=[send.ap().opt()], outs=[gath.ap().opt()]),
             f"cc_big{l}", reads=[("send", l)], writes=[("gath", l)])
        ss, sg = self.ssend[l], self.sgath[l]
        S.cc(lambda e: e.collective_compute("AllGather", ALU.bypass, replica_groups=[list(range(NCORE))],
                                            ins=[ss.ap().opt()], outs=[sg.ap().opt()]),
             f"cc_small{l}", reads=[("ssend", l)], writes=[("sgath", l)])

    def kv_load(self, l, kitem, vitem, cp, ni):
        S = self.S
        s = self.slot_rr % 3
        self.slot_rr += 1
        slot = self.SL[s]
        g3 = self.gath[l].ap().rearrange("(c r) t -> r c t", c=NCORE)
        ksrc = g3[kitem * 128:(kitem + 1) * 128, 2 * cp:2 * cp + 2, :]
        vsrc = g3[vitem * 128:(vitem + 1) * 128, 2 * cp:2 * cp + 2, :]
        kd = slot[:, 0:2048].rearrange("p (c t) -> p c t", c=2)
        vd = slot[:, 2048:4096].rearrange("p (c t) -> p c t", c=2)
        S.dma("sync", lambda e: e.dma_start(out=kd, in_=ksrc), f"d_slot{s}", reads=[("gath", l)], writes=[("slot", s), ("slotw", s)])
        S.dma("sync", lambda e: e.dma_start(out=vd, in_=vsrc), f"d_slotv{s}", reads=[("gath", l)], writes=[("slotv", s), ("slotx", s)])
        return s, kd, slot[:, 2048:4096].rearrange("p (c j d) -> p c j d", c=2, j=8)

    def attn_unit(self, sbank, kT, vT, q_ap, n, extra, exp_bias, obank, dbank, ocols, first, last, skeys, pkeys):
        S = self.S
        pt = self.PT[self.pt_rr % 3]
        ptk = ("PT", self.pt_rr % 3)
        self.pt_rr += 1
        nex = len(extra)
        S.pe(lambda e: e.matmul(self.PS[sbank][:, 0:n], lhsT=kT, rhs=q_ap, start=True, stop=(nex == 0)),
             reads=skeys, writes=[("ps", sbank)])
        for xi, (lhs, rhs, c0, c1, xkeys) in enumerate(extra):
            S.pe((lambda lhs, rhs, c0, c1, xi: (lambda e: e.matmul(self.PS[sbank][:, c0:c1], lhsT=lhs, rhs=rhs, start=False, stop=(xi == nex - 1))))(lhs, rhs, c0, c1, xi),
                 reads=xkeys, writes=[("ps", sbank)])
        pkl = [ptk] + [ptk + (c,) for c in (0, 128, 256, 384)]
        if exp_bias is None:
            S.act(lambda e: e.activation(out=pt[:, 0:n], in_=self.PS[sbank][:, 0:n], func=AF.Exp, scale=SCALE),
                  reads=[("ps", sbank)], writes=[ptk] + [ptk + (c,) for c in (0, 128, 256, 384)])
        else:
            pkl = []
            for (c0, c1, bap, bkeys) in exp_bias:
                S.act((lambda c0, c1, bap: (lambda e: e.activation(out=pt[:, c0:c1], in_=self.PS[sbank][:, c0:c1], func=AF.Exp, scale=SCALE, bias=bap)))(c0, c1, bap),
                      reads=[("ps", sbank)] + bkeys, writes=[ptk + (c0,)])
                pkl.append(ptk + (c0,))
        def pv():
            S.pe(lambda e: e.matmul(self.PS[obank][:, ocols], lhsT=vT, rhs=pt[:, 0:n], start=first, stop=last, skip_group_check=True),
                 reads=pkl + pkeys, writes=[("ps", obank)])
            S.pe(lambda e: e.matmul(self.PS[dbank][:, ocols], lhsT=self.onesb, rhs=pt[:, 0:n], start=first, stop=last, skip_group_check=True),
                 reads=pkl + ["cb"], writes=[("ps", dbank)])
        prev = self.pending
        self.pending = pv
        if prev is not None:
            prev()

    def attn_flush(self):
        if self.pending is not None:
            p = self.pending
            self.pending = None
            p()

    def finalize_head(self, obank, dbank, dst, dkeys, ncols=512, sink_cols=None):
        S = self.S
        self.attn_flush()
        rd = self.tq[self.tq_rr % 3]
        rk = ("tq", self.tq_rr % 3)
        self.tq_rr += 1
        if sink_cols is None:
            S.dve(lambda e: e.tensor_scalar(out=rd[:, 0:ncols], in0=self.PS[dbank][:, 0:ncols], scalar1=1e-30, scalar2=None, op0=ALU.max),
                  reads=[("ps", dbank)], writes=[rk])
        else:
            for hh, sc in enumerate(sink_cols):
                S.dve((lambda hh, sc: (lambda e: e.tensor_scalar(out=rd[:, hh * 128:(hh + 1) * 128], in0=self.PS[dbank][:, hh * 128:(hh + 1) * 128],
                                                                  scalar1=sc, scalar2=None, op0=ALU.add)))(hh, sc),
                      reads=[("ps", dbank), "v_sink"], writes=[rk])
        S.dve(lambda e: e.reciprocal(out=rd[:, 0:ncols], in_=rd[:, 0:ncols]), reads=[rk], writes=[rk])
        S.dve(lambda e: e.tensor_tensor(out=dst, in0=self.PS[obank][:, 0:ncols] if len(dst.shape) == 2 else self.PS[obank][:, 0:ncols].rearrange("p (h t) -> p h t", h=4),
                                        in1=rd[:, 0:ncols] if len(dst.shape) == 2 else rd[:, 0:ncols].rearrange("p (h t) -> p h t", h=4), op=ALU.mult),
              reads=[("ps", obank), rk], writes=dkeys)

    def even_attention(self, l):
        S = self.S
        A = self.A
        XF = self.XF
        j2 = l // 2
        ev = self.ev
        wm = A[0:32, 0:4096]
        swam = A[:, 4096:4096 + 1152].rearrange("p (s q) -> p s q", s=9)
        selbT = [A[0:32, 5248:6272], A[0:32, 6272:7296]]
        kmT = A[:, 7296:7552]
        akeys = [("A", c, h) for c in range(16) for h in range(2)]
        S.dma("sync", lambda e: e.dma_start(out=wm, in_=ev["wm"]), "d_ac0", writes=akeys + ["wm"])
        S.dma("sync", lambda e: e.dma_start(out=A[:, 4096:4096 + 1152], in_=ev["swamask"]), "d_ac1", writes=["swam"])
        S.dma("sync", lambda e: e.dma_start(out=XF[:, 0:768], in_=ev["gsel"]), "d_ac2", writes=["gsel"])
        S.dma("sync", lambda e: e.dma_start(out=self.vec[:, self.V_SINK:self.V_SINK + 8], in_=ev["sinks"][:, j2 * 8:(j2 + 1) * 8]), "d_ac3", writes=["v_sink"])
        S.act(lambda e: e.activation(out=self.vec[:, self.V_SINK:self.V_SINK + 8], in_=self.vec[:, self.V_SINK:self.V_SINK + 8], func=AF.Exp),
              reads=["v_sink"], writes=["v_sink"])
        kg = self.tq[0][:]
        S.dma("sync", lambda e: e.dma_start(out=kg.rearrange("p (c f) -> p c f", c=8), in_=self.sgath[l].ap().rearrange("(c p) f -> p c f", p=128)),
              "d_ac4", reads=[("sgath", l)], writes=[("tq", 0)])
        kg5 = kg.rearrange("p (c2 par h j) -> p c2 par h j", c2=4, par=2, h=8)
        S.dve(lambda e: e.tensor_tensor(out=kmT.rearrange("p (h j c2) -> p h j c2", h=8, j=8),
                                        in0=kg5[:, :, 0, :, :].transpose([0, 2, 3, 1]), in1=kg5[:, :, 1, :, :].transpose([0, 2, 3, 1]), op=ALU.add),
              reads=[("tq", 0)], writes=["kmT"])
        gbias, past01, own01 = XF[:, 0:256], XF[:, 256:512], XF[:, 512:768]
        gm, sel, m8 = XF[:, 768:1024], XF[:, 2048:2304], XF[:, 2304:2368]
        ni = 20
        for h in range(8):
            sb_ = selbT[h % 2]
            sbk = ("selbT", h % 2)
            for j in range(8):
                S.pe((lambda h, j: (lambda e: e.matmul(self.PS[6][:, j * 32:(j + 1) * 32], lhsT=self.QT[:, h, j * 128:(j + 1) * 128],
                                                       rhs=kmT[:, h * 32:(h + 1) * 32], start=True, stop=True)))(h, j),
                     reads=[("Q", h, j // 4), "kmT"], writes=[("ps", 6)])
            S.dve(lambda e: e.tensor_tensor(out=gm, in0=self.PS[6][:, 0:256], in1=gbias, op=ALU.add), reads=[("ps", 6), "gsel"], writes=["gm"])
            for j in range(8):
                S.dve((lambda j: (lambda e: e.max(out=m8[:, j * 8:(j + 1) * 8], in_=gm[:, j * 32:(j + 1) * 32])))(j), reads=["gm"], writes=["m8"])
                S.dve((lambda j: (lambda e: e.tensor_scalar(out=sel[:, j * 32:(j + 1) * 32], in0=gm[:, j * 32:(j + 1) * 32],
                                                            scalar1=m8[:, j * 8 + 2:j * 8 + 3], scalar2=None, op0=ALU.is_ge)))(j),
                      reads=["gm", "m8"], writes=["sel"])
            S.dve(lambda e: e.tensor_tensor(out=sel, in0=sel, in1=past01, op=ALU.mult), reads=["sel", "gsel"], writes=["sel"])
            S.dve(lambda e: e.tensor_tensor(out=sel, in0=sel, in1=own01, op=ALU.add), reads=["sel", "gsel"], writes=["sel"])
            S.dve(lambda e: e.tensor_scalar(out=self.selb[:], in0=sel, scalar1=-1.0, scalar2=-NEG, op0=ALU.add, op1=ALU.mult), reads=["sel"], writes=["selb"])
            for j in range(8):
                S.pe((lambda j: (lambda e: e.transpose(self.PSB[0:32, j * 128:(j + 1) * 128], self.selb[:, j * 32:(j + 1) * 32], self.identb)))(j),
                     reads=["selb", "cb"], writes=["psb"])
            S.act((lambda sb_: (lambda e: e.copy(out=sb_, in_=self.PSB[0:32, :])))(sb_), reads=["psb"], writes=[sbk])
            first = True
            for cp in range(4):
                s, kd, vd = self.kv_load(l, h, 10 + h, cp, ni)
                for cc in range(2):
                    ck = 2 * cp + cc
                    for jk in range(8):
                        b = (8 * jk + ck) // 2
                        groups = []
                        if jk < 4:
                            groups.append((0, jk * 128, 512))
                        groups.append((1, max(512, jk * 128), 1024))
                        for (grp, c0, c1) in groups:
                            n = c1 - c0
                            sbank = self.next_ps(0, 3)
                            extra = []
                            if c0 == jk * 128:
                                extra.append((self.identb, self.cmask[:, ck, :], 0, 128, ["cb", "cmask"]))
                            extra.append((wm[:, b * 128:(b + 1) * 128], sb_[:, c0:c1], 0, n, ["wm", sbk]))
                            last = (ck == 7 and jk == (3 if grp == 0 else 7))
                            fst = (ck == 0 and jk == 0)
                            hq = [("Q", h, grp)]
                            self.attn_unit(sbank, kd[:, cc, jk * 128:(jk + 1) * 128], vd[:, cc, jk, :], self.QT[:, h, c0:c1], n, extra, None,
                                           3 + grp, 5 + grp if False else (5 if grp == 0 else 6), slice(c0 - grp * 512, c1 - grp * 512), fst, last,
                                           [("slot", s)] + hq, [("slotv", s)])
            for grp in range(2):
                self.finalize_head(3 + grp, 5 if grp == 0 else 6, self.QT[:, h, grp * 512:(grp + 1) * 512], [("Q", h, grp)])
        g3 = self.gath[l].ap().rearrange("(c r) t -> r c t", c=NCORE)
        for g in range(2):
            kitem, vitem = 8 + g, 18 + g
            for j in range(8):
                s = self.slot_rr % 3
                self.slot_rr += 1
                slot = self.SL[s]
                kw = slot[:, 0:1152].rearrange("p (s t) -> p s t", s=9)
                vw = slot[:, 2048:2048 + 1152].rearrange("p (s t) -> p s t", s=9)
                rows_k = slice(kitem * 128, (kitem + 1) * 128)
                rows_v = slice(vitem * 128, (vitem + 1) * 128)
                S.dma("sync", (lambda kw, j, rows_k: (lambda e: e.dma_start(out=kw[:, 1:9, :], in_=g3[rows_k, :, j * 128:(j + 1) * 128])))(kw, j, rows_k),
                      f"d_slot{s}", reads=[("gath", l)], writes=[("slot", s)])
                S.dma("sync", (lambda vw, j, rows_v: (lambda e: e.dma_start(out=vw[:, 1:9, :], in_=g3[rows_v, :, j * 128:(j + 1) * 128])))(vw, j, rows_v),
                      f"d_slotv{s}", reads=[("gath", l)], writes=[("slotv", s)])
                if j > 0:
                    S.dma("sync", (lambda kw, j, rows_k: (lambda e: e.dma_start(out=kw[:, 0, :], in_=g3[rows_k, 7, (j - 1) * 128:j * 128])))(kw, j, rows_k),
                          f"d_slotw{s}", reads=[("gath", l)], writes=[("slotw", s)])
                    S.dma("sync", (lambda vw, j, rows_v: (lambda e: e.dma_start(out=vw[:, 0, :], in_=g3[rows_v, 7, (j - 1) * 128:j * 128])))(vw, j, rows_v),
                          f"d_slotx{s}", reads=[("gath", l)], writes=[("slotx", s)])
                cands = list(range(0 if j > 0 else 1, 9))
                qap = self.QT[:, 8 + 4 * g:12 + 4 * g, j * 128:(j + 1) * 128]
                qkeys = [("Q", 8 + 4 * g + hh, j // 4) for hh in range(4)]
                ob, db = 3 + (j % 2), 5 + (j % 2)
                for si, sidx in enumerate(cands):
                    sbank = self.next_ps(0, 3)
                    extra = [(self.identb, swam[:, sidx, :], hh * 128, (hh + 1) * 128, ["cb", "swam"]) for hh in range(4)]
                    kk = [("slot", s), ("slotw", s)] + qkeys
                    self.attn_unit(sbank, kw[:, sidx, :], vw[:, sidx, :], qap, 512, extra, None, ob, db, slice(0, 512),
                                   si == 0, si == len(cands) - 1, kk, [("slotv", s), ("slotx", s)])
                sinks = [self.vec[:, self.V_SINK + 4 * g + hh:self.V_SINK + 4 * g + hh + 1] for hh in range(4)]
                self.finalize_head(ob, db, qap, qkeys, 512, sink_cols=sinks)

    def out_proj(self, l):
        S = self.S
        W = self.W[l]
        for g in range(8):
            s, wv = self.wslot_load(W["out"], 0, g * 256, 256, None)
            for ci in range(2):
                fo = g * 2 + ci
                for half in range(2):
                    cs = slice(half * 512, (half + 1) * 512)
                    b = self.next_ps(0, 4)
                    for hc in range(16):
                        S.pe((lambda b, wv, hc, ci, cs: (lambda e: e.matmul(self.PS[b][:], lhsT=wv[:, hc, ci * 128:(ci + 1) * 128], rhs=self.QT[:, hc, cs],
                                                                            start=(hc == 0), stop=(hc == 15))))(b, wv, hc, ci, cs),
                             reads=[("slot", s), ("Q", hc, half)], writes=[("ps", b)])
                    S.dve((lambda b, fo, cs: (lambda e: e.scalar_tensor_tensor(out=self.xT[:, fo, cs], in0=self.PS[b][:], scalar=self.modcol(l, 2, fo),
                                                                                in1=self.xT[:, fo, cs], op0=ALU.mult, op1=ALU.add)))(b, fo, cs),
                          reads=[("ps", b), ("x", fo, half), "v_mod"], writes=[("x", fo, half)])

    def mlp(self, l):
        S = self.S
        W = self.W[l]
        A3 = self.A[:].rearrange("p (c t) -> p c t", c=16)
        for qd in range(4):
            for g in range(8):
                s, wv = self.wslot_load(W["up"], 0, qd * 2048 + g * 256, 256, None)
                for ci in range(2):
                    fc = g * 2 + ci
                    for half in range(2):
                        cs = slice(half * 512, (half + 1) * 512)
                        b = self.next_ps(0, 4)
                        for kc in range(16):
                            S.pe((lambda b, wv, kc, ci, cs: (lambda e: e.matmul(self.PS[b][:], lhsT=wv[:, kc, ci * 128:(ci + 1) * 128], rhs=A3[:, kc, cs],
                                                                                start=(kc == 0), stop=(kc == 15))))(b, wv, kc, ci, cs),
                                 reads=[("slot", s), ("A", kc, half)], writes=[("ps", b)])
                        t = self.tq[self.tq_rr % 3]
                        tk = ("tq", self.tq_rr % 3)
                        self.tq_rr += 1
                        S.act((lambda b, t: (lambda e: e.activation(out=t[:], in_=self.PS[b][:], func=AF.Relu)))(b, t), reads=[("ps", b)], writes=[tk])
                        S.dve((lambda t, fc, cs: (lambda e: e.tensor_tensor(out=self.QT[:, fc, cs], in0=t[:], in1=t[:], op=ALU.mult)))(t, fc, cs),
                              reads=[tk], writes=[("Q", fc, half)])
            for g in range(8):
                s, wv = self.wslot_load(W["down"], qd * 2048, g * 256, 256, None)
                for ci in range(2):
                    fo = g * 2 + ci
                    for half in range(2):
                        cs = slice(half * 512, (half + 1) * 512)
                        b = self.next_ps(0, 4)
                        for fc in range(16):
                            S.pe((lambda b, wv, fc, ci, cs: (lambda e: e.matmul(self.PS[b][:], lhsT=wv[:, fc, ci * 128:(ci + 1) * 128], rhs=self.QT[:, fc, cs],
                                                                                start=(fc == 0), stop=(fc == 15))))(b, wv, fc, ci, cs),
                                 reads=[("slot", s), ("Q", fc, half)], writes=[("ps", b)])
                        S.dve((lambda b, fo, cs: (lambda e: e.scalar_tensor_tensor(out=self.xT[:, fo, cs], in0=self.PS[b][:], scalar=self.modcol(l, 5, fo),
                                                                                    in1=self.xT[:, fo, cs], op0=ALU.mult, op1=ALU.add)))(b, fo, cs),
                              reads=[("ps", b), ("x", fo, half), "v_mod"], writes=[("x", fo, half)])

    def layer_A(self, l):
        S = self.S
        vec = self.vec
        self.derive(l)
        self.norm_to_A(lambda ch: vec[:, self.V_DER + ch:self.V_DER + ch + 1], lambda ch: self.modcol(l, 0, ch), ("v_der", 0))
        spec = even_spec() if l % 2 == 0 else odd_spec()
        self.projections(l, spec)
        if l % 2 == 0:
            S.dma("sync", lambda e: e.dma_start(out=self.ssend[l][:, :], in_=self.XF[:, 2368:2432]), f"d_ss{l}", reads=["ksumT"], writes=[("ssend", l)])

    def layer_B(self, l):
        vec = self.vec
        if l % 2 == 0:
            self.even_attention(l)
        else:
            self.odd_attention(l)
        self.out_proj(l)
        if self.stop_after_mix and l == self.nlayers - 1:
            return
        self.norm_to_A(lambda ch: vec[:, self.V_DER + 16 + ch:self.V_DER + 16 + ch + 1], lambda ch: self.modcol(l, 3, ch), ("v_der", 1))
        self.mlp(l)

    def save_state(self):
        S = self.S
        allx = [("x", c, h) for c in range(16) for h in range(2)]
        allq = [("Q", c, h) for c in range(16) for h in range(2)]
        S.dma("sync", lambda e: e.dma_start(out=self._xso.rearrange("c p t -> p c t"), in_=self.xT[:]), "d_so0", reads=allx, writes=["xso"])
        S.dma("sync", lambda e: e.dma_start(out=self._qso, in_=self.QT[:].rearrange("p c t -> p (c t)")), "d_so1", reads=allq, writes=["qso"])
        S.dma("sync", lambda e: e.dma_start(out=self._gso, in_=self.gsb[:].rearrange("p j n -> p (j n)")), "d_so2", reads=["gsb"], writes=["gso"])
        S.dma("sync", lambda e: e.dma_start(out=self._pso, in_=self.vec[:, self.V_POSB:self.V_POSB + 2]), "d_so3", reads=["v_posb"], writes=["pso"])

    def restore_state(self):
        S = self.S
        allq = [("Q", c, h) for c in range(16) for h in range(2)]
        S.dma("sync", lambda e: e.dma_start(out=self.QT[:].rearrange("p c t -> p (c t)"), in_=self.nc_in("qstate_in")), "d_si1", writes=allq)
        S.dma("sync", lambda e: e.dma_start(out=self.gsb[:].rearrange("p j n -> p (j n)"), in_=self.nc_in("gstate_in")), "d_si2", writes=["gsb"])
        S.dma("sync", lambda e: e.dma_start(out=self.vec[:, self.V_POSB:self.V_POSB + 2], in_=self.nc_in("pstate_in")), "d_si3", writes=["v_posb"])

    def store_x(self):
        S = self.S
        outT = self._outT
        self.out_ops = [S.dma("sync", lambda e: e.dma_start(out=outT.rearrange("c p t -> p c t"), in_=self.xT[:]), "d_out",
                              reads=[("x", c, h) for c in range(16) for h in range(2)], writes=["outT"])]

    def final(self):
        S = self.S
        vec = self.vec
        A3 = self.A[:].rearrange("p (c t) -> p c t", c=16)
        for half in range(2):
            cs = slice(half * 512, (half + 1) * 512)
            bank = 4 + half
            for ch in range(16):
                sq = self.tq[ch % 2]
                S.act((lambda sq, ch, cs: (lambda e: e.activation(out=sq[:], in_=self.xT[:, ch, cs], func=AF.Square)))(sq, ch, cs),
                      reads=[("x", ch, half)], writes=[("tq", ch % 2)])
                S.pe((lambda sq, ch, bank: (lambda e: e.matmul(self.PS[bank][:], lhsT=self.onesf, rhs=sq[:], start=(ch == 0), stop=(ch == 15))))(sq, ch, bank),
                     reads=[("tq", ch % 2), "cf"], writes=[("ps", bank)])
            rs = self.XF[:, 1024 + half * 512:1024 + (half + 1) * 512]
            S.act((lambda bank, rs: (lambda e: e.activation(out=rs, in_=self.PS[bank][:], func=AF.Sqrt, scale=1.0 / D, bias=EPS)))(bank, rs),
                  reads=[("ps", bank)], writes=[("rstd", half)])
            S.dve((lambda rs: (lambda e: e.reciprocal(out=rs, in_=rs)))(rs), reads=[("rstd", half)], writes=[("rstd", half)])
            for ch in range(16):
                S.dve((lambda ch, cs, rs: (lambda e: e.scalar_tensor_tensor(out=self.xT[:, ch, cs], in0=self.xT[:, ch, cs],
                                                                            scalar=vec[:, self.V_G + 128 + ch:self.V_G + 128 + ch + 1], in1=rs,
                                                                            op0=ALU.mult, op1=ALU.mult)))(ch, cs, rs),
                      reads=[("x", ch, half), ("rstd", half), "v_g"], writes=[("x", ch, half)])
        self.store_x()

    def odd_small(self, l):
        S = self.S
        XF = self.XF
        j2 = l // 2
        A3 = self.A[:].rearrange("p (c t) -> p c t", c=16)
        s = self.slot_rr % 3
        self.slot_rr += 1
        wv = self.SL[s][:, 0:512].rearrange("p (k n) -> p k n", k=16)
        wsm = self.W[l]["sm"]
        S.dma("gpsimd", lambda e: e.dma_start(out=wv, in_=wsm.rearrange("(k p) n -> p k n", p=128)), f"d_slot{s}",
              writes=[("slot", s), ("slotv", s), ("slotw", s), ("slotx", s)])
        S.dma("sync", lambda e: e.dma_start(out=XF[:, 2072:2080], in_=self.od["fbias"][:, j2 * 8:(j2 + 1) * 8]), "d_fb", writes=["fbias"])
        for j in range(8):
            for kc in range(16):
                S.pe((lambda kc, j: (lambda e: e.matmul(self.PS[6][:, j * 32:(j + 1) * 32], lhsT=A3[:, kc, j * 128:(j + 1) * 128], rhs=wv[:, kc, :],
                                                        start=(kc == 0), stop=(kc == 15))))(kc, j),
                     reads=[("slot", s), ("A", kc, j // 4)], writes=[("ps", 6)])
        ps3 = self.PS[6][:, 0:256].rearrange("p (j n) -> p j n", j=8)
        z = XF[:, 1536:1600].rearrange("p (j h) -> p j h", j=8)
        S.dve(lambda e: e.tensor_tensor(out=z, in0=ps3[:, :, 0:8], in1=XF[:, 2072:2080].unsqueeze(1).broadcast_to([128, 8, 8]), op=ALU.add),
              reads=[("ps", 6), "fbias"], writes=[("rstd", 1)])
        S.act(lambda e: e.activation(out=XF[:, 1536:1600], in_=XF[:, 1536:1600], func=AF.Exp, scale=-1.0), reads=[("rstd", 1)], writes=[("rstd", 1)])
        S.act(lambda e: e.activation(out=XF[:, 1536:1600], in_=XF[:, 1536:1600], func=AF.Ln, bias=1.0), reads=[("rstd", 1)], writes=[("rstd", 1)])
        S.act(lambda e: e.activation(out=self.gsb[:], in_=ps3[:, :, 8:32], func=AF.Sigmoid), reads=[("ps", 6)], writes=["gsb"])
        S.dma("sync", lambda e: e.dma_start(out=self.ssend[l][:, 0:64], in_=XF[:, 1536:1600]), f"d_ss{l}", reads=[("rstd", 1)], writes=[("ssend", l)])

    def cmp_partials(self, l, idx):
        S = self.S
        XF = self.XF
        j2 = l // 2
        typ = idx // 2
        kb = idx % 2
        if idx % 2 == 0:
            s = self.slot_rr % 3
            self.slot_rr += 1
            self.w1slot = s
            w1v = self.SL[s][:, 0:4096].rearrange("p (l m) -> p l m", l=32)
            self.w1v = w1v
            src = self.od["w1"][j2][typ].rearrange("(l d) m -> d l m", d=128)
            S.dma("gpsimd", lambda e: e.dma_start(out=w1v, in_=src), f"d_slot{s}", writes=[("slot", s), ("slotv", s), ("slotw", s), ("slotx", s)])
            if typ == 0:
                S.dma("gpsimd", lambda e: e.dma_start(out=self.posT[:], in_=self.od["posT"][:, j2 * 64:(j2 + 1) * 64].rearrange("p (t l) -> p t l", t=2)),
                      "d_pos", writes=["posT"])
            for li in range(32):
                S.pe((lambda li, w1v, typ: (lambda e: e.matmul(self.PS[6][:, 300:301], lhsT=w1v[:, li, :], rhs=self.posT[:, typ, li:li + 1],
                                                               start=(li == 0), stop=(li == 31))))(li, w1v, typ),
                     reads=[("slot", s), "posT"], writes=[("ps", 6)])
            S.act((lambda typ: (lambda e: e.copy(out=self.vec[:, self.V_POSB + typ:self.V_POSB + typ + 1], in_=self.PS[6][:, 300:301])))(typ),
                  reads=[("ps", 6)], writes=["v_posb"])
        s = self.w1slot
        w1v = self.w1v
        kv = self.kst[kb][:].rearrange("p (n r) -> p n r", r=16)
        for ab in range(2):
            b = self.next_ps(0, 4)
            for li in range(16):
                S.pe((lambda b, li, ab, w1v, kv: (lambda e: e.matmul(self.PS[b][:, 0:64], lhsT=w1v[:, ab * 16 + li, :], rhs=kv[:, :, li],
                                                                      start=(li == 0), stop=(li == 15))))(b, li, ab, w1v, kv),
                     reads=[("slot", s), ("kst", kb)], writes=[("ps", b)])
            o = 1024 + idx * 128 + ab * 64
            S.act((lambda b, o: (lambda e: e.copy(out=XF[:, o:o + 64], in_=self.PS[b][:, 0:64])))(b, o), reads=[("ps", b)], writes=[("rstd", 0)])
        if idx == 3:
            S.dma("sync", lambda e: e.dma_start(out=self.ssend[l][:, 64:576], in_=XF[:, 1024:1536]), f"d_ss{l}", reads=[("rstd", 0)], writes=[("ssend", l)])

    def gate_bcast(self, bank, c0, n, head, br, tcols):
        S = self.S
        A = self.A
        r = head * 3 + br
        wh = A[0:24, r * 64:(r + 1) * 64]
        gT = A[0:24, 10624:11648]
        for half in range(2):
            S.pe((lambda half: (lambda e: e.matmul(self.PS[bank][half * 64:(half + 1) * 64, c0:c0 + n], lhsT=wh, rhs=gT[:, tcols], start=True, stop=True)))(half),
                 reads=["wh", "gT"], writes=[("ps", bank)])

    def odd_attention(self, l):
        S = self.S
        A = self.A
        XF = self.XF
        od = self.od
        j2 = l // 2
        g3 = self.gath[l].ap().rearrange("(c r) t -> r c t", c=NCORE)
        sg = self.sgath[l].ap().rearrange("(c p) f -> p c f", p=128)
        wh = A[:, 0:4096]
        winm = A[:, 4096:5632].rearrange("p (s q) -> p s q", s=12)
        cmpm = A[:, 5632:6016].rearrange("p (s q) -> p s q", s=3)
        selbTn = [A[:, 6016:7040], A[:, 7040:8064]]
        ovl = A[:, 8064:8576].rearrange("p (n b) -> p n b", n=4)
        kcT = [A[:, 8576:9088], A[:, 9088:9600]]
        vcv = [A[:, 9600:10112].rearrange("p (n d) -> p n d", n=4), A[:, 10112:10624].rearrange("p (n d) -> p n d", n=4)]
        gT = A[0:24, 10624:11648]
        odacc = A[:, 11648:15744].rearrange("p (h t) -> p h t", h=4)
        hid = A[:, 15744:16256]
        akeys = [("A", c, h) for c in range(16) for h in range(2)]
        S.dma("sync", lambda e: e.dma_start(out=wh, in_=od["wh"]), "d_ac0", writes=akeys + ["wh"])
        S.dma("sync", lambda e: e.dma_start(out=A[:, 4096:5632], in_=od["winmask"]), "d_ac1", writes=["winm"])
        S.dma("sync", lambda e: e.dma_start(out=A[:, 5632:6016], in_=od["cmpmask"]), "d_ac2", writes=["cmpm"])
        S.dma("sync", lambda e: e.dma_start(out=A[:, 8064:8576], in_=od["ovl"]), "d_ac3", writes=["ovl"])
        S.dma("sync", lambda e: e.dma_start(out=XF[:, 576:816], in_=od["selbase"]), "d_ac4", writes=["selbase"])
        S.dma("sync", lambda e: e.dma_start(out=XF[:, 2064:2072], in_=od["onehot"]), "d_ac5", writes=["onehot"])
        S.dma("gpsimd", lambda e: e.dma_start(out=self.w2sb[:], in_=od["w2"][j2].rearrange("p (t m) -> p t m", t=2)), "d_w2", writes=["w2sb"])
        for j in range(8):
            S.pe((lambda j: (lambda e: e.transpose(self.PSB[0:24, j * 128:(j + 1) * 128], self.gsb[:, j, :], self.identb)))(j),
                 reads=["gsb", "cb"], writes=["psb"])
        S.act(lambda e: e.copy(out=gT, in_=self.PSB[0:24, :]), reads=["psb"], writes=["gT"])
        for idx in range(4):
            typ, g = idx // 2, idx % 2
            for ab in range(2):
                S.dma("sync", (lambda idx, ab: (lambda e: e.dma_start(out=self.tq[ab][:].rearrange("p (c f) -> p c f", c=8),
                                                                       in_=sg[:, :, 64 + idx * 128 + ab * 64:64 + idx * 128 + (ab + 1) * 64])))(idx, ab),
                      f"d_tq{ab}", reads=[("sgath", l)], writes=[("tq", ab)])
            t2 = self.tq[2]
            pbg = XF[:, 1024:1536]
            S.dve(lambda e: e.tensor_copy(out=t2[:].rearrange("p (j c m) -> p j c m", j=8, c=8),
                                          in_=self.tq[0][:].rearrange("p (c j m) -> p c j m", c=8, j=8).transpose([0, 2, 1, 3])),
                  reads=[("tq", 0)], writes=[("tq", 2)])
            S.dve(lambda e: e.tensor_copy(out=pbg.rearrange("p (j c m) -> p j c m", j=8, c=8),
                                          in_=self.tq[1][:].rearrange("p (c j m) -> p c j m", c=8, j=8).transpose([0, 2, 1, 3])),
                  reads=[("tq", 1)], writes=[("rstd", 0)])
            S.dve(lambda e: e.tensor_tensor(out=t2[:, 0:511], in0=t2[:, 0:511], in1=pbg[:, 1:512], op=ALU.add), reads=[("tq", 2), ("rstd", 0)], writes=[("tq", 2)])
            pb = self.vec[:, self.V_POSB + typ:self.V_POSB + typ + 1]
            S.dve((lambda pb: (lambda e: e.tensor_scalar(out=t2[:], in0=t2[:], scalar1=pb, scalar2=None, op0=ALU.add)))(pb), reads=[("tq", 2), "v_posb"], writes=[("tq", 2)])
            x2 = XF[:, 1536:2048]
            S.dve(lambda e: e.tensor_tensor(out=x2, in0=t2[:], in1=t2[:], op=ALU.mult), reads=[("tq", 2)], writes=[("rstd", 1)])
            S.dve(lambda e: e.tensor_scalar(out=x2, in0=x2, scalar1=0.044715, scalar2=1.0, op0=ALU.mult, op1=ALU.add), reads=[("rstd", 1)], writes=[("rstd", 1)])
            S.dve(lambda e: e.tensor_tensor(out=x2, in0=x2, in1=t2[:], op=ALU.mult), reads=[("rstd", 1), ("tq", 2)], writes=[("rstd", 1)])
            S.act(lambda e: e.activation(out=x2, in_=x2, func=AF.Tanh, scale=0.7978845608028654), reads=[("rstd", 1)], writes=[("rstd", 1)])
            S.dve(lambda e: e.tensor_scalar(out=x2, in0=x2, scalar1=1.0, scalar2=0.5, op0=ALU.add, op1=ALU.mult), reads=[("rstd", 1)], writes=[("rstd", 1)])
            S.dve(lambda e: e.tensor_tensor(out=hid, in0=x2, in1=t2[:], op=ALU.mult), reads=[("rstd", 1), ("tq", 2)], writes=["hid"])
            if typ == 0:
                S.pe(lambda e: e.matmul(self.PS[6][:], lhsT=self.w2sb[:, 0, :], rhs=hid, start=True, stop=True), reads=["hid", "w2sb"], writes=[("ps", 6)])
                S.act((lambda g: (lambda e: e.copy(out=kcT[g], in_=self.PS[6][:])))(g), reads=[("ps", 6)], writes=[("kcT", g)])
            else:
                for nt in range(4):
                    S.pe((lambda nt: (lambda e: e.matmul(self.PS[6][:, nt * 128:(nt + 1) * 128], lhsT=hid[:, nt * 128:(nt + 1) * 128], rhs=self.w2sb[:, 1, :],
                                                         start=True, stop=True)))(nt),
                         reads=["hid", "w2sb"], writes=[("ps", 6)])
                S.act((lambda g: (lambda e: e.copy(out=vcv[g], in_=self.PS[6][:].rearrange("p (n d) -> p n d", n=4))))(g), reads=[("ps", 6)], writes=[("vcv", g)])
        lfg = self.tq[0]
        S.dma("sync", lambda e: e.dma_start(out=lfg[:].rearrange("p (c f) -> p c f", c=8), in_=sg[:, :, 0:64]), "d_tq0", reads=[("sgath", l)], writes=[("tq", 0)])
        S.pe(lambda e: e.matmul(self.PS[4][:], lhsT=self.tri, rhs=lfg[:], start=True, stop=True), reads=[("tq", 0), "cf"], writes=[("ps", 4)])
        S.pe(lambda e: e.matmul(self.PS[5][:], lhsT=self.onesf, rhs=lfg[:], start=True, stop=True), reads=[("tq", 0), "cf"], writes=[("ps", 5)])
        ta, tb = self.tq[1], self.tq[2]
        perm = lambda ap: ap.rearrange("p (c j h) -> p c j h", c=8, j=8).transpose([0, 2, 1, 3])
        glob = lambda ap: ap.rearrange("p (j c h) -> p j c h", j=8, c=8)
        S.dve(lambda e: e.tensor_copy(out=glob(ta[:]), in_=perm(self.PS[5][:])), reads=[("ps", 5)], writes=[("tq", 1)])
        S.dve(lambda e: e.memset(tb[:, 0:8], 0.0), writes=[("tq", 2)])
        S.dve(lambda e: e.tensor_copy(out=tb[:, 8:512], in_=ta[:, 0:504]), reads=[("tq", 1)], writes=[("tq", 2)])
        cur, nxt, ck_, nk_ = tb, ta, ("tq", 2), ("tq", 1)
        for sh in (1, 2, 4, 8, 16, 32):
            w = sh * 8
            S.dve((lambda cur, nxt, w: (lambda e: e.tensor_tensor(out=nxt[:, w:512], in0=cur[:, w:512], in1=cur[:, 0:512 - w], op=ALU.add)))(cur, nxt, w),
                  reads=[ck_], writes=[nk_])
            S.dve((lambda cur, nxt, w: (lambda e: e.tensor_copy(out=nxt[:, 0:w], in_=cur[:, 0:w])))(cur, nxt, w), reads=[ck_], writes=[nk_])
            cur, nxt, ck_, nk_ = nxt, cur, nk_, ck_
        off = cur
        offk = ck_
        NCf = XF[:, 0:512]
        S.dve(lambda e: e.tensor_tensor(out=glob(NCf), in0=perm(self.PS[4][:]), in1=glob(off[:]), op=ALU.add), reads=[("ps", 4), offk], writes=["NCf"])
        tmp = nxt
        S.dve(lambda e: e.tensor_tensor(out=tmp[:].rearrange("p (j h c) -> p j h c", j=8, h=8),
                                        in0=glob(off[:]).transpose([0, 1, 3, 2]),
                                        in1=XF[:, 2064:2072].unsqueeze(1).unsqueeze(1).broadcast_to([128, 8, 8, 8]), op=ALU.mult),
              reads=[offk, "onehot"], writes=[nk_])
        offq = XF[:, 512:576]
        S.dve(lambda e: e.tensor_reduce(out=offq, in_=tmp[:].rearrange("p (a c) -> p a c", c=8), axis=AX.X, op=ALU.add), reads=[nk_], writes=["offq"])
        NC3 = NCf.rearrange("p (b h) -> p b h", h=8)
        oq3 = offq.rearrange("p (j h) -> p j h", h=8)
        ni = 24
        for h in range(8):
            fbt = XF[:, 1024 + (h % 2) * 512:1024 + (h % 2 + 1) * 512]
            fbk = ("rstd", h % 2)
            fb3 = fbt.rearrange("p (b j) -> p b j", j=8)
            S.dve((lambda h, fb3: (lambda e: e.tensor_tensor(out=fb3, in0=NC3[:, :, h].unsqueeze(2).broadcast_to([128, 64, 8]),
                                                             in1=oq3[:, :, h].unsqueeze(1).broadcast_to([128, 64, 8]), op=ALU.subtract)))(h, fb3),
                  reads=["NCf", "offq"], writes=[fbk])
            for cp in range(4):
                s, kd, vd = self.kv_load(l, h, 12 + h, cp, ni)
                for cc in range(2):
                    ck = 2 * cp + cc
                    for jk in range(8):
                        b = 8 * jk + ck
                        groups = []
                        if jk < 4:
                            groups.append((0, jk * 128, 512))
                        groups.append((1, max(512, jk * 128), 1024))
                        for (grp, c0, c1) in groups:
                            n = c1 - c0
                            sbank = self.next_ps(0, 3)
                            extra = []
                            if c0 == jk * 128:
                                extra.append((self.identb, self.cmask[:, ck, :], 0, 128, ["cb", "cmask"]))
                            ebias = [((jq * 128 - c0), (jq * 128 - c0) + 128, fb3[:, b, jq:jq + 1], [fbk]) for jq in range(c0 // 128, c1 // 128)]
                            last = (ck == 7 and jk == (3 if grp == 0 else 7))
                            fst = (ck == 0 and jk == 0)
                            self.attn_unit(sbank, kd[:, cc, jk * 128:(jk + 1) * 128], vd[:, cc, jk, :], self.QT[:, h, c0:c1], n, extra, ebias,
                                           3 + grp, 5 + grp, slice(c0 - grp * 512, c1 - grp * 512), fst, last,
                                           [("slot", s), ("Q", h, grp)], [("slotv", s)])
            for grp in range(2):
                self.finalize_head(3 + grp, 5 + grp, self.QT[:, h, grp * 512:(grp + 1) * 512], [("Q", h, grp)])
        impa, tmp2, m1, m2 = XF[:, 816:944], XF[:, 944:1072], XF[:, 2048:2056], XF[:, 2056:2064]
        def nsa_group(g):
            qh0 = 8 + 4 * g
            for j in range(8):
                ntb = j // 2
                qap = self.QT[:, qh0:qh0 + 4, j * 128:(j + 1) * 128]
                qkeys = [("Q", qh0 + hh, j // 4) for hh in range(4)]
                for nt in range(ntb + 1):
                    sbank = self.next_ps(0, 3)
                    slot_m = None
                    if nt == ntb:
                        slot_m = 0 if j % 2 == 0 else 1
                    elif nt == ntb - 1 and j % 2 == 0:
                        slot_m = 2
                    S.pe((lambda sbank, nt, qap, slot_m: (lambda e: e.matmul(self.PS[sbank][:], lhsT=kcT[g][:, nt * 128:(nt + 1) * 128], rhs=qap,
                                                                             start=True, stop=(slot_m is None))))(sbank, nt, qap, slot_m),
                         reads=[("kcT", g)] + qkeys, writes=[("ps", sbank)])
                    if slot_m is not None:
                        for hh in range(4):
                            S.pe((lambda sbank, hh, slot_m: (lambda e: e.matmul(self.PS[sbank][:, hh * 128:(hh + 1) * 128], lhsT=self.identb, rhs=cmpm[:, slot_m, :],
                                                                                start=False, stop=(hh == 3))))(sbank, hh, slot_m),
                                 reads=["cb", "cmpm"], writes=[("ps", sbank)])
                    S.act((lambda sbank, nt: (lambda e: e.activation(out=self.PC[:, nt, :], in_=self.PS[sbank][:], func=AF.Exp, scale=SCALE)))(sbank, nt),
                          reads=[("ps", sbank)], writes=[("PC", nt)])
                    S.pe((lambda nt: (lambda e: e.matmul(self.PS[3][:], lhsT=vcv[g][:, nt, :], rhs=self.PC[:, nt, :], start=(nt == 0), stop=(nt == ntb),
                                                         skip_group_check=True)))(nt),
                         reads=[("PC", nt), ("vcv", g)], writes=[("ps", 3)])
                    S.pe((lambda nt: (lambda e: e.matmul(self.PS[5][:], lhsT=self.onesb, rhs=self.PC[:, nt, :], start=(nt == 0), stop=(nt == ntb),
                                                         skip_group_check=True)))(nt),
                         reads=[("PC", nt), "cb"], writes=[("ps", 5)])
                rd = self.tq[self.tq_rr % 3]
                rk = ("tq", self.tq_rr % 3)
                self.tq_rr += 1
                S.dve((lambda rd: (lambda e: e.tensor_scalar(out=rd[:], in0=self.PS[5][:], scalar1=1e-30, scalar2=None, op0=ALU.max)))(rd), reads=[("ps", 5)], writes=[rk])
                S.dve((lambda rd: (lambda e: e.reciprocal(out=rd[:], in_=rd[:])))(rd), reads=[rk], writes=[rk])
                for nt in range(ntb + 1):
                    S.dve((lambda rd, nt: (lambda e: e.tensor_tensor(out=self.PC[:, nt, :], in0=self.PC[:, nt, :], in1=rd[:], op=ALU.mult)))(rd, nt),
                          reads=[("PC", nt), rk], writes=[("PC", nt)])
                nmm = 4 * (ntb + 1)
                k = 0
                for hh in range(4):
                    for nt in range(ntb + 1):
                        S.pe((lambda hh, nt, k: (lambda e: e.matmul(self.PS[6][:, 0:128], lhsT=self.PC[:, nt, hh * 128:(hh + 1) * 128], rhs=ovl[:, nt, :],
                                                                    start=(k == 0), stop=(k == nmm - 1))))(hh, nt, k),
                             reads=[("PC", nt), "ovl"], writes=[("ps", 6)])
                        k += 1
                w0 = 576 + 112 - 16 * j
                S.dve((lambda w0: (lambda e: e.tensor_tensor(out=impa, in0=self.PS[6][:, 0:128], in1=XF[:, w0:w0 + 128], op=ALU.add)))(w0),
                      reads=[("ps", 6), "selbase"], writes=["impa"])
                S.dve(lambda e: e.memset(impa[:, 0:1], 100.0), reads=["impa"], writes=["impa"])
                S.dve(lambda e: e.max(out=m1, in_=impa), reads=["impa"], writes=["m1"])
                S.dve(lambda e: e.match_replace(out=tmp2, in_to_replace=m1, in_values=impa, imm_value=-1e30), reads=["impa", "m1"], writes=["tmp2"])
                S.dve(lambda e: e.max(out=m2, in_=tmp2), reads=["tmp2"], writes=["m2"])
                S.dve(lambda e: e.tensor_scalar(out=tmp2, in0=impa, scalar1=m2[:, 7:8], scalar2=None, op0=ALU.is_ge), reads=["impa", "m2"], writes=["tmp2"])
                S.dve(lambda e: e.tensor_scalar(out=self.selb[:, 0:128], in0=tmp2, scalar1=-1.0, scalar2=-NEG, op0=ALU.add, op1=ALU.mult), reads=["tmp2"], writes=["selb"])
                S.pe(lambda e: e.transpose(self.PSB[:, 0:128], self.selb[:, 0:128], self.identb), reads=["selb", "cb"], writes=["psb"])
                S.act((lambda j: (lambda e: e.copy(out=selbTn[g][:, j * 128:(j + 1) * 128], in_=self.PSB[:, 0:128])))(j), reads=["psb"], writes=[("selbTn", g)])
                for hh in range(4):
                    self.gate_bcast(4, hh * 128, 128, 4 * g + hh, 0, slice(j * 128, (j + 1) * 128))
                S.dve((lambda rd: (lambda e: e.tensor_tensor(out=rd[:], in0=self.PS[4][:], in1=rd[:], op=ALU.mult)))(rd), reads=[("ps", 4), rk], writes=[rk])
                S.dve((lambda rd, j: (lambda e: e.tensor_tensor(out=odacc[:, :, j * 128:(j + 1) * 128], in0=self.PS[3][:].rearrange("p (h t) -> p h t", h=4),
                                                                in1=rd[:].rearrange("p (h t) -> p h t", h=4), op=ALU.mult)))(rd, j),
                      reads=[("ps", 3), rk], writes=[("odacc", j)])
            for hh in range(4):
                h = qh0 + hh
                for cp in range(4):
                    s, kd, vd = self.kv_load(l, 8 + g, 20 + g, cp, ni)
                    for cc in range(2):
                        ck = 2 * cp + cc
                        for jk in range(8):
                            gb = 8 * jk + ck
                            pbase = 64 * ((2 * gb) // 64)
                            r0 = (2 * gb) % 64
                            groups = []
                            if jk < 4:
                                groups.append((0, jk * 128, 512))
                            groups.append((1, max(512, jk * 128), 1024))
                            for (grp, c0, c1) in groups:
                                n = c1 - c0
                                sbank = self.next_ps(0, 3)
                                extra = []
                                if c0 == jk * 128:
                                    extra.append((self.identb, self.cmask[:, ck, :], 0, 128, ["cb", "cmask"]))
                                extra.append((wh[pbase:pbase + 64, r0 * 64:r0 * 64 + 128], selbTn[g][pbase:pbase + 64, c0:c1], 0, n, ["wh", ("selbTn", g)]))
                                last = (ck == 7 and jk == (3 if grp == 0 else 7))
                                fst = (ck == 0 and jk == 0)
                                self.attn_unit(sbank, kd[:, cc, jk * 128:(jk + 1) * 128], vd[:, cc, jk, :], self.QT[:, h, c0:c1], n, extra, None,
                                               3 + grp, 5 + grp, slice(c0 - grp * 512, c1 - grp * 512), fst, last,
                                               [("slot", s), ("Q", h, grp)], [("slotv", s)])
                self.attn_flush()
                for grp in range(2):
                    rd = self.tq[self.tq_rr % 3]
                    rk = ("tq", self.tq_rr % 3)
                    self.tq_rr += 1
                    t2_ = self.tq[self.tq_rr % 3]
                    tk2 = ("tq", self.tq_rr % 3)
                    self.tq_rr += 1
                    S.dve((lambda rd, grp: (lambda e: e.tensor_scalar(out=rd[:], in0=self.PS[5 + grp][:], scalar1=1e-30, scalar2=None, op0=ALU.max)))(rd, grp),
                          reads=[("ps", 5 + grp)], writes=[rk])
                    S.dve((lambda rd: (lambda e: e.reciprocal(out=rd[:], in_=rd[:])))(rd), reads=[rk], writes=[rk])
                    gbank = 5 + grp
                    self.gate_bcast(gbank, 0, 512, 4 * g + hh, 1, slice(grp * 512, (grp + 1) * 512))
                    S.dve((lambda rd, gbank: (lambda e: e.tensor_tensor(out=rd[:], in0=self.PS[gbank][:], in1=rd[:], op=ALU.mult)))(rd, gbank),
                          reads=[("ps", gbank), rk], writes=[rk])
                    S.dve((lambda rd, t2_, grp: (lambda e: e.tensor_tensor(out=t2_[:], in0=self.PS[3 + grp][:], in1=rd[:], op=ALU.mult)))(rd, t2_, grp),
                          reads=[("ps", 3 + grp), rk], writes=[tk2])
                    oa = odacc[:, hh, grp * 512:(grp + 1) * 512]
                    S.dve((lambda t2_, oa: (lambda e: e.tensor_tensor(out=oa, in0=oa, in1=t2_[:], op=ALU.add)))(t2_, oa),
                          reads=[tk2] + [("odacc", jq) for jq in range(grp * 4, grp * 4 + 4)], writes=[("odacc", jq) for jq in range(grp * 4, grp * 4 + 4)])
            kitem, vitem = 10 + g, 22 + g
            for j in range(8):
                s = self.slot_rr % 3
                self.slot_rr += 1
                slot = self.SL[s]
                kw = slot[:, 0:1536].rearrange("p (s t) -> p s t", s=12)
                vw = slot[:, 2048:2048 + 1536].rearrange("p (s t) -> p s t", s=12)
                rows_k = slice(kitem * 128, (kitem + 1) * 128)
                rows_v = slice(vitem * 128, (vitem + 1) * 128)
                S.dma("sync", (lambda kw, j, rows_k: (lambda e: e.dma_start(out=kw[:, 4:12, :], in_=g3[rows_k, :, j * 128:(j + 1) * 128])))(kw, j, rows_k),
                      f"d_slot{s}", reads=[("gath", l)], writes=[("slot", s)])
                S.dma("sync", (lambda vw, j, rows_v: (lambda e: e.dma_start(out=vw[:, 4:12, :], in_=g3[rows_v, :, j * 128:(j + 1) * 128])))(vw, j, rows_v),
                      f"d_slotv{s}", reads=[("gath", l)], writes=[("slotv", s)])
                if j > 0:
                    S.dma("sync", (lambda kw, j, rows_k: (lambda e: e.dma_start(out=kw[:, 0:4, :], in_=g3[rows_k, 4:8, (j - 1) * 128:j * 128])))(kw, j, rows_k),
                          f"d_slotw{s}", reads=[("gath", l)], writes=[("slotw", s)])
                    S.dma("sync", (lambda vw, j, rows_v: (lambda e: e.dma_start(out=vw[:, 0:4, :], in_=g3[rows_v, 4:8, (j - 1) * 128:j * 128])))(vw, j, rows_v),
                          f"d_slotx{s}", reads=[("gath", l)], writes=[("slotx", s)])
                cands = list(range(0 if j > 0 else 4, 12))
                qap = self.QT[:, qh0:qh0 + 4, j * 128:(j + 1) * 128]
                qkeys = [("Q", qh0 + hh, j // 4) for hh in range(4)]
                ob, db = 3 + (j % 2), 5 + (j % 2)
                for si, sidx in enumerate(cands):
                    sbank = self.next_ps(0, 3)
                    extra = [(self.identb, winm[:, sidx, :], hh * 128, (hh + 1) * 128, ["cb", "winm"]) for hh in range(4)]
                    self.attn_unit(sbank, kw[:, sidx, :], vw[:, sidx, :], qap, 512, extra, None, ob, db, slice(0, 512),
                                   si == 0, si == len(cands) - 1, [("slot", s), ("slotw", s)] + qkeys, [("slotv", s), ("slotx", s)])
                self.attn_flush()
                rd = self.tq[self.tq_rr % 3]
                rk = ("tq", self.tq_rr % 3)
                self.tq_rr += 1
                t2_ = self.tq[self.tq_rr % 3]
                tk2 = ("tq", self.tq_rr % 3)
                self.tq_rr += 1
                S.dve((lambda rd, db: (lambda e: e.tensor_scalar(out=rd[:], in0=self.PS[db][:], scalar1=1e-30, scalar2=None, op0=ALU.max)))(rd, db),
                      reads=[("ps", db)], writes=[rk])
                S.dve((lambda rd: (lambda e: e.reciprocal(out=rd[:], in_=rd[:])))(rd), reads=[rk], writes=[rk])
                for hh in range(4):
                    self.gate_bcast(db, hh * 128, 128, 4 * g + hh, 2, slice(j * 128, (j + 1) * 128))
                S.dve((lambda rd, db: (lambda e: e.tensor_tensor(out=rd[:], in0=self.PS[db][:], in1=rd[:], op=ALU.mult)))(rd, db), reads=[("ps", db), rk], writes=[rk])
                S.dve((lambda rd, t2_, ob: (lambda e: e.tensor_tensor(out=t2_[:], in0=self.PS[ob][:], in1=rd[:], op=ALU.mult)))(rd, t2_, ob),
                      reads=[("ps", ob), rk], writes=[tk2])
                S.dve((lambda t2_, j, qap: (lambda e: e.tensor_tensor(out=qap, in0=odacc[:, :, j * 128:(j + 1) * 128],
                                                                      in1=t2_[:].rearrange("p (h t) -> p h t", h=4), op=ALU.add)))(t2_, j, qap),
                      reads=[tk2, ("odacc", j)], writes=qkeys)

        for g in range(2):
            nsa_group(g)

def tok_index(c):
    j = np.arange(8)[:, None]
    i = np.arange(128)[None, :]
    return ((8 * j + c) * 128 + i).reshape(-1)


def bf(a):
    return np.ascontiguousarray(a.astype(ml_dtypes.bfloat16))


def host_consts(c):
    tok = tok_index(c)
    inv = 1.0 / (10000.0 ** (np.arange(0, 128, 2, dtype=np.float32) / 128.0))
    ang = tok.astype(np.float32)[:, None] * inv[None, :]
    cos = np.cos(ang).astype(np.float32).T
    sin = np.sin(ang).astype(np.float32).T
    ropecos = np.concatenate([cos, cos], 0)
    ropesin = np.concatenate([-sin, sin], 0)
    ii = np.arange(128)
    cm = np.zeros((128, 8, 128), np.float32)
    for ck in range(8):
        if ck > c:
            cm[:, ck, :] = NEG
        elif ck == c:
            cm[:, ck, :] = np.where(ii[:, None] <= ii[None, :], 0.0, NEG)
    identf = np.eye(128, dtype=np.float32)
    Rm = np.zeros((128, 128), np.float32)
    for m in range(128):
        Rm[(m + 64) % 128, m] = 1.0
    onesf = np.ones((128, 128), np.float32)
    tri = (ii[:, None] <= ii[None, :]).astype(np.float32)
    constf = np.concatenate([identf, Rm, onesf, tri], 1)
    constb = bf(np.concatenate([identf, onesf], 1))
    wm = np.zeros((32, 4096), np.float32)
    for k in range(32):
        wm[k, k * 128:(k + 1) * 128] = 1.0
    sw = np.full((128, 9, 128), NEG, np.float32)
    for s in range(9):
        diff = c + 1 - s
        if diff == 0:
            sw[:, s, :] = np.where(ii[:, None] <= ii[None, :], 0.0, NEG)
        elif diff == 1:
            sw[:, s, :] = np.where(ii[:, None] > ii[None, :], 0.0, NEG)
    gb = np.zeros((128, 8, 32), np.float32)
    past = np.zeros((128, 8, 32), np.float32)
    own = np.zeros((128, 8, 32), np.float32)
    for j in range(8):
        o = (8 * j + c) // 2
        gb[:, j, o:] = -1e30
        past[:, j, :o] = 1.0
        own[:, j, o] = 1.0
    gsel = np.concatenate([gb.reshape(128, 256), past.reshape(128, 256), own.reshape(128, 256)], 1)
    wh = np.zeros((128, 4096), np.float32)
    for k in range(128):
        wh[k, (k % 64) * 64:(k % 64 + 1) * 64] = 1.0
    wn = np.full((128, 12, 128), NEG, np.float32)
    for s in range(12):
        diff = c + 4 - s
        if diff == 0:
            wn[:, s, :] = np.where(ii[:, None] <= ii[None, :], 0.0, NEG)
        elif diff in (1, 2, 3):
            wn[:, s, :] = 0.0
        elif diff == 4:
            wn[:, s, :] = np.where(ii[:, None] > ii[None, :], 0.0, NEG)
    cmpm = np.zeros((128, 3, 128), np.float32)
    for slot, sh in enumerate((0, 64, 128)):
        rel = ii[:, None] - sh - 8 * c
        cmpm[:, slot, :] = np.where(16 * rel + 31 <= ii[None, :], 0.0, NEG)
    ovl = np.zeros((128, 4, 128), np.float32)
    for nt in range(4):
        cs_ = 16 * (128 * nt + ii)[:, None]
        ss_ = 64 * np.arange(128)[None, :]
        ovl[:, nt, :] = ((cs_ < ss_ + 64) & (cs_ + 32 > ss_)).astype(np.float32)
    hi = (ii >= 64).astype(np.int64)[:, None]
    dd = (np.arange(240)[None, :] - 112) - 2 * c
    selbase = np.where((dd == hi) | (dd == hi - 1), 100.0, np.where(dd > hi, -100.0, 0.0)).astype(np.float32)
    onehot = np.zeros((128, 8), np.float32)
    onehot[:, c] = 1.0
    return dict(ropecos=ropecos, ropesin=ropesin, cmask=bf(cm.reshape(128, 1024)), constf=constf, constb=constb,
                wm=bf(wm), swamask=bf(sw.reshape(128, 9 * 128)), gsel=gsel,
                wh=bf(wh), winmask=bf(wn.reshape(128, 12 * 128)), cmpmask=bf(cmpm.reshape(128, 3 * 128)), ovl=bf(ovl.reshape(128, 512)),
                selbase=np.ascontiguousarray(selbase), onehot=onehot)


def make_in_maps(inp, nlayers, need_mlp_last=True):
    x = np.asarray(inp["x"], np.float32)[0]
    shared = {}
    cvec = np.asarray(inp["c"], np.float32)[0]
    shared["cT"] = np.ascontiguousarray(cvec.reshape(16, 128).T)
    gv = np.concatenate([np.asarray(inp["norm_mix_g"], np.float32).reshape(4, 16, 128).transpose(2, 0, 1).reshape(128, 64),
                         np.asarray(inp["norm_mlp_g"], np.float32).reshape(4, 16, 128).transpose(2, 0, 1).reshape(128, 64),
                         np.asarray(inp["final_norm_g"], np.float32).reshape(16, 128).T], 1)
    shared["gvec"] = np.ascontiguousarray(gv)
    sinks = np.asarray(inp["even_sinks"], np.float32)
    shared["sinksb"] = np.ascontiguousarray(np.broadcast_to(sinks.reshape(1, 16), (128, 16)))
    shared["fbias"] = np.ascontiguousarray(np.broadcast_to(np.asarray(inp["fox_forget_b"], np.float32).reshape(1, 16), (128, 16)))
    pos = [np.asarray(inp["nsa_k_pos"], np.float32), np.asarray(inp["nsa_v_pos"], np.float32)]
    shared["posT"] = np.ascontiguousarray(np.concatenate([pos[t][j2].T for j2 in range(2) for t in range(2)], 1))
    for j2 in range(2):
        shared[f"w1_{j2}_0"] = np.asarray(inp["nsa_k_w1"][j2], np.float32)
        shared[f"w1_{j2}_1"] = np.asarray(inp["nsa_v_w1"][j2], np.float32)
        shared[f"w2_{j2}"] = np.ascontiguousarray(np.concatenate([np.asarray(inp["nsa_k_w2"][j2], np.float32), np.asarray(inp["nsa_v_w2"][j2], np.float32)], 1))
    for l in range(nlayers):
        j = l // 2
        if l % 2 == 0:
            fm, tm = host_w_layout(np.asarray(inp["even_w_in"][j], np.float32), even_spec())
            shared[f"wout{l}"] = np.asarray(inp["even_w_out"][j], np.float32)
        else:
            w = np.asarray(inp["odd_w_in"][j], np.float32)
            fm, tm = host_w_layout(w, odd_spec())
            shared[f"wsm{l}"] = np.ascontiguousarray(np.concatenate([w[:, 3072:3080], w[:, 5640:5664]], 1))
            shared[f"wout{l}"] = np.asarray(inp["odd_w_out"][j], np.float32)
        shared[f"wfm{l}"] = fm
        shared[f"wtm{l}"] = tm
        if need_mlp_last or l < nlayers - 1:
            shared[f"wup{l}"] = np.asarray(inp["mlp_up"][l], np.float32)
            shared[f"wdown{l}"] = np.asarray(inp["mlp_down"][l], np.float32)
    ada_w = np.asarray(inp["ada_w"], np.float32)
    ada_b = np.asarray(inp["ada_b"], np.float32)
    in_maps = []
    for c in range(NCORE):
        m = dict(shared)
        tok = tok_index(c)
        m["xT"] = np.ascontiguousarray(x[tok].T.reshape(16, 128, TL))
        m.update(host_consts(c))
        m["adaw"] = np.ascontiguousarray(ada_w[:, :, c * 1536:(c + 1) * 1536])
        m["adabT"] = np.ascontiguousarray(ada_b[:, c * 1536:(c + 1) * 1536].reshape(4, 12, 128).transpose(2, 0, 1).reshape(128, 48))
        in_maps.append(m)
    return in_maps


def _run_phase(ph, in_maps, extra):
    b = Builder(DEPTH, False, True, phase=ph)
    nc = b.build()
    names = set(b._ins.keys())
    maps = []
    for c in range(NCORE):
        m = dict(in_maps[c])
        m.update(extra[c])
        missing = names - set(m.keys())
        assert not missing, missing
        maps.append({k: m[k] for k in names})
    res = run_bass_kernel_spmd(nc, maps, core_ids=list(range(NCORE)))
    return res.results


def _assemble(results):
    out = np.zeros((SEQ, D), np.float32)
    for c in range(NCORE):
        oT = np.asarray(results[c]["outT"]).reshape(D, TL)
        out[tok_index(c)] = oT.T
    return out[None]


def kernel_multi(debug_cb=None, **inputs):
    in_maps = make_in_maps(inputs, DEPTH)
    r = _run_phase("M", in_maps, [{} for _ in range(NCORE)])
    modgath = np.concatenate([np.asarray(r[c]["modsend_out"]) for c in range(NCORE)], 0)
    extra = [{"modgath_in": modgath} for _ in range(NCORE)]
    for ph in range(DEPTH + 1):
        r = _run_phase(ph, in_maps, extra)
        if ph == DEPTH:
            return _assemble(r)
        gath = np.concatenate([np.asarray(r[c][f"send{ph}_out"]) for c in range(NCORE)], 0)
        sgath = np.concatenate([np.asarray(r[c][f"ssend{ph}_out"]) for c in range(NCORE)], 0)
        if debug_cb is not None:
            debug_cb(ph, r)
        extra = [{"modgath_in": modgath, "xstate_in": np.asarray(r[c]["xstate_out"]), "qstate_in": np.asarray(r[c]["qstate_out"]),
                  "gstate_in": np.asarray(r[c]["gstate_out"]), "pstate_in": np.asarray(r[c]["pstate_out"]),
                  f"gath{ph}_in": gath, f"sgath{ph}_in": sgath} for c in range(NCORE)]


def kernel_fused(**inputs):
    b = Builder(DEPTH, False, True)
    in_maps = make_in_maps(inputs, DEPTH)
    nc = b.build()
    names = set(b._ins.keys())
    in_maps = [{k: m[k] for k in names} for m in in_maps]
    res = run_bass_kernel_spmd(nc, in_maps, core_ids=list(range(NCORE)))
    return _assemble(res.results)


def kernel(**inputs):
    return kernel_multi(**inputs)
```

```python
import contextlib
import numpy as np
import ml_dtypes
import concourse.bass as bass
import concourse.mybir as mybir
from concourse.bass_utils import run_bass_kernel_spmd

F32 = mybir.dt.float32
BF16 = mybir.dt.bfloat16
AF = mybir.ActivationFunctionType
ALU = mybir.AluOpType
AX = mybir.AxisListType

NCORE = 8
D = 2048
SEQ = 8192
TL = 1024
NCH = 16
DFF = 8192
DEPTH = 4
SCALE = 128.0 ** -0.5
NEG = -30000.0
EPS = 1e-6
ENGS = ("tensor", "vector", "scalar", "gpsimd", "sync")


class Sched:
    def __init__(self, same_engine_sync=True):
        self.ops = []
        self.buf = {}
        self.same_engine_sync = same_engine_sync
        self.ext_keys = set()
        self.ext_ops = []

    def op(self, eng, fn, reads=(), writes=(), kind="c", dsem=None):
        deps = set()
        for k in reads:
            st = self.buf.setdefault(k, [None, []])
            if st[0] is not None:
                deps.add(st[0])
        for k in writes:
            st = self.buf.setdefault(k, [None, []])
            if st[0] is not None:
                deps.add(st[0])
            deps.update(st[1])
        i = len(self.ops)
        self.ops.append(dict(eng=eng, fn=fn, deps=deps, kind=kind, dsem=dsem))
        if kind == "d" and any(k in self.ext_keys for k in writes):
            self.ext_ops.append(i)
        for k in reads:
            self.buf[k][1].append(i)
        for k in writes:
            self.buf[k] = [i, []]
        return i

    def pe(self, fn, reads=(), writes=()):
        return self.op("tensor", fn, reads, writes)

    def dve(self, fn, reads=(), writes=()):
        return self.op("vector", fn, reads, writes)

    def act(self, fn, reads=(), writes=()):
        return self.op("scalar", fn, reads, writes)

    def pool(self, fn, reads=(), writes=()):
        return self.op("gpsimd", fn, reads, writes)

    def dma(self, eng, fn, dsem, reads=(), writes=()):
        return self.op(eng, fn, reads, writes, kind="d", dsem=dsem)

    def cc(self, fn, dsem, reads=(), writes=()):
        return self.op("gpsimd", fn, reads, writes, kind="cc", dsem=dsem)

    def dsem_names(self):
        return sorted({o["dsem"] for o in self.ops if o["dsem"] is not None})

    def emit(self, block, sems, final_wait_ops=()):
        ops = self.ops
        n = len(ops)
        needed = [False] * n
        for o in ops:
            for d in o["deps"]:
                needed[d] = True
        for d in final_wait_ops:
            needed[d] = True
        cnt = {}
        semval = [None] * n
        for i, o in enumerate(ops):
            if o["kind"] == "c":
                if needed[i]:
                    key = "e_" + o["eng"]
                    cnt[key] = cnt.get(key, 0) + 1
                    semval[i] = (key, cnt[key], 1)
            elif o["kind"] == "d":
                key = o["dsem"]
                cnt[key] = cnt.get(key, 0) + 16
                semval[i] = (key, cnt[key], 16)
            else:
                key = o["dsem"]
                cnt[key] = cnt.get(key, 0) + 1
                semval[i] = (key, cnt[key], 1)
        per_eng = {e: [] for e in ENGS}
        for i, o in enumerate(ops):
            per_eng[o["eng"]].append(i)
        self.stats = {e: len(v) for e, v in per_eng.items()}
        self.stats["sems"] = dict(cnt)
        ses = self.same_engine_sync

        def make(engname):
            def body(eng):
                waited = {}
                for i in per_eng[engname]:
                    o = ops[i]
                    need = {}
                    for d in o["deps"]:
                        od = ops[d]
                        if od["eng"] == engname and od["kind"] == "c":
                            if engname == "tensor" or not ses:
                                continue
                        k, v, _ = semval[d]
                        if v > need.get(k, 0):
                            need[k] = v
                    for k, v in need.items():
                        if waited.get(k, 0) < v:
                            eng.wait_ge(sems[k], v)
                            waited[k] = v
                    ins = o["fn"](eng)
                    if semval[i] is not None:
                        k, v, inc = semval[i]
                        ins.then_inc(sems[k], inc)
                if engname == "sync":
                    for d in final_wait_ops:
                        k, v, _ = semval[d]
                        eng.wait_ge(sems[k], v)
            return body

        block.tensor(make("tensor"))
        block.vector(make("vector"))
        block.scalar(make("scalar"))
        block.gpsimd(make("gpsimd"))
        block.sync(make("sync"))


def even_spec():
    fm = []
    for h in range(8):
        fm.append((h * 128, True, ("q", h)))
    for h in range(8):
        fm.append((3072 + h * 128, True, ("q", 8 + h)))
    for h in range(8):
        fm.append((1024 + h * 128, True, ("k", h)))
    for g in range(2):
        fm.append((4096 + g * 128, True, ("k", 8 + g)))
    tm = []
    for h in range(8):
        tm.append((2048 + h * 128, 10 + h))
    for g in range(2):
        tm.append((4352 + g * 128, 18 + g))
    return dict(fm=fm, tm=tm, nitems=20, nk=10, small=None)


def odd_spec():
    fm = []
    for h in range(8):
        fm.append((h * 128, False, ("q", h)))
    for h in range(8):
        fm.append((3080 + h * 128, True, ("q", 8 + h)))
    for h in range(8):
        fm.append((1024 + h * 128, False, ("k", h)))
    for g in range(2):
        fm.append((4616 + g * 128, True, ("k", 8 + g)))
    for g in range(2):
        fm.append((5128 + g * 128, True, ("k", 10 + g)))
    for g in range(2):
        fm.append((4104 + g * 128, True, ("cmp", g)))
    for g in range(2):
        fm.append((4360 + g * 128, False, ("cmp", 2 + g)))
    tm = []
    for h in range(8):
        tm.append((2048 + h * 128, 12 + h))
    for g in range(2):
        tm.append((4872 + g * 128, 20 + g))
    for g in range(2):
        tm.append((5384 + g * 128, 22 + g))
    return dict(fm=fm, tm=tm, nitems=24, nk=12, small=(3072, 5640))


def host_w_layout(w_in, spec):
    fm = np.concatenate([w_in[:, o:o + 128] for (o, _, _) in spec["fm"]], axis=1)
    tm = np.concatenate([w_in[:, o:o + 128] for (o, _) in spec["tm"]], axis=1)
    return np.ascontiguousarray(fm), np.ascontiguousarray(tm)


class H:
    def __init__(self, ap):
        self._ap = ap

    def ap(self):
        return self._ap

    def __getitem__(self, k):
        return self._ap[k]


class Builder:
    def __init__(self, nlayers=DEPTH, stop_after_mix=False, final_norm=True, phase=None):
        self.phase = phase
        self.nlayers = nlayers
        self.stop_after_mix = stop_after_mix
        self.final_norm = final_norm
        self.nc = bass.Bass("TRN2", target_bir_lowering=False)
        self.S = Sched()
        self.psrr = 0
        self.uid = 0
        self.slot_rr = 0
        self.pt_rr = 0
        self.tq_rr = 0
        self._ins = {}
        self.out_ops = []
        self.pending = None

    def dram_in(self, name, shape, dt=F32):
        ap = self.nc.dram_tensor(name, list(shape), dt, kind="ExternalInput").ap()
        self._ins[name] = ap
        return ap

    def sb(self, name, shape, dt):
        return self.es.enter_context(self.nc.sbuf_tensor(name, list(shape), dt))

    def next_ps(self, lo=0, hi=7):
        b = lo + (self.psrr % (hi - lo))
        self.psrr += 1
        return b

    def build(self):
        nc = self.nc
        S = self.S
        NL = self.nlayers
        with contextlib.ExitStack() as es:
            self.es = es
            ph = self.phase
            if ph is None:
                need_A = set(range(NL))
                need_B = set(range(NL))
                mlp_layers = set(l for l in range(NL) if not self.stop_after_mix or l < NL - 1)
            elif ph == "M":
                need_A, need_B, mlp_layers = set(), set(), set()
            else:
                need_A = {ph} if ph < DEPTH else set()
                need_B = {ph - 1} if ph >= 1 else set()
                mlp_layers = set(need_B)
            self.need_A, self.need_B = need_A, need_B
            if ph in (None, 0):
                self.dram_in("xT", [16, 128, TL])
            elif ph != "M":
                self.dram_in("xstate_in", [16, 128, TL])
                self.dram_in("qstate_in", [128, 16 * TL], BF16)
                self.dram_in("gstate_in", [128, 192], BF16)
                self.dram_in("pstate_in", [128, 2])
            if ph is None or ph == DEPTH:
                self._outT = nc.dram_tensor("outT", [16, 128, TL], F32, kind="ExternalOutput").ap()
                S.ext_keys.add("outT")
            elif ph != "M":
                self._xso = nc.dram_tensor("xstate_out", [16, 128, TL], F32, kind="ExternalOutput").ap()
                self._qso = nc.dram_tensor("qstate_out", [128, 16 * TL], BF16, kind="ExternalOutput").ap()
                self._gso = nc.dram_tensor("gstate_out", [128, 192], BF16, kind="ExternalOutput").ap()
                self._pso = nc.dram_tensor("pstate_out", [128, 2], F32, kind="ExternalOutput").ap()
                S.ext_keys.update(["xso", "qso", "gso", "pso"])
            self.dram_in("ropecos", [128, TL])
            self.dram_in("ropesin", [128, TL])
            self.dram_in("cmask", [128, 8 * 128], BF16)
            self.dram_in("constf", [128, 4 * 128])
            self.dram_in("constb", [128, 2 * 128], BF16)
            self.dram_in("cT", [128, 16])
            if ph in (None, "M"):
                self.dram_in("adaw", [DEPTH, D, 1536])
            self.dram_in("adabT", [128, 48])
            self.dram_in("gvec", [128, 9 * 16])
            W = {}
            for l in sorted(need_A | need_B):
                W[l] = {}
                if l in need_A:
                    if l % 2 == 0:
                        W[l].update(fm=self.dram_in(f"wfm{l}", [D, 26 * 128]), tm=self.dram_in(f"wtm{l}", [D, 10 * 128]))
                    else:
                        W[l].update(fm=self.dram_in(f"wfm{l}", [D, 32 * 128]), tm=self.dram_in(f"wtm{l}", [D, 12 * 128]),
                                    sm=self.dram_in(f"wsm{l}", [D, 32]))
                if l in need_B:
                    W[l]["out"] = self.dram_in(f"wout{l}", [D, D])
                    if l in mlp_layers:
                        W[l]["up"] = self.dram_in(f"wup{l}", [D, DFF])
                        W[l]["down"] = self.dram_in(f"wdown{l}", [DFF, D])
            self.W = W
            ev = dict(
                wm=self.dram_in("wm", [32, 4096], BF16),
                swamask=self.dram_in("swamask", [128, 9 * 128], BF16),
                gsel=self.dram_in("gsel", [128, 3 * 256]),
                sinks=self.dram_in("sinksb", [128, 2 * 8]),
            )
            self.ev = ev
            any_odd = any(l % 2 == 1 for l in (need_A | need_B))
            od = dict(
                wh=self.dram_in("wh", [128, 4096], BF16),
                winmask=self.dram_in("winmask", [128, 12 * 128], BF16),
                cmpmask=self.dram_in("cmpmask", [128, 3 * 128], BF16),
                ovl=self.dram_in("ovl", [128, 4 * 128], BF16),
                selbase=self.dram_in("selbase", [128, 240]),
                onehot=self.dram_in("onehot", [128, 8]),
                fbias=self.dram_in("fbias", [128, 16]),
                posT=self.dram_in("posT", [128, 4 * 32]),
                w1=[[self.dram_in(f"w1_{j2}_{t}", [4096, 128]) for t in range(2)] if (2 * j2 + 1) in need_A else None for j2 in range(2)],
                w2=[self.dram_in(f"w2_{j2}", [128, 256]) if (2 * j2 + 1) in need_B else None for j2 in range(2)],
            ) if any_odd else None
            self.od = od
            self.send, self.gath, self.ssend, self.sgath = {}, {}, {}, {}
            if ph is None:
                self.modsend = nc.dram_tensor("modsend", [128, 48], F32)
                self.modgath = nc.dram_tensor("modgath", [NCORE * 128, 48], F32)
            elif ph == "M":
                self.modsend = H(nc.dram_tensor("modsend_out", [128, 48], F32, kind="ExternalOutput").ap())
                S.ext_keys.add("modsend")
            else:
                self.modgath = H(self.dram_in("modgath_in", [NCORE * 128, 48]))
            for l in sorted(need_A | need_B):
                ni = 20 if l % 2 == 0 else 24
                nsm = 64 if l % 2 == 0 else 576
                if ph is None:
                    self.send[l] = nc.dram_tensor(f"send{l}", [ni * 128, TL], BF16)
                    self.gath[l] = nc.dram_tensor(f"gath{l}", [NCORE * ni * 128, TL], BF16)
                    self.ssend[l] = nc.dram_tensor(f"ssend{l}", [128, nsm], F32)
                    self.sgath[l] = nc.dram_tensor(f"sgath{l}", [NCORE * 128, nsm], F32)
                else:
                    if l in need_A:
                        self.send[l] = H(nc.dram_tensor(f"send{l}_out", [ni * 128, TL], BF16, kind="ExternalOutput").ap())
                        self.ssend[l] = H(nc.dram_tensor(f"ssend{l}_out", [128, nsm], F32, kind="ExternalOutput").ap())
                        S.ext_keys.update([("send", l), ("ssend", l)])
                    if l in need_B:
                        self.gath[l] = H(self.dram_in(f"gath{l}_in", [NCORE * ni * 128, TL], BF16))
                        self.sgath[l] = H(self.dram_in(f"sgath{l}_in", [NCORE * 128, nsm]))

            self.xT = self.sb("xT_sb", [128, 16, TL], F32)
            self.A = self.sb("A_sb", [128, 16 * TL], BF16)
            self.QT = self.sb("QT_sb", [128, 16, TL], BF16)
            self.SL = [self.sb(f"slot{i}", [128, 4096], BF16) for i in range(3)]
            self.cosT = self.sb("cosT", [128, TL], F32)
            self.sinT = self.sb("sinT", [128, TL], F32)
            self.cmask = self.sb("cmask_sb", [128, 8, 128], BF16)
            self.cf = self.sb("constf_sb", [128, 4, 128], F32)
            self.cb = self.sb("constb_sb", [128, 2, 128], BF16)
            self.XF = self.sb("XF", [128, 2560], F32)
            self.vec = self.sb("vec", [128, 1024], F32)
            self.tq = [self.sb(f"tq{i}", [128, 512], F32) for i in range(3)]
            self.PT = [self.sb(f"PT{i}", [128, 512], BF16) for i in range(3)]
            self.kst = [self.sb(f"kst{i}", [128, TL], BF16) for i in range(2)]
            self.vst = [self.sb(f"vst{i}", [128, 2, 8, 128], BF16) for i in range(2)]
            self.selb = self.sb("selb", [128, 256], BF16)
            self.gsb = self.sb("gsb", [128, 8, 24], BF16)
            self.w2sb = self.sb("w2sb", [128, 2, 128], BF16)
            self.posT = self.sb("posT_sb", [128, 2, 32], BF16)
            self.PC = self.sb("PC", [128, 4, 512], BF16)
            self.PS = [es.enter_context(nc.psum_tensor(f"ps{i}", [128, 512], F32)) for i in range(7)]
            self.PSB = es.enter_context(nc.psum_tensor("psb", [128, 1024], BF16))
            self.identf = self.cf[:, 0, :]
            self.Rm = self.cf[:, 1, :]
            self.onesf = self.cf[:, 2, :]
            self.tri = self.cf[:, 3, :]
            self.identb = self.cb[:, 0, :]
            self.onesb = self.cb[:, 1, :]
            self.V_COND = 0
            self.V_MOD = 16
            self.V_G = 400
            self.V_ADAB = 544
            self.V_DER = 592
            self.V_SINK = 700
            self.V_POSB = 720

            if ph is None:
                self.load_consts("xT")
                self.compute_mods_local()
                self.mods_gather()
                for l in range(NL):
                    self.layer_A(l)
                    self.exchange(l)
                    self.layer_B(l)
                if self.final_norm:
                    self.final()
                else:
                    self.store_x()
            elif ph == "M":
                self.load_consts(None)
                self.compute_mods_local()
            else:
                self.load_consts("xT" if ph == 0 else "xstate_in")
                self.mods_load()
                if ph >= 1:
                    self.restore_state()
                    self.derive(ph - 1)
                    self.layer_B(ph - 1)
                if ph < DEPTH:
                    self.layer_A(ph)
                    self.save_state()
                else:
                    self.final()

            sems = {}
            for e in ENGS:
                sems["e_" + e] = es.enter_context(nc.semaphore("e_" + e))
            for nme in S.dsem_names():
                sems[nme] = es.enter_context(nc.semaphore(nme))
            block = es.enter_context(nc.Block())
            S.emit(block, sems, final_wait_ops=sorted(set(self.out_ops) | set(S.ext_ops)))
        return nc

    def load_consts(self, xname):
        S = self.S
        if xname is not None:
            xT_d = self.nc_in(xname)
            S.dma("sync", lambda e: e.dma_start(out=self.xT[:], in_=xT_d.rearrange("c p t -> p c t")), "d_x",
                  writes=[("x", c, h) for c in range(16) for h in range(2)])
        for (nm, dst, key) in [("ropecos", self.cosT, "cos"), ("ropesin", self.sinT, "sin"), ("cT", self.vec[:, 0:16], "v_cond"),
                               ("adabT", self.vec[:, self.V_ADAB:self.V_ADAB + 48], "v_adab"),
                               ("gvec", self.vec[:, self.V_G:self.V_G + 144], "v_g")]:
            src = self.nc_in(nm)
            d = dst if nm not in ("ropecos", "ropesin") else dst[:]
            S.dma("sync", (lambda d, src: (lambda e: e.dma_start(out=d, in_=src)))(d, src), "d_c_" + key, writes=[key])
        S.dma("sync", lambda e: e.dma_start(out=self.cmask[:], in_=self.nc_in("cmask").rearrange("p (a b) -> p a b", a=8)), "d_c_cmask", writes=["cmask"])
        S.dma("sync", lambda e: e.dma_start(out=self.cf[:], in_=self.nc_in("constf").rearrange("p (a b) -> p a b", a=4)), "d_c_cf", writes=["cf"])
        S.dma("sync", lambda e: e.dma_start(out=self.cb[:], in_=self.nc_in("constb").rearrange("p (a b) -> p a b", a=2)), "d_c_cb", writes=["cb"])

    def nc_in(self, name):
        return self._ins[name]

    def compute_mods_local(self):
        S = self.S
        vec = self.vec
        cond = vec[:, 0:16]
        S.act(lambda e: e.activation(out=cond, in_=cond, func=AF.Silu), reads=["v_cond"], writes=["v_cond"])
        adaw = self.nc_in("adaw")
        modrow = self.tq[0]
        for l in range(DEPTH):
            banks = [0, 1, 2]
            for kc in range(16):
                for nb in range(3):
                    st = self.tq[nb]
                    src = adaw[l, kc * 128:(kc + 1) * 128, nb * 512:(nb + 1) * 512]
                    S.dma("sync", (lambda st, src: (lambda e: e.dma_start(out=st[:], in_=src)))(st, src), f"d_tq{nb}",
                          writes=[("tq", nb)])
                    S.pe((lambda st, kc, nb: (lambda e: e.matmul(self.PS[nb][0:1, :], lhsT=cond[:, kc:kc + 1], rhs=st[:],
                                                                  start=(kc == 0), stop=(kc == 15))))(st, kc, nb),
                         reads=[("tq", nb), "v_cond"], writes=[("ps", nb)])
            for nb in range(3):
                S.act((lambda nb: (lambda e: e.copy(out=self.XF[0:1, nb * 512:(nb + 1) * 512], in_=self.PS[nb][0:1, :])))(nb),
                      reads=[("ps", nb)], writes=[("xfrow", nb)])
            for k in range(12):
                nb = k // 4
                S.pe((lambda k: (lambda e: e.matmul(self.PS[3][:, k:k + 1], lhsT=self.XF[0:1, k * 128:(k + 1) * 128],
                                                    rhs=self.onesf[0:1, 0:1], start=True, stop=True)))(k),
                     reads=[("xfrow", nb), "cf"], writes=[("ps", 3)])
            S.dve((lambda l: (lambda e: e.tensor_tensor(out=self.XF[:, 2432 + l * 12:2432 + (l + 1) * 12], in0=self.PS[3][:, 0:12],
                                                        in1=vec[:, self.V_ADAB + l * 12:self.V_ADAB + (l + 1) * 12], op=ALU.add)))(l),
                  reads=[("ps", 3), "v_adab"], writes=["modloc"])
        S.dma("sync", lambda e: e.dma_start(out=self.modsend[:, :], in_=self.XF[:, 2432:2480]), "d_modsend", reads=["modloc"], writes=["modsend"])

    def mods_gather(self):
        S = self.S
        S.cc(lambda e: e.collective_compute("AllGather", ALU.bypass, replica_groups=[list(range(NCORE))],
                                            ins=[self.modsend.ap().opt()], outs=[self.modgath.ap().opt()]),
             "cc_mod", reads=["modsend"], writes=["modgath"])
        self.mods_load()

    def mods_load(self):
        S = self.S
        vec = self.vec
        S.dma("sync", lambda e: e.dma_start(out=vec[:, self.V_MOD:self.V_MOD + 384].rearrange("p (r f) -> p r f", r=8),
                                            in_=self.modgath.ap().rearrange("(r p) f -> p r f", p=128)),
              "d_modload", reads=["modgath"], writes=["v_mod"])

    def modcol(self, l, part, ch):
        gch = part * 16 + ch
        r, k = gch // 12, gch % 12
        o = self.V_MOD + r * 48 + l * 12 + k
        return self.vec[:, o:o + 1]

    def derive(self, l):
        S = self.S
        vec = self.vec
        for which, part, goff in ((0, 1, l * 16), (1, 4, 64 + l * 16)):
            for ch in range(16):
                dst = vec[:, self.V_DER + which * 16 + ch:self.V_DER + which * 16 + ch + 1]
                S.dve((lambda dst, part, ch, goff: (lambda e: e.scalar_tensor_tensor(
                    out=dst, in0=self.modcol(l, part, ch), scalar=1.0, in1=vec[:, self.V_G + goff + ch:self.V_G + goff + ch + 1],
                    op0=ALU.add, op1=ALU.mult)))(dst, part, ch, goff),
                    reads=["v_mod", "v_g"], writes=[("v_der", which)])

    def norm_to_A(self, gmul_of, gadd_of, derkey):
        S = self.S
        A3 = self.A[:].rearrange("p (c t) -> p c t", c=16)
        for half in range(2):
            cs = slice(half * 512, (half + 1) * 512)
            bank = 4 + half
            for ch in range(16):
                sq = self.tq[ch % 2]
                S.act((lambda sq, ch, cs: (lambda e: e.activation(out=sq[:], in_=self.xT[:, ch, cs], func=AF.Square)))(sq, ch, cs),
                      reads=[("x", ch, half)], writes=[("tq", ch % 2)])
                S.pe((lambda sq, ch, bank: (lambda e: e.matmul(self.PS[bank][:], lhsT=self.onesf, rhs=sq[:], start=(ch == 0), stop=(ch == 15))))(sq, ch, bank),
                     reads=[("tq", ch % 2), "cf"], writes=[("ps", bank)])
            rs = self.XF[:, 1024 + half * 512:1024 + (half + 1) * 512]
            S.act((lambda bank, rs: (lambda e: e.activation(out=rs, in_=self.PS[bank][:], func=AF.Sqrt, scale=1.0 / D, bias=EPS)))(bank, rs),
                  reads=[("ps", bank)], writes=[("rstd", half)])
            S.dve((lambda rs: (lambda e: e.reciprocal(out=rs, in_=rs)))(rs), reads=[("rstd", half)], writes=[("rstd", half)])
            for ch in range(16):
                t = self.tq[2]
                S.dve((lambda t, ch, cs, rs: (lambda e: e.scalar_tensor_tensor(out=t[:], in0=self.xT[:, ch, cs], scalar=gmul_of(ch), in1=rs,
                                                                                op0=ALU.mult, op1=ALU.mult)))(t, ch, cs, rs),
                      reads=[("x", ch, half), ("rstd", half), derkey, "v_mod"], writes=[("tq", 2)])
                S.act((lambda t, ch, cs: (lambda e: e.activation(out=A3[:, ch, cs], in_=t[:], func=AF.Identity, bias=gadd_of(ch), scale=1.0)))(t, ch, cs),
                      reads=[("tq", 2), "v_mod"], writes=[("A", ch, half)])

    def wslot_load(self, wd, row0, col0, ncols, key):
        S = self.S
        s = self.slot_rr % 3
        self.slot_rr += 1
        slot = self.SL[s]
        src = wd[row0:row0 + 2048, col0:col0 + ncols].rearrange("(k p) n -> p k n", p=128)
        dst = slot[:, 0:16 * ncols].rearrange("p (k n) -> p k n", k=16)
        S.dma("gpsimd", lambda e: e.dma_start(out=dst, in_=src), f"d_slot{s}", writes=[("slot", s), ("slotv", s), ("slotw", s), ("slotx", s)])
        return s, dst

    def projections(self, l, spec):
        S = self.S
        W = self.W[l]
        A3 = self.A[:].rearrange("p (c t) -> p c t", c=16)
        nfm = len(spec["fm"])
        send = self.send[l]
        self.send_ops = []
        for g in range(nfm // 2):
            s, wv = self.wslot_load(W["fm"], 0, g * 256, 256, None)
            for ci in range(2):
                (_, rope, dest) = spec["fm"][g * 2 + ci]
                if dest[0] == "q":
                    dst_full = self.QT[:, dest[1], :]
                    dkeys = [("Q", dest[1], 0), ("Q", dest[1], 1)]
                elif dest[0] == "k":
                    kb = dest[1] % 2
                    dst_full = self.kst[kb][:]
                    dkeys = [("kst", kb), ("kst", kb)]
                else:
                    kb = dest[1] % 2
                    dst_full = self.kst[kb][:]
                    dkeys = [("kst", kb), ("kst", kb)]
                for half in range(2):
                    cs = slice(half * 512, (half + 1) * 512)
                    b = self.next_ps(0, 4)
                    for kc in range(16):
                        S.pe((lambda b, wv, kc, ci, cs: (lambda e: e.matmul(self.PS[b][:], lhsT=wv[:, kc, ci * 128:(ci + 1) * 128], rhs=A3[:, kc, cs],
                                                                            start=(kc == 0), stop=(kc == 15))))(b, wv, kc, ci, cs),
                             reads=[("slot", s), ("A", kc, half)], writes=[("ps", b)])
                    dst = dst_full[:, cs]
                    if not rope:
                        S.act((lambda b, dst: (lambda e: e.copy(out=dst, in_=self.PS[b][:])))(b, dst), reads=[("ps", b)], writes=[dkeys[half]])
                    else:
                        q32, t1, t2 = self.tq
                        rb = 4 + (self.uid % 2)
                        self.uid += 1
                        S.act((lambda b: (lambda e: e.copy(out=q32[:], in_=self.PS[b][:])))(b), reads=[("ps", b)], writes=[("tq", 0)])
                        S.pe((lambda rb: (lambda e: e.matmul(self.PS[rb][:], lhsT=self.Rm, rhs=q32[:], start=True, stop=True)))(rb),
                             reads=[("tq", 0), "cf"], writes=[("ps", rb)])
                        S.dve((lambda cs: (lambda e: e.tensor_tensor(out=t1[:], in0=q32[:], in1=self.cosT[:, cs], op=ALU.mult)))(cs),
                              reads=[("tq", 0), "cos"], writes=[("tq", 1)])
                        S.dve((lambda rb, cs: (lambda e: e.tensor_tensor(out=t2[:], in0=self.PS[rb][:], in1=self.sinT[:, cs], op=ALU.mult)))(rb, cs),
                              reads=[("ps", rb), "sin"], writes=[("tq", 2)])
                        S.dve((lambda dst: (lambda e: e.tensor_tensor(out=dst, in0=t1[:], in1=t2[:], op=ALU.add)))(dst),
                              reads=[("tq", 1), ("tq", 2)], writes=[dkeys[half]])
                if dest[0] == "k":
                    item = dest[1]
                    kb = item % 2
                    if l % 2 == 0 and item < 8:
                        S.dve((lambda kb, item: (lambda e: e.tensor_reduce(out=self.XF[:, 2368 + item * 8:2368 + item * 8 + 8],
                                                                            in_=self.kst[kb][:].rearrange("p (j t) -> p j t", j=8), axis=AX.X, op=ALU.add)))(kb, item),
                              reads=[("kst", kb)], writes=["ksumT"])
                    o = S.dma("sync", (lambda kb, item: (lambda e: e.dma_start(out=send[item * 128:(item + 1) * 128, :], in_=self.kst[kb][:])))(kb, item),
                              f"d_kst{kb}", reads=[("kst", kb)], writes=[("send", l)])
                elif dest[0] == "cmp":
                    self.cmp_partials(l, dest[1])
        if l % 2 == 1:
            self.odd_small(l)
        ntm = len(spec["tm"])
        for g in range(ntm // 2):
            s, wv = self.wslot_load(W["tm"], 0, g * 256, 256, None)
            vb = g % 2
            for j in range(8):
                b = self.next_ps(0, 4)
                for kc in range(16):
                    S.pe((lambda b, wv, kc, j: (lambda e: e.matmul(self.PS[b][:, 0:256], lhsT=A3[:, kc, j * 128:(j + 1) * 128], rhs=wv[:, kc, :],
                                                                   start=(kc == 0), stop=(kc == 15))))(b, wv, kc, j),
                         reads=[("slot", s), ("A", kc, j // 4)], writes=[("ps", b)])
                S.act((lambda b, vb, j: (lambda e: e.copy(out=self.vst[vb][:, :, j, :], in_=self.PS[b][:, 0:256].rearrange("p (h d) -> p h d", h=2))))(b, vb, j),
                      reads=[("ps", b)], writes=[("vst", vb)])
            for ci in range(2):
                item = spec["tm"][g * 2 + ci][1]
                S.dma("sync", (lambda vb, ci, item: (lambda e: e.dma_start(out=send[item * 128:(item + 1) * 128, :],
                                                                           in_=self.vst[vb][:, ci, :, :].rearrange("p j d -> p (j d)"))))(vb, ci, item),
                      f"d_vst{vb}", reads=[("vst", vb)], writes=[("send", l)])

    def exchange(self, l):
        S = self.S
        send, gath = self.send[l], self.gath[l]
        S.cc(lambda e: e.collective_compute("AllGather", ALU.bypass, replica_groups=[list(range(NCORE))],
                                            ins=[send.ap().opt()], outs=[gath.ap().opt()]),
             f"cc_big{l}", reads=[("send", l)], writes=[("gath", l)])
        ss, sg = self.ssend[l], self.sgath[l]
        S.cc(lambda e: e.collective_compute("AllGather", ALU.bypass, replica_groups=[list(range(NCORE))],
                                            ins=[ss.ap().opt()], outs=[sg.ap().opt()]),
             f"cc_small{l}", reads=[("ssend", l)], writes=[("sgath", l)])

    def kv_load(self, l, kitem, vitem, cp, ni):
        S = self.S
        s = self.slot_rr % 3
        self.slot_rr += 1
        slot = self.SL[s]
        g3 = self.gath[l].ap().rearrange("(c r) t -> r c t", c=NCORE)
        ksrc = g3[kitem * 128:(kitem + 1) * 128, 2 * cp:2 * cp + 2, :]
        vsrc = g3[vitem * 128:(vitem + 1) * 128, 2 * cp:2 * cp + 2, :]
        kd = slot[:, 0:2048].rearrange("p (c t) -> p c t", c=2)
        vd = slot[:, 2048:4096].rearrange("p (c t) -> p c t", c=2)
        S.dma("sync", lambda e: e.dma_start(out=kd, in_=ksrc), f"d_slot{s}", reads=[("gath", l)], writes=[("slot", s), ("slotw", s)])
        S.dma("sync", lambda e: e.dma_start(out=vd, in_=vsrc), f"d_slotv{s}", reads=[("gath", l)], writes=[("slotv", s), ("slotx", s)])
        return s, kd, slot[:, 2048:4096].rearrange("p (c j d) -> p c j d", c=2, j=8)

    def attn_unit(self, sbank, kT, vT, q_ap, n, extra, exp_bias, obank, dbank, ocols, first, last, skeys, pkeys):
        S = self.S
        pt = self.PT[self.pt_rr % 3]
        ptk = ("PT", self.pt_rr % 3)
        self.pt_rr += 1
        nex = len(extra)
        S.pe(lambda e: e.matmul(self.PS[sbank][:, 0:n], lhsT=kT, rhs=q_ap, start=True, stop=(nex == 0)),
             reads=skeys, writes=[("ps", sbank)])
        for xi, (lhs, rhs, c0, c1, xkeys) in enumerate(extra):
            S.pe((lambda lhs, rhs, c0, c1, xi: (lambda e: e.matmul(self.PS[sbank][:, c0:c1], lhsT=lhs, rhs=rhs, start=False, stop=(xi == nex - 1))))(lhs, rhs, c0, c1, xi),
                 reads=xkeys, writes=[("ps", sbank)])
        pkl = [ptk] + [ptk + (c,) for c in (0, 128, 256, 384)]
        if exp_bias is None:
            S.act(lambda e: e.activation(out=pt[:, 0:n], in_=self.PS[sbank][:, 0:n], func=AF.Exp, scale=SCALE),
                  reads=[("ps", sbank)], writes=[ptk] + [ptk + (c,) for c in (0, 128, 256, 384)])
        else:
            pkl = []
            for (c0, c1, bap, bkeys) in exp_bias:
                S.act((lambda c0, c1, bap: (lambda e: e.activation(out=pt[:, c0:c1], in_=self.PS[sbank][:, c0:c1], func=AF.Exp, scale=SCALE, bias=bap)))(c0, c1, bap),
                      reads=[("ps", sbank)] + bkeys, writes=[ptk + (c0,)])
                pkl.append(ptk + (c0,))
        def pv():
            S.pe(lambda e: e.matmul(self.PS[obank][:, ocols], lhsT=vT, rhs=pt[:, 0:n], start=first, stop=last, skip_group_check=True),
                 reads=pkl + pkeys, writes=[("ps", obank)])
            S.pe(lambda e: e.matmul(self.PS[dbank][:, ocols], lhsT=self.onesb, rhs=pt[:, 0:n], start=first, stop=last, skip_group_check=True),
                 reads=pkl + ["cb"], writes=[("ps", dbank)])
        prev = self.pending
        self.pending = pv
        if prev is not None:
            prev()

    def attn_flush(self):
        if self.pending is not None:
            p = self.pending
            self.pending = None
            p()

    def finalize_head(self, obank, dbank, dst, dkeys, ncols=512, sink_cols=None):
        S = self.S
        self.attn_flush()
        rd = self.tq[self.tq_rr % 3]
        rk = ("tq", self.tq_rr % 3)
        self.tq_rr += 1
        if sink_cols is None:
            S.dve(lambda e: e.tensor_scalar(out=rd[:, 0:ncols], in0=self.PS[dbank][:, 0:ncols], scalar1=1e-30, scalar2=None, op0=ALU.max),
                  reads=[("ps", dbank)], writes=[rk])
        else:
            for hh, sc in enumerate(sink_cols):
                S.dve((lambda hh, sc: (lambda e: e.tensor_scalar(out=rd[:, hh * 128:(hh + 1) * 128], in0=self.PS[dbank][:, hh * 128:(hh + 1) * 128],
                                                                  scalar1=sc, scalar2=None, op0=ALU.add)))(hh, sc),
                      reads=[("ps", dbank), "v_sink"], writes=[rk])
        S.dve(lambda e: e.reciprocal(out=rd[:, 0:ncols], in_=rd[:, 0:ncols]), reads=[rk], writes=[rk])
        S.dve(lambda e: e.tensor_tensor(out=dst, in0=self.PS[obank][:, 0:ncols] if len(dst.shape) == 2 else self.PS[obank][:, 0:ncols].rearrange("p (h t) -> p h t", h=4),
                                        in1=rd[:, 0:ncols] if len(dst.shape) == 2 else rd[:, 0:ncols].rearrange("p (h t) -> p h t", h=4), op=ALU.mult),
              reads=[("ps", obank), rk], writes=dkeys)

    def even_attention(self, l):
        S = self.S
        A = self.A
        XF = self.XF
        j2 = l // 2
        ev = self.ev
        wm = A[0:32, 0:4096]
        swam = A[:, 4096:4096 + 1152].rearrange("p (s q) -> p s q", s=9)
        selbT = [A[0:32, 5248:6272], A[0:32, 6272:7296]]
        kmT = A[:, 7296:7552]
        akeys = [("A", c, h) for c in range(16) for h in range(2)]
        S.dma("sync", lambda e: e.dma_start(out=wm, in_=ev["wm"]), "d_ac0", writes=akeys + ["wm"])
        S.dma("sync", lambda e: e.dma_start(out=A[:, 4096:4096 + 1152], in_=ev["swamask"]), "d_ac1", writes=["swam"])
        S.dma("sync", lambda e: e.dma_start(out=XF[:, 0:768], in_=ev["gsel"]), "d_ac2", writes=["gsel"])
        S.dma("sync", lambda e: e.dma_start(out=self.vec[:, self.V_SINK:self.V_SINK + 8], in_=ev["sinks"][:, j2 * 8:(j2 + 1) * 8]), "d_ac3", writes=["v_sink"])
        S.act(lambda e: e.activation(out=self.vec[:, self.V_SINK:self.V_SINK + 8], in_=self.vec[:, self.V_SINK:self.V_SINK + 8], func=AF.Exp),
              reads=["v_sink"], writes=["v_sink"])
        kg = self.tq[0][:]
        S.dma("sync", lambda e: e.dma_start(out=kg.rearrange("p (c f) -> p c f", c=8), in_=self.sgath[l].ap().rearrange("(c p) f -> p c f", p=128)),
              "d_ac4", reads=[("sgath", l)], writes=[("tq", 0)])
        kg5 = kg.rearrange("p (c2 par h j) -> p c2 par h j", c2=4, par=2, h=8)
        S.dve(lambda e: e.tensor_tensor(out=kmT.rearrange("p (h j c2) -> p h j c2", h=8, j=8),
                                        in0=kg5[:, :, 0, :, :].transpose([0, 2, 3, 1]), in1=kg5[:, :, 1, :, :].transpose([0, 2, 3, 1]), op=ALU.add),
              reads=[("tq", 0)], writes=["kmT"])
        gbias, past01, own01 = XF[:, 0:256], XF[:, 256:512], XF[:, 512:768]
        gm, sel, m8 = XF[:, 768:1024], XF[:, 2048:2304], XF[:, 2304:2368]
        ni = 20
        for h in range(8):
            sb_ = selbT[h % 2]
            sbk = ("selbT", h % 2)
            for j in range(8):
                S.pe((lambda h, j: (lambda e: e.matmul(self.PS[6][:, j * 32:(j + 1) * 32], lhsT=self.QT[:, h, j * 128:(j + 1) * 128],
                                                       rhs=kmT[:, h * 32:(h + 1) * 32], start=True, stop=True)))(h, j),
                     reads=[("Q", h, j // 4), "kmT"], writes=[("ps", 6)])
            S.dve(lambda e: e.tensor_tensor(out=gm, in0=self.PS[6][:, 0:256], in1=gbias, op=ALU.add), reads=[("ps", 6), "gsel"], writes=["gm"])
            for j in range(8):
                S.dve((lambda j: (lambda e: e.max(out=m8[:, j * 8:(j + 1) * 8], in_=gm[:, j * 32:(j + 1) * 32])))(j), reads=["gm"], writes=["m8"])
                S.dve((lambda j: (lambda e: e.tensor_scalar(out=sel[:, j * 32:(j + 1) * 32], in0=gm[:, j * 32:(j + 1) * 32],
                                                            scalar1=m8[:, j * 8 + 2:j * 8 + 3], scalar2=None, op0=ALU.is_ge)))(j),
                      reads=["gm", "m8"], writes=["sel"])
            S.dve(lambda e: e.tensor_tensor(out=sel, in0=sel, in1=past01, op=ALU.mult), reads=["sel", "gsel"], writes=["sel"])
            S.dve(lambda e: e.tensor_tensor(out=sel, in0=sel, in1=own01, op=ALU.add), reads=["sel", "gsel"], writes=["sel"])
            S.dve(lambda e: e.tensor_scalar(out=self.selb[:], in0=sel, scalar1=-1.0, scalar2=-NEG, op0=ALU.add, op1=ALU.mult), reads=["sel"], writes=["selb"])
            for j in range(8):
                S.pe((lambda j: (lambda e: e.transpose(self.PSB[0:32, j * 128:(j + 1) * 128], self.selb[:, j * 32:(j + 1) * 32], self.identb)))(j),
                     reads=["selb", "cb"], writes=["psb"])
            S.act((lambda sb_: (lambda e: e.copy(out=sb_, in_=self.PSB[0:32, :])))(sb_), reads=["psb"], writes=[sbk])
            first = True
            for cp in range(4):
                s, kd, vd = self.kv_load(l, h, 10 + h, cp, ni)
                for cc in range(2):
                    ck = 2 * cp + cc
                    for jk in range(8):
                        b = (8 * jk + ck) // 2
                        groups = []
                        if jk < 4:
                            groups.append((0, jk * 128, 512))
                        groups.append((1, max(512, jk * 128), 1024))
                        for (grp, c0, c1) in groups:
                            n = c1 - c0
                            sbank = self.next_ps(0, 3)
                            extra = []
                            if c0 == jk * 128:
                                extra.append((self.identb, self.cmask[:, ck, :], 0, 128, ["cb", "cmask"]))
                            extra.append((wm[:, b * 128:(b + 1) * 128], sb_[:, c0:c1], 0, n, ["wm", sbk]))
                            last = (ck == 7 and jk == (3 if grp == 0 else 7))
                            fst = (ck == 0 and jk == 0)
                            hq = [("Q", h, grp)]
                            self.attn_unit(sbank, kd[:, cc, jk * 128:(jk + 1) * 128], vd[:, cc, jk, :], self.QT[:, h, c0:c1], n, extra, None,
                                           3 + grp, 5 + grp if False else (5 if grp == 0 else 6), slice(c0 - grp * 512, c1 - grp * 512), fst, last,
                                           [("slot", s)] + hq, [("slotv", s)])
            for grp in range(2):
                self.finalize_head(3 + grp, 5 if grp == 0 else 6, self.QT[:, h, grp * 512:(grp + 1) * 512], [("Q", h, grp)])
        g3 = self.gath[l].ap().rearrange("(c r) t -> r c t", c=NCORE)
        for g in range(2):
            kitem, vitem = 8 + g, 18 + g
            for j in range(8):
                s = self.slot_rr % 3
                self.slot_rr += 1
                slot = self.SL[s]
                kw = slot[:, 0:1152].rearrange("p (s t) -> p s t", s=9)
                vw = slot[:, 2048:2048 + 1152].rearrange("p (s t) -> p s t", s=9)
                rows_k = slice(kitem * 128, (kitem + 1) * 128)
                rows_v = slice(vitem * 128, (vitem + 1) * 128)
                S.dma("sync", (lambda kw, j, rows_k: (lambda e: e.dma_start(out=kw[:, 1:9, :], in_=g3[rows_k, :, j * 128:(j + 1) * 128])))(kw, j, rows_k),
                      f"d_slot{s}", reads=[("gath", l)], writes=[("slot", s)])
                S.dma("sync", (lambda vw, j, rows_v: (lambda e: e.dma_start(out=vw[:, 1:9, :], in_=g3[rows_v, :, j * 128:(j + 1) * 128])))(vw, j, rows_v),
                      f"d_slotv{s}", reads=[("gath", l)], writes=[("slotv", s)])
                if j > 0:
                    S.dma("sync", (lambda kw, j, rows_k: (lambda e: e.dma_start(out=kw[:, 0, :], in_=g3[rows_k, 7, (j - 1) * 128:j * 128])))(kw, j, rows_k),
                          f"d_slotw{s}", reads=[("gath", l)], writes=[("slotw", s)])
                    S.dma("sync", (lambda vw, j, rows_v: (lambda e: e.dma_start(out=vw[:, 0, :], in_=g3[rows_v, 7, (j - 1) * 128:j * 128])))(vw, j, rows_v),
                          f"d_slotx{s}", reads=[("gath", l)], writes=[("slotx", s)])
                cands = list(range(0 if j > 0 else 1, 9))
                qap = self.QT[:, 8 + 4 * g:12 + 4 * g, j * 128:(j + 1) * 128]
                qkeys = [("Q", 8 + 4 * g + hh, j // 4) for hh in range(4)]
                ob, db = 3 + (j % 2), 5 + (j % 2)
                for si, sidx in enumerate(cands):
                    sbank = self.next_ps(0, 3)
                    extra = [(self.identb, swam[:, sidx, :], hh * 128, (hh + 1) * 128, ["cb", "swam"]) for hh in range(4)]
                    kk = [("slot", s), ("slotw", s)] + qkeys
                    self.attn_unit(sbank, kw[:, sidx, :], vw[:, sidx, :], qap, 512, extra, None, ob, db, slice(0, 512),
                                   si == 0, si == len(cands) - 1, kk, [("slotv", s), ("slotx", s)])
                sinks = [self.vec[:, self.V_SINK + 4 * g + hh:self.V_SINK + 4 * g + hh + 1] for hh in range(4)]
                self.finalize_head(ob, db, qap, qkeys, 512, sink_cols=sinks)

    def out_proj(self, l):
        S = self.S
        W = self.W[l]
        for g in range(8):
            s, wv = self.wslot_load(W["out"], 0, g * 256, 256, None)
            for ci in range(2):
                fo = g * 2 + ci
                for half in range(2):
                    cs = slice(half * 512, (half + 1) * 512)
                    b = self.next_ps(0, 4)
                    for hc in range(16):
                        S.pe((lambda b, wv, hc, ci, cs: (lambda e: e.matmul(self.PS[b][:], lhsT=wv[:, hc, ci * 128:(ci + 1) * 128], rhs=self.QT[:, hc, cs],
                                                                            start=(hc == 0), stop=(hc == 15))))(b, wv, hc, ci, cs),
                             reads=[("slot", s), ("Q", hc, half)], writes=[("ps", b)])
                    S.dve((lambda b, fo, cs: (lambda e: e.scalar_tensor_tensor(out=self.xT[:, fo, cs], in0=self.PS[b][:], scalar=self.modcol(l, 2, fo),
                                                                                in1=self.xT[:, fo, cs], op0=ALU.mult, op1=ALU.add)))(b, fo, cs),
                          reads=[("ps", b), ("x", fo, half), "v_mod"], writes=[("x", fo, half)])

    def mlp(self, l):
        S = self.S
        W = self.W[l]
        A3 = self.A[:].rearrange("p (c t) -> p c t", c=16)
        for qd in range(4):
            for g in range(8):
                s, wv = self.wslot_load(W["up"], 0, qd * 2048 + g * 256, 256, None)
                for ci in range(2):
                    fc = g * 2 + ci
                    for half in range(2):
                        cs = slice(half * 512, (half + 1) * 512)
                        b = self.next_ps(0, 4)
                        for kc in range(16):
                            S.pe((lambda b, wv, kc, ci, cs: (lambda e: e.matmul(self.PS[b][:], lhsT=wv[:, kc, ci * 128:(ci + 1) * 128], rhs=A3[:, kc, cs],
                                                                                start=(kc == 0), stop=(kc == 15))))(b, wv, kc, ci, cs),
                                 reads=[("slot", s), ("A", kc, half)], writes=[("ps", b)])
                        t = self.tq[self.tq_rr % 3]
                        tk = ("tq", self.tq_rr % 3)
                        self.tq_rr += 1
                        S.act((lambda b, t: (lambda e: e.activation(out=t[:], in_=self.PS[b][:], func=AF.Relu)))(b, t), reads=[("ps", b)], writes=[tk])
                        S.dve((lambda t, fc, cs: (lambda e: e.tensor_tensor(out=self.QT[:, fc, cs], in0=t[:], in1=t[:], op=ALU.mult)))(t, fc, cs),
                              reads=[tk], writes=[("Q", fc, half)])
            for g in range(8):
                s, wv = self.wslot_load(W["down"], qd * 2048, g * 256, 256, None)
                for ci in range(2):
                    fo = g * 2 + ci
                    for half in range(2):
                        cs = slice(half * 512, (half + 1) * 512)
                        b = self.next_ps(0, 4)
                        for fc in range(16):
                            S.pe((lambda b, wv, fc, ci, cs: (lambda e: e.matmul(self.PS[b][:], lhsT=wv[:, fc, ci * 128:(ci + 1) * 128], rhs=self.QT[:, fc, cs],
                                                                                start=(fc == 0), stop=(fc == 15))))(b, wv, fc, ci, cs),
                                 reads=[("slot", s), ("Q", fc, half)], writes=[("ps", b)])
                        S.dve((lambda b, fo, cs: (lambda e: e.scalar_tensor_tensor(out=self.xT[:, fo, cs], in0=self.PS[b][:], scalar=self.modcol(l, 5, fo),
                                                                                    in1=self.xT[:, fo, cs], op0=ALU.mult, op1=ALU.add)))(b, fo, cs),
                              reads=[("ps", b), ("x", fo, half), "v_mod"], writes=[("x", fo, half)])

    def layer_A(self, l):
        S = self.S
        vec = self.vec
        self.derive(l)
        self.norm_to_A(lambda ch: vec[:, self.V_DER + ch:self.V_DER + ch + 1], lambda ch: self.modcol(l, 0, ch), ("v_der", 0))
        spec = even_spec() if l % 2 == 0 else odd_spec()
        self.projections(l, spec)
        if l % 2 == 0:
            S.dma("sync", lambda e: e.dma_start(out=self.ssend[l][:, :], in_=self.XF[:, 2368:2432]), f"d_ss{l}", reads=["ksumT"], writes=[("ssend", l)])

    def layer_B(self, l):
        vec = self.vec
        if l % 2 == 0:
            self.even_attention(l)
        else:
            self.odd_attention(l)
        self.out_proj(l)
        if self.stop_after_mix and l == self.nlayers - 1:
            return
        self.norm_to_A(lambda ch: vec[:, self.V_DER + 16 + ch:self.V_DER + 16 + ch + 1], lambda ch: self.modcol(l, 3, ch), ("v_der", 1))
        self.mlp(l)

    def save_state(self):
        S = self.S
        allx = [("x", c, h) for c in range(16) for h in range(2)]
        allq = [("Q", c, h) for c in range(16) for h in range(2)]
        S.dma("sync", lambda e: e.dma_start(out=self._xso.rearrange("c p t -> p c t"), in_=self.xT[:]), "d_so0", reads=allx, writes=["xso"])
        S.dma("sync", lambda e: e.dma_start(out=self._qso, in_=self.QT[:].rearrange("p c t -> p (c t)")), "d_so1", reads=allq, writes=["qso"])
        S.dma("sync", lambda e: e.dma_start(out=self._gso, in_=self.gsb[:].rearrange("p j n -> p (j n)")), "d_so2", reads=["gsb"], writes=["gso"])
        S.dma("sync", lambda e: e.dma_start(out=self._pso, in_=self.vec[:, self.V_POSB:self.V_POSB + 2]), "d_so3", reads=["v_posb"], writes=["pso"])

    def restore_state(self):
        S = self.S
        allq = [("Q", c, h) for c in range(16) for h in range(2)]
        S.dma("sync", lambda e: e.dma_start(out=self.QT[:].rearrange("p c t -> p (c t)"), in_=self.nc_in("qstate_in")), "d_si1", writes=allq)
        S.dma("sync", lambda e: e.dma_start(out=self.gsb[:].rearrange("p j n -> p (j n)"), in_=self.nc_in("gstate_in")), "d_si2", writes=["gsb"])
        S.dma("sync", lambda e: e.dma_start(out=self.vec[:, self.V_POSB:self.V_POSB + 2], in_=self.nc_in("pstate_in")), "d_si3", writes=["v_posb"])

    def store_x(self):
        S = self.S
        outT = self._outT
        self.out_ops = [S.dma("sync", lambda e: e.dma_start(out=outT.rearrange("c p t -> p c t"), in_=self.xT[:]), "d_out",
                              reads=[("x", c, h) for c in range(16) for h in range(2)], writes=["outT"])]

    def final(self):
        S = self.S
        vec = self.vec
        A3 = self.A[:].rearrange("p (c t) -> p c t", c=16)
        for half in range(2):
            cs = slice(half * 512, (half + 1) * 512)
            bank = 4 + half
            for ch in range(16):
                sq = self.tq[ch % 2]
                S.act((lambda sq, ch, cs: (lambda e: e.activation(out=sq[:], in_=self.xT[:, ch, cs], func=AF.Square)))(sq, ch, cs),
                      reads=[("x", ch, half)], writes=[("tq", ch % 2)])
                S.pe((lambda sq, ch, bank: (lambda e: e.matmul(self.PS[bank][:], lhsT=self.onesf, rhs=sq[:], start=(ch == 0), stop=(ch == 15))))(sq, ch, bank),
                     reads=[("tq", ch % 2), "cf"], writes=[("ps", bank)])
            rs = self.XF[:, 1024 + half * 512:1024 + (half + 1) * 512]
            S.act((lambda bank, rs: (lambda e: e.activation(out=rs, in_=self.PS[bank][:], func=AF.Sqrt, scale=1.0 / D, bias=EPS)))(bank, rs),
                  reads=[("ps", bank)], writes=[("rstd", half)])
            S.dve((lambda rs: (lambda e: e.reciprocal(out=rs, in_=rs)))(rs), reads=[("rstd", half)], writes=[("rstd", half)])
            for ch in range(16):
                S.dve((lambda ch, cs, rs: (lambda e: e.scalar_tensor_tensor(out=self.xT[:, ch, cs], in0=self.xT[:, ch, cs],
                                                                            scalar=vec[:, self.V_G + 128 + ch:self.V_G + 128 + ch + 1], in1=rs,
                                                                            op0=ALU.mult, op1=ALU.mult)))(ch, cs, rs),
                      reads=[("x", ch, half), ("rstd", half), "v_g"], writes=[("x", ch, half)])
        self.store_x()

    def odd_small(self, l):
        S = self.S
        XF = self.XF
        j2 = l // 2
        A3 = self.A[:].rearrange("p (c t) -> p c t", c=16)
        s = self.slot_rr % 3
        self.slot_rr += 1
        wv = self.SL[s][:, 0:512].rearrange("p (k n) -> p k n", k=16)
        wsm = self.W[l]["sm"]
        S.dma("gpsimd", lambda e: e.dma_start(out=wv, in_=wsm.rearrange("(k p) n -> p k n", p=128)), f"d_slot{s}",
              writes=[("slot", s), ("slotv", s), ("slotw", s), ("slotx", s)])
        S.dma("sync", lambda e: e.dma_start(out=XF[:, 2072:2080], in_=self.od["fbias"][:, j2 * 8:(j2 + 1) * 8]), "d_fb", writes=["fbias"])
        for j in range(8):
            for kc in range(16):
                S.pe((lambda kc, j: (lambda e: e.matmul(self.PS[6][:, j * 32:(j + 1) * 32], lhsT=A3[:, kc, j * 128:(j + 1) * 128], rhs=wv[:, kc, :],
                                                        start=(kc == 0), stop=(kc == 15))))(kc, j),
                     reads=[("slot", s), ("A", kc, j // 4)], writes=[("ps", 6)])
        ps3 = self.PS[6][:, 0:256].rearrange("p (j n) -> p j n", j=8)
        z = XF[:, 1536:1600].rearrange("p (j h) -> p j h", j=8)
        S.dve(lambda e: e.tensor_tensor(out=z, in0=ps3[:, :, 0:8], in1=XF[:, 2072:2080].unsqueeze(1).broadcast_to([128, 8, 8]), op=ALU.add),
              reads=[("ps", 6), "fbias"], writes=[("rstd", 1)])
        S.act(lambda e: e.activation(out=XF[:, 1536:1600], in_=XF[:, 1536:1600], func=AF.Exp, scale=-1.0), reads=[("rstd", 1)], writes=[("rstd", 1)])
        S.act(lambda e: e.activation(out=XF[:, 1536:1600], in_=XF[:, 1536:1600], func=AF.Ln, bias=1.0), reads=[("rstd", 1)], writes=[("rstd", 1)])
        S.act(lambda e: e.activation(out=self.gsb[:], in_=ps3[:, :, 8:32], func=AF.Sigmoid), reads=[("ps", 6)], writes=["gsb"])
        S.dma("sync", lambda e: e.dma_start(out=self.ssend[l][:, 0:64], in_=XF[:, 1536:1600]), f"d_ss{l}", reads=[("rstd", 1)], writes=[("ssend", l)])

    def cmp_partials(self, l, idx):
        S = self.S
        XF = self.XF
        j2 = l // 2
        typ = idx // 2
        kb = idx % 2
        if idx % 2 == 0:
            s = self.slot_rr % 3
            self.slot_rr += 1
            self.w1slot = s
            w1v = self.SL[s][:, 0:4096].rearrange("p (l m) -> p l m", l=32)
            self.w1v = w1v
            src = self.od["w1"][j2][typ].rearrange("(l d) m -> d l m", d=128)
            S.dma("gpsimd", lambda e: e.dma_start(out=w1v, in_=src), f"d_slot{s}", writes=[("slot", s), ("slotv", s), ("slotw", s), ("slotx", s)])
            if typ == 0:
                S.dma("gpsimd", lambda e: e.dma_start(out=self.posT[:], in_=self.od["posT"][:, j2 * 64:(j2 + 1) * 64].rearrange("p (t l) -> p t l", t=2)),
                      "d_pos", writes=["posT"])
            for li in range(32):
                S.pe((lambda li, w1v, typ: (lambda e: e.matmul(self.PS[6][:, 300:301], lhsT=w1v[:, li, :], rhs=self.posT[:, typ, li:li + 1],
                                                               start=(li == 0), stop=(li == 31))))(li, w1v, typ),
                     reads=[("slot", s), "posT"], writes=[("ps", 6)])
            S.act((lambda typ: (lambda e: e.copy(out=self.vec[:, self.V_POSB + typ:self.V_POSB + typ + 1], in_=self.PS[6][:, 300:301])))(typ),
                  reads=[("ps", 6)], writes=["v_posb"])
        s = self.w1slot
        w1v = self.w1v
        kv = self.kst[kb][:].rearrange("p (n r) -> p n r", r=16)
        for ab in range(2):
            b = self.next_ps(0, 4)
            for li in range(16):
                S.pe((lambda b, li, ab, w1v, kv: (lambda e: e.matmul(self.PS[b][:, 0:64], lhsT=w1v[:, ab * 16 + li, :], rhs=kv[:, :, li],
                                                                      start=(li == 0), stop=(li == 15))))(b, li, ab, w1v, kv),
                     reads=[("slot", s), ("kst", kb)], writes=[("ps", b)])
            o = 1024 + idx * 128 + ab * 64
            S.act((lambda b, o: (lambda e: e.copy(out=XF[:, o:o + 64], in_=self.PS[b][:, 0:64])))(b, o), reads=[("ps", b)], writes=[("rstd", 0)])
        if idx == 3:
            S.dma("sync", lambda e: e.dma_start(out=self.ssend[l][:, 64:576], in_=XF[:, 1024:1536]), f"d_ss{l}", reads=[("rstd", 0)], writes=[("ssend", l)])

    def gate_bcast(self, bank, c0, n, head, br, tcols):
        S = self.S
        A = self.A
        r = head * 3 + br
        wh = A[0:24, r * 64:(r + 1) * 64]
        gT = A[0:24, 10624:11648]
        for half in range(2):
            S.pe((lambda half: (lambda e: e.matmul(self.PS[bank][half * 64:(half + 1) * 64, c0:c0 + n], lhsT=wh, rhs=gT[:, tcols], start=True, stop=True)))(half),
                 reads=["wh", "gT"], writes=[("ps", bank)])

    def odd_attention(self, l):
        S = self.S
        A = self.A
        XF = self.XF
        od = self.od
        j2 = l // 2
        g3 = self.gath[l].ap().rearrange("(c r) t -> r c t", c=NCORE)
        sg = self.sgath[l].ap().rearrange("(c p) f -> p c f", p=128)
        wh = A[:, 0:4096]
        winm = A[:, 4096:5632].rearrange("p (s q) -> p s q", s=12)
        cmpm = A[:, 5632:6016].rearrange("p (s q) -> p s q", s=3)
        selbTn = [A[:, 6016:7040], A[:, 7040:8064]]
        ovl = A[:, 8064:8576].rearrange("p (n b) -> p n b", n=4)
        kcT = [A[:, 8576:9088], A[:, 9088:9600]]
        vcv = [A[:, 9600:10112].rearrange("p (n d) -> p n d", n=4), A[:, 10112:10624].rearrange("p (n d) -> p n d", n=4)]
        gT = A[0:24, 10624:11648]
        odacc = A[:, 11648:15744].rearrange("p (h t) -> p h t", h=4)
        hid = A[:, 15744:16256]
        akeys = [("A", c, h) for c in range(16) for h in range(2)]
        S.dma("sync", lambda e: e.dma_start(out=wh, in_=od["wh"]), "d_ac0", writes=akeys + ["wh"])
        S.dma("sync", lambda e: e.dma_start(out=A[:, 4096:5632], in_=od["winmask"]), "d_ac1", writes=["winm"])
        S.dma("sync", lambda e: e.dma_start(out=A[:, 5632:6016], in_=od["cmpmask"]), "d_ac2", writes=["cmpm"])
        S.dma("sync", lambda e: e.dma_start(out=A[:, 8064:8576], in_=od["ovl"]), "d_ac3", writes=["ovl"])
        S.dma("sync", lambda e: e.dma_start(out=XF[:, 576:816], in_=od["selbase"]), "d_ac4", writes=["selbase"])
        S.dma("sync", lambda e: e.dma_start(out=XF[:, 2064:2072], in_=od["onehot"]), "d_ac5", writes=["onehot"])
        S.dma("gpsimd", lambda e: e.dma_start(out=self.w2sb[:], in_=od["w2"][j2].rearrange("p (t m) -> p t m", t=2)), "d_w2", writes=["w2sb"])
        for j in range(8):
            S.pe((lambda j: (lambda e: e.transpose(self.PSB[0:24, j * 128:(j + 1) * 128], self.gsb[:, j, :], self.identb)))(j),
                 reads=["gsb", "cb"], writes=["psb"])
        S.act(lambda e: e.copy(out=gT, in_=self.PSB[0:24, :]), reads=["psb"], writes=["gT"])
        for idx in range(4):
            typ, g = idx // 2, idx % 2
            for ab in range(2):
                S.dma("sync", (lambda idx, ab: (lambda e: e.dma_start(out=self.tq[ab][:].rearrange("p (c f) -> p c f", c=8),
                                                                       in_=sg[:, :, 64 + idx * 128 + ab * 64:64 + idx * 128 + (ab + 1) * 64])))(idx, ab),
                      f"d_tq{ab}", reads=[("sgath", l)], writes=[("tq", ab)])
            t2 = self.tq[2]
            pbg = XF[:, 1024:1536]
            S.dve(lambda e: e.tensor_copy(out=t2[:].rearrange("p (j c m) -> p j c m", j=8, c=8),
                                          in_=self.tq[0][:].rearrange("p (c j m) -> p c j m", c=8, j=8).transpose([0, 2, 1, 3])),
                  reads=[("tq", 0)], writes=[("tq", 2)])
            S.dve(lambda e: e.tensor_copy(out=pbg.rearrange("p (j c m) -> p j c m", j=8, c=8),
                                          in_=self.tq[1][:].rearrange("p (c j m) -> p c j m", c=8, j=8).transpose([0, 2, 1, 3])),
                  reads=[("tq", 1)], writes=[("rstd", 0)])
            S.dve(lambda e: e.tensor_tensor(out=t2[:, 0:511], in0=t2[:, 0:511], in1=pbg[:, 1:512], op=ALU.add), reads=[("tq", 2), ("rstd", 0)], writes=[("tq", 2)])
            pb = self.vec[:, self.V_POSB + typ:self.V_POSB + typ + 1]
            S.dve((lambda pb: (lambda e: e.tensor_scalar(out=t2[:], in0=t2[:], scalar1=pb, scalar2=None, op0=ALU.add)))(pb), reads=[("tq", 2), "v_posb"], writes=[("tq", 2)])
            x2 = XF[:, 1536:2048]
            S.dve(lambda e: e.tensor_tensor(out=x2, in0=t2[:], in1=t2[:], op=ALU.mult), reads=[("tq", 2)], writes=[("rstd", 1)])
            S.dve(lambda e: e.tensor_scalar(out=x2, in0=x2, scalar1=0.044715, scalar2=1.0, op0=ALU.mult, op1=ALU.add), reads=[("rstd", 1)], writes=[("rstd", 1)])
            S.dve(lambda e: e.tensor_tensor(out=x2, in0=x2, in1=t2[:], op=ALU.mult), reads=[("rstd", 1), ("tq", 2)], writes=[("rstd", 1)])
            S.act(lambda e: e.activation(out=x2, in_=x2, func=AF.Tanh, scale=0.7978845608028654), reads=[("rstd", 1)], writes=[("rstd", 1)])
            S.dve(lambda e: e.tensor_scalar(out=x2, in0=x2, scalar1=1.0, scalar2=0.5, op0=ALU.add, op1=ALU.mult), reads=[("rstd", 1)], writes=[("rstd", 1)])
            S.dve(lambda e: e.tensor_tensor(out=hid, in0=x2, in1=t2[:], op=ALU.mult), reads=[("rstd", 1), ("tq", 2)], writes=["hid"])
            if typ == 0:
                S.pe(lambda e: e.matmul(self.PS[6][:], lhsT=self.w2sb[:, 0, :], rhs=hid, start=True, stop=True), reads=["hid", "w2sb"], writes=[("ps", 6)])
                S.act((lambda g: (lambda e: e.copy(out=kcT[g], in_=self.PS[6][:])))(g), reads=[("ps", 6)], writes=[("kcT", g)])
            else:
                for nt in range(4):
                    S.pe((lambda nt: (lambda e: e.matmul(self.PS[6][:, nt * 128:(nt + 1) * 128], lhsT=hid[:, nt * 128:(nt + 1) * 128], rhs=self.w2sb[:, 1, :],
                                                         start=True, stop=True)))(nt),
                         reads=["hid", "w2sb"], writes=[("ps", 6)])
                S.act((lambda g: (lambda e: e.copy(out=vcv[g], in_=self.PS[6][:].rearrange("p (n d) -> p n d", n=4))))(g), reads=[("ps", 6)], writes=[("vcv", g)])
        lfg = self.tq[0]
        S.dma("sync", lambda e: e.dma_start(out=lfg[:].rearrange("p (c f) -> p c f", c=8), in_=sg[:, :, 0:64]), "d_tq0", reads=[("sgath", l)], writes=[("tq", 0)])
        S.pe(lambda e: e.matmul(self.PS[4][:], lhsT=self.tri, rhs=lfg[:], start=True, stop=True), reads=[("tq", 0), "cf"], writes=[("ps", 4)])
        S.pe(lambda e: e.matmul(self.PS[5][:], lhsT=self.onesf, rhs=lfg[:], start=True, stop=True), reads=[("tq", 0), "cf"], writes=[("ps", 5)])
        ta, tb = self.tq[1], self.tq[2]
        perm = lambda ap: ap.rearrange("p (c j h) -> p c j h", c=8, j=8).transpose([0, 2, 1, 3])
        glob = lambda ap: ap.rearrange("p (j c h) -> p j c h", j=8, c=8)
        S.dve(lambda e: e.tensor_copy(out=glob(ta[:]), in_=perm(self.PS[5][:])), reads=[("ps", 5)], writes=[("tq", 1)])
        S.dve(lambda e: e.memset(tb[:, 0:8], 0.0), writes=[("tq", 2)])
        S.dve(lambda e: e.tensor_copy(out=tb[:, 8:512], in_=ta[:, 0:504]), reads=[("tq", 1)], writes=[("tq", 2)])
        cur, nxt, ck_, nk_ = tb, ta, ("tq", 2), ("tq", 1)
        for sh in (1, 2, 4, 8, 16, 32):
            w = sh * 8
            S.dve((lambda cur, nxt, w: (lambda e: e.tensor_tensor(out=nxt[:, w:512], in0=cur[:, w:512], in1=cur[:, 0:512 - w], op=ALU.add)))(cur, nxt, w),
                  reads=[ck_], writes=[nk_])
            S.dve((lambda cur, nxt, w: (lambda e: e.tensor_copy(out=nxt[:, 0:w], in_=cur[:, 0:w])))(cur, nxt, w), reads=[ck_], writes=[nk_])
            cur, nxt, ck_, nk_ = nxt, cur, nk_, ck_
        off = cur
        offk = ck_
        NCf = XF[:, 0:512]
        S.dve(lambda e: e.tensor_tensor(out=glob(NCf), in0=perm(self.PS[4][:]), in1=glob(off[:]), op=ALU.add), reads=[("ps", 4), offk], writes=["NCf"])
        tmp = nxt
        S.dve(lambda e: e.tensor_tensor(out=tmp[:].rearrange("p (j h c) -> p j h c", j=8, h=8),
                                        in0=glob(off[:]).transpose([0, 1, 3, 2]),
                                        in1=XF[:, 2064:2072].unsqueeze(1).unsqueeze(1).broadcast_to([128, 8, 8, 8]), op=ALU.mult),
              reads=[offk, "onehot"], writes=[nk_])
        offq = XF[:, 512:576]
        S.dve(lambda e: e.tensor_reduce(out=offq, in_=tmp[:].rearrange("p (a c) -> p a c", c=8), axis=AX.X, op=ALU.add), reads=[nk_], writes=["offq"])
        NC3 = NCf.rearrange("p (b h) -> p b h", h=8)
        oq3 = offq.rearrange("p (j h) -> p j h", h=8)
        ni = 24
        for hb in range(2):
            S.dve((lambda hb: (lambda e: e.memset(self.kst[hb][0:64, :], 0.0)))(hb), writes=[("kst", hb)])
        v8, lo8, hi8 = XF[0:64, 2080:2088], XF[0:64, 2088:2096], self.selb[0:64, 0:8]
        for h in range(8):
            hb = h % 2
            orow = self.kst[hb]
            ok = ("kst", hb)
            S.dve((lambda h: (lambda e: e.tensor_scalar(out=v8, in0=oq3[0:64, :, h], scalar1=-1.0 / SCALE, scalar2=None, op0=ALU.mult)))(h),
                  reads=["offq"], writes=["v8"])
            S.dve(lambda e: e.tensor_copy(out=hi8, in_=v8), reads=["v8"], writes=["selb"])
            S.dve(lambda e: e.tensor_tensor(out=lo8, in0=v8, in1=hi8, op=ALU.subtract), reads=["v8", "selb"], writes=["lo8"])
            S.dve((lambda orow: (lambda e: e.tensor_copy(out=orow[0:1, :].rearrange("p (j t) -> p j t", j=8),
                                                         in_=hi8[0:1, :].unsqueeze(2).broadcast_to([1, 8, 128]))))(orow),
                  reads=["selb"], writes=[ok])
            S.dve((lambda orow: (lambda e: e.tensor_copy(out=orow[32:33, :].rearrange("p (j t) -> p j t", j=8),
                                                         in_=lo8[32:33, :].unsqueeze(2).broadcast_to([1, 8, 128]))))(orow),
                  reads=["lo8"], writes=[ok])
            for cp in range(4):
                s, kd, vd = self.kv_load(l, h, 12 + h, cp, ni)
                for cc in range(2):
                    ck = 2 * cp + cc
                    for jk in range(8):
                        b = 8 * jk + ck
                        groups = []
                        if jk < 4:
                            groups.append((0, jk * 128, 512))
                        groups.append((1, max(512, jk * 128), 1024))
                        for (grp, c0, c1) in groups:
                            n = c1 - c0
                            sbank = self.next_ps(0, 3)
                            extra = []
                            if c0 == jk * 128:
                                extra.append((self.identb, self.cmask[:, ck, :], 0, 128, ["cb", "cmask"]))
                            extra.append((self.onesb[0:64, :], orow[0:64, c0:c1], 0, n, ["cb", ok]))
                            ebias = [(0, n, NC3[:, b, h:h + 1], ["NCf"])]
                            last = (ck == 7 and jk == (3 if grp == 0 else 7))
                            fst = (ck == 0 and jk == 0)
                            self.attn_unit(sbank, kd[:, cc, jk * 128:(jk + 1) * 128], vd[:, cc, jk, :], self.QT[:, h, c0:c1], n, extra, ebias,
                                           3 + grp, 5 + grp, slice(c0 - grp * 512, c1 - grp * 512), fst, last,
                                           [("slot", s), ("Q", h, grp)], [("slotv", s)])
            for grp in range(2):
                self.finalize_head(3 + grp, 5 + grp, self.QT[:, h, grp * 512:(grp + 1) * 512], [("Q", h, grp)])
        impa, tmp2, m1, m2 = XF[:, 816:944], XF[:, 944:1072], XF[:, 2048:2056], XF[:, 2056:2064]
        def nsa_group(g):
            qh0 = 8 + 4 * g
            for j in range(8):
                ntb = j // 2
                qap = self.QT[:, qh0:qh0 + 4, j * 128:(j + 1) * 128]
                qkeys = [("Q", qh0 + hh, j // 4) for hh in range(4)]
                for nt in range(ntb + 1):
                    sbank = self.next_ps(0, 3)
                    slot_m = None
                    if nt == ntb:
                        slot_m = 0 if j % 2 == 0 else 1
                    elif nt == ntb - 1 and j % 2 == 0:
                        slot_m = 2
                    S.pe((lambda sbank, nt, qap, slot_m: (lambda e: e.matmul(self.PS[sbank][:], lhsT=kcT[g][:, nt * 128:(nt + 1) * 128], rhs=qap,
                                                                             start=True, stop=(slot_m is None))))(sbank, nt, qap, slot_m),
                         reads=[("kcT", g)] + qkeys, writes=[("ps", sbank)])
                    if slot_m is not None:
                        for hh in range(4):
                            S.pe((lambda sbank, hh, slot_m: (lambda e: e.matmul(self.PS[sbank][:, hh * 128:(hh + 1) * 128], lhsT=self.identb, rhs=cmpm[:, slot_m, :],
                                                                                start=False, stop=(hh == 3))))(sbank, hh, slot_m),
                                 reads=["cb", "cmpm"], writes=[("ps", sbank)])
                    S.act((lambda sbank, nt: (lambda e: e.activation(out=self.PC[:, nt, :], in_=self.PS[sbank][:], func=AF.Exp, scale=SCALE)))(sbank, nt),
                          reads=[("ps", sbank)], writes=[("PC", nt)])
                    S.pe((lambda nt: (lambda e: e.matmul(self.PS[3][:], lhsT=vcv[g][:, nt, :], rhs=self.PC[:, nt, :], start=(nt == 0), stop=(nt == ntb),
                                                         skip_group_check=True)))(nt),
                         reads=[("PC", nt), ("vcv", g)], writes=[("ps", 3)])
                    S.pe((lambda nt: (lambda e: e.matmul(self.PS[5][:], lhsT=self.onesb, rhs=self.PC[:, nt, :], start=(nt == 0), stop=(nt == ntb),
                                                         skip_group_check=True)))(nt),
                         reads=[("PC", nt), "cb"], writes=[("ps", 5)])
                rd = self.tq[self.tq_rr % 3]
                rk = ("tq", self.tq_rr % 3)
                self.tq_rr += 1
                S.dve((lambda rd: (lambda e: e.tensor_scalar(out=rd[:], in0=self.PS[5][:], scalar1=1e-30, scalar2=None, op0=ALU.max)))(rd), reads=[("ps", 5)], writes=[rk])
                S.dve((lambda rd: (lambda e: e.reciprocal(out=rd[:], in_=rd[:])))(rd), reads=[rk], writes=[rk])
                for nt in range(ntb + 1):
                    S.dve((lambda rd, nt: (lambda e: e.tensor_tensor(out=self.PC[:, nt, :], in0=self.PC[:, nt, :], in1=rd[:], op=ALU.mult)))(rd, nt),
                          reads=[("PC", nt), rk], writes=[("PC", nt)])
                nmm = 4 * (ntb + 1)
                k = 0
                for hh in range(4):
                    for nt in range(ntb + 1):
                        S.pe((lambda hh, nt, k: (lambda e: e.matmul(self.PS[6][:, 0:128], lhsT=self.PC[:, nt, hh * 128:(hh + 1) * 128], rhs=ovl[:, nt, :],
                                                                    start=(k == 0), stop=(k == nmm - 1))))(hh, nt, k),
                             reads=[("PC", nt), "ovl"], writes=[("ps", 6)])
                        k += 1
                w0 = 576 + 112 - 16 * j
                S.dve((lambda w0: (lambda e: e.tensor_tensor(out=impa, in0=self.PS[6][:, 0:128], in1=XF[:, w0:w0 + 128], op=ALU.add)))(w0),
                      reads=[("ps", 6), "selbase"], writes=["impa"])
                S.dve(lambda e: e.memset(impa[:, 0:1], 100.0), reads=["impa"], writes=["impa"])
                S.dve(lambda e: e.max(out=m1, in_=impa), reads=["impa"], writes=["m1"])
                S.dve(lambda e: e.match_replace(out=tmp2, in_to_replace=m1, in_values=impa, imm_value=-1e30), reads=["impa", "m1"], writes=["tmp2"])
                S.dve(lambda e: e.max(out=m2, in_=tmp2), reads=["tmp2"], writes=["m2"])
                S.dve(lambda e: e.tensor_scalar(out=tmp2, in0=impa, scalar1=m2[:, 7:8], scalar2=None, op0=ALU.is_ge), reads=["impa", "m2"], writes=["tmp2"])
                S.dve(lambda e: e.tensor_scalar(out=self.selb[:, 0:128], in0=tmp2, scalar1=-1.0, scalar2=-NEG, op0=ALU.add, op1=ALU.mult), reads=["tmp2"], writes=["selb"])
                S.pe(lambda e: e.transpose(self.PSB[:, 0:128], self.selb[:, 0:128], self.identb), reads=["selb", "cb"], writes=["psb"])
                S.act((lambda j: (lambda e: e.copy(out=selbTn[g][:, j * 128:(j + 1) * 128], in_=self.PSB[:, 0:128])))(j), reads=["psb"], writes=[("selbTn", g)])
                for hh in range(4):
                    self.gate_bcast(4, hh * 128, 128, 4 * g + hh, 0, slice(j * 128, (j + 1) * 128))
                S.dve((lambda rd: (lambda e: e.tensor_tensor(out=rd[:], in0=self.PS[4][:], in1=rd[:], op=ALU.mult)))(rd), reads=[("ps", 4), rk], writes=[rk])
                S.dve((lambda rd, j: (lambda e: e.tensor_tensor(out=odacc[:, :, j * 128:(j + 1) * 128], in0=self.PS[3][:].rearrange("p (h t) -> p h t", h=4),
                                                                in1=rd[:].rearrange("p (h t) -> p h t", h=4), op=ALU.mult)))(rd, j),
                      reads=[("ps", 3), rk], writes=[("odacc", j)])
            for hh in range(4):
                h = qh0 + hh
                for cp in range(4):
                    s, kd, vd = self.kv_load(l, 8 + g, 20 + g, cp, ni)
                    for cc in range(2):
                        ck = 2 * cp + cc
                        for jk in range(8):
                            gb = 8 * jk + ck
                            pbase = 64 * ((2 * gb) // 64)
                            r0 = (2 * gb) % 64
                            groups = []
                            if jk < 4:
                                groups.append((0, jk * 128, 512))
                            groups.append((1, max(512, jk * 128), 1024))
                            for (grp, c0, c1) in groups:
                                n = c1 - c0
                                sbank = self.next_ps(0, 3)
                                extra = []
                                if c0 == jk * 128:
                                    extra.append((self.identb, self.cmask[:, ck, :], 0, 128, ["cb", "cmask"]))
                                extra.append((wh[pbase:pbase + 64, r0 * 64:r0 * 64 + 128], selbTn[g][pbase:pbase + 64, c0:c1], 0, n, ["wh", ("selbTn", g)]))
                                last = (ck == 7 and jk == (3 if grp == 0 else 7))
                                fst = (ck == 0 and jk == 0)
                                self.attn_unit(sbank, kd[:, cc, jk * 128:(jk + 1) * 128], vd[:, cc, jk, :], self.QT[:, h, c0:c1], n, extra, None,
                                               3 + grp, 5 + grp, slice(c0 - grp * 512, c1 - grp * 512), fst, last,
                                               [("slot", s), ("Q", h, grp)], [("slotv", s)])
                self.attn_flush()
                for grp in range(2):
                    rd = self.tq[self.tq_rr % 3]
                    rk = ("tq", self.tq_rr % 3)
                    self.tq_rr += 1
                    t2_ = self.tq[self.tq_rr % 3]
                    tk2 = ("tq", self.tq_rr % 3)
                    self.tq_rr += 1
                    S.dve((lambda rd, grp: (lambda e: e.tensor_scalar(out=rd[:], in0=self.PS[5 + grp][:], scalar1=1e-30, scalar2=None, op0=ALU.max)))(rd, grp),
                          reads=[("ps", 5 + grp)], writes=[rk])
                    S.dve((lambda rd: (lambda e: e.reciprocal(out=rd[:], in_=rd[:])))(rd), reads=[rk], writes=[rk])
                    gbank = 5 + grp
                    self.gate_bcast(gbank, 0, 512, 4 * g + hh, 1, slice(grp * 512, (grp + 1) * 512))
                    S.dve((lambda rd, gbank: (lambda e: e.tensor_tensor(out=rd[:], in0=self.PS[gbank][:], in1=rd[:], op=ALU.mult)))(rd, gbank),
                          reads=[("ps", gbank), rk], writes=[rk])
                    S.dve((lambda rd, t2_, grp: (lambda e: e.tensor_tensor(out=t2_[:], in0=self.PS[3 + grp][:], in1=rd[:], op=ALU.mult)))(rd, t2_, grp),
                          reads=[("ps", 3 + grp), rk], writes=[tk2])
                    oa = odacc[:, hh, grp * 512:(grp + 1) * 512]
                    S.dve((lambda t2_, oa: (lambda e: e.tensor_tensor(out=oa, in0=oa, in1=t2_[:], op=ALU.add)))(t2_, oa),
                          reads=[tk2] + [("odacc", jq) for jq in range(grp * 4, grp * 4 + 4)], writes=[("odacc", jq) for jq in range(grp * 4, grp * 4 + 4)])
            kitem, vitem = 10 + g, 22 + g
            for j in range(8):
                s = self.slot_rr % 3
                self.slot_rr += 1
                slot = self.SL[s]
                kw = slot[:, 0:1536].rearrange("p (s t) -> p s t", s=12)
                vw = slot[:, 2048:2048 + 1536].rearrange("p (s t) -> p s t", s=12)
                rows_k = slice(kitem * 128, (kitem + 1) * 128)
                rows_v = slice(vitem * 128, (vitem + 1) * 128)
                S.dma("sync", (lambda kw, j, rows_k: (lambda e: e.dma_start(out=kw[:, 4:12, :], in_=g3[rows_k, :, j * 128:(j + 1) * 128])))(kw, j, rows_k),
                      f"d_slot{s}", reads=[("gath", l)], writes=[("slot", s)])
                S.dma("sync", (lambda vw, j, rows_v: (lambda e: e.dma_start(out=vw[:, 4:12, :], in_=g3[rows_v, :, j * 128:(j + 1) * 128])))(vw, j, rows_v),
                      f"d_slotv{s}", reads=[("gath", l)], writes=[("slotv", s)])
                if j > 0:
                    S.dma("sync", (lambda kw, j, rows_k: (lambda e: e.dma_start(out=kw[:, 0:4, :], in_=g3[rows_k, 4:8, (j - 1) * 128:j * 128])))(kw, j, rows_k),
                          f"d_slotw{s}", reads=[("gath", l)], writes=[("slotw", s)])
                    S.dma("sync", (lambda vw, j, rows_v: (lambda e: e.dma_start(out=vw[:, 0:4, :], in_=g3[rows_v, 4:8, (j - 1) * 128:j * 128])))(vw, j, rows_v),
                          f"d_slotx{s}", reads=[("gath", l)], writes=[("slotx", s)])
                cands = list(range(0 if j > 0 else 4, 12))
                qap = self.QT[:, qh0:qh0 + 4, j * 128:(j + 1) * 128]
                qkeys = [("Q", qh0 + hh, j // 4) for hh in range(4)]
                ob, db = 3 + (j % 2), 5 + (j % 2)
                for si, sidx in enumerate(cands):
                    sbank = self.next_ps(0, 3)
                    extra = [(self.identb, winm[:, sidx, :], hh * 128, (hh + 1) * 128, ["cb", "winm"]) for hh in range(4)]
                    self.attn_unit(sbank, kw[:, sidx, :], vw[:, sidx, :], qap, 512, extra, None, ob, db, slice(0, 512),
                                   si == 0, si == len(cands) - 1, [("slot", s), ("slotw", s)] + qkeys, [("slotv", s), ("slotx", s)])
                self.attn_flush()
                rd = self.tq[self.tq_rr % 3]
                rk = ("tq", self.tq_rr % 3)
                self.tq_rr += 1
                t2_ = self.tq[self.tq_rr % 3]
                tk2 = ("tq", self.tq_rr % 3)
                self.tq_rr += 1
                S.dve((lambda rd, db: (lambda e: e.tensor_scalar(out=rd[:], in0=self.PS[db][:], scalar1=1e-30, scalar2=None, op0=ALU.max)))(rd, db),
                      reads=[("ps", db)], writes=[rk])
                S.dve((lambda rd: (lambda e: e.reciprocal(out=rd[:], in_=rd[:])))(rd), reads=[rk], writes=[rk])
                for hh in range(4):
                    self.gate_bcast(db, hh * 128, 128, 4 * g + hh, 2, slice(j * 128, (j + 1) * 128))
                S.dve((lambda rd, db: (lambda e: e.tensor_tensor(out=rd[:], in0=self.PS[db][:], in1=rd[:], op=ALU.mult)))(rd, db), reads=[("ps", db), rk], writes=[rk])
                S.dve((lambda rd, t2_, ob: (lambda e: e.tensor_tensor(out=t2_[:], in0=self.PS[ob][:], in1=rd[:], op=ALU.mult)))(rd, t2_, ob),
                      reads=[("ps", ob), rk], writes=[tk2])
                S.dve((lambda t2_, j, qap: (lambda e: e.tensor_tensor(out=qap, in0=odacc[:, :, j * 128:(j + 1) * 128],
                                                                      in1=t2_[:].rearrange("p (h t) -> p h t", h=4), op=ALU.add)))(t2_, j, qap),
                      reads=[tk2, ("odacc", j)], writes=qkeys)

        for g in range(2):
            nsa_group(g)

def tok_index(c):
    j = np.arange(8)[:, None]
    i = np.arange(128)[None, :]
    return ((8 * j + c) * 128 + i).reshape(-1)


def bf(a):
    return np.ascontiguousarray(a.astype(ml_dtypes.bfloat16))


def host_consts(c):
    tok = tok_index(c)
    inv = 1.0 / (10000.0 ** (np.arange(0, 128, 2, dtype=np.float32) / 128.0))
    ang = tok.astype(np.float32)[:, None] * inv[None, :]
    cos = np.cos(ang).astype(np.float32).T
    sin = np.sin(ang).astype(np.float32).T
    ropecos = np.concatenate([cos, cos], 0)
    ropesin = np.concatenate([-sin, sin], 0)
    ii = np.arange(128)
    cm = np.zeros((128, 8, 128), np.float32)
    for ck in range(8):
        if ck > c:
            cm[:, ck, :] = NEG
        elif ck == c:
            cm[:, ck, :] = np.where(ii[:, None] <= ii[None, :], 0.0, NEG)
    identf = np.eye(128, dtype=np.float32)
    Rm = np.zeros((128, 128), np.float32)
    for m in range(128):
        Rm[(m + 64) % 128, m] = 1.0
    onesf = np.ones((128, 128), np.float32)
    tri = (ii[:, None] <= ii[None, :]).astype(np.float32)
    constf = np.concatenate([identf, Rm, onesf, tri], 1)
    constb = bf(np.concatenate([identf, onesf], 1))
    wm = np.zeros((32, 4096), np.float32)
    for k in range(32):
        wm[k, k * 128:(k + 1) * 128] = 1.0
    sw = np.full((128, 9, 128), NEG, np.float32)
    for s in range(9):
        diff = c + 1 - s
        if diff == 0:
            sw[:, s, :] = np.where(ii[:, None] <= ii[None, :], 0.0, NEG)
        elif diff == 1:
            sw[:, s, :] = np.where(ii[:, None] > ii[None, :], 0.0, NEG)
    gb = np.zeros((128, 8, 32), np.float32)
    past = np.zeros((128, 8, 32), np.float32)
    own = np.zeros((128, 8, 32), np.float32)
    for j in range(8):
        o = (8 * j + c) // 2
        gb[:, j, o:] = -1e30
        past[:, j, :o] = 1.0
        own[:, j, o] = 1.0
    gsel = np.concatenate([gb.reshape(128, 256), past.reshape(128, 256), own.reshape(128, 256)], 1)
    wh = np.zeros((128, 4096), np.float32)
    for k in range(128):
        wh[k, (k % 64) * 64:(k % 64 + 1) * 64] = 1.0
    wn = np.full((128, 12, 128), NEG, np.float32)
    for s in range(12):
        diff = c + 4 - s
        if diff == 0:
            wn[:, s, :] = np.where(ii[:, None] <= ii[None, :], 0.0, NEG)
        elif diff in (1, 2, 3):
            wn[:, s, :] = 0.0
        elif diff == 4:
            wn[:, s, :] = np.where(ii[:, None] > ii[None, :], 0.0, NEG)
    cmpm = np.zeros((128, 3, 128), np.float32)
    for slot, sh in enumerate((0, 64, 128)):
        rel = ii[:, None] - sh - 8 * c
        cmpm[:, slot, :] = np.where(16 * rel + 31 <= ii[None, :], 0.0, NEG)
    ovl = np.zeros((128, 4, 128), np.float32)
    for nt in range(4):
        cs_ = 16 * (128 * nt + ii)[:, None]
        ss_ = 64 * np.arange(128)[None, :]
        ovl[:, nt, :] = ((cs_ < ss_ + 64) & (cs_ + 32 > ss_)).astype(np.float32)
    hi = (ii >= 64).astype(np.int64)[:, None]
    dd = (np.arange(240)[None, :] - 112) - 2 * c
    selbase = np.where((dd == hi) | (dd == hi - 1), 100.0, np.where(dd > hi, -100.0, 0.0)).astype(np.float32)
    onehot = np.zeros((128, 8), np.float32)
    onehot[:, c] = 1.0
    return dict(ropecos=ropecos, ropesin=ropesin, cmask=bf(cm.reshape(128, 1024)), constf=constf, constb=constb,
                wm=bf(wm), swamask=bf(sw.reshape(128, 9 * 128)), gsel=gsel,
                wh=bf(wh), winmask=bf(wn.reshape(128, 12 * 128)), cmpmask=bf(cmpm.reshape(128, 3 * 128)), ovl=bf(ovl.reshape(128, 512)),
                selbase=np.ascontiguousarray(selbase), onehot=onehot)


def make_in_maps(inp, nlayers, need_mlp_last=True):
    x = np.asarray(inp["x"], np.float32)[0]
    shared = {}
    cvec = np.asarray(inp["c"], np.float32)[0]
    shared["cT"] = np.ascontiguousarray(cvec.reshape(16, 128).T)
    gv = np.concatenate([np.asarray(inp["norm_mix_g"], np.float32).reshape(4, 16, 128).transpose(2, 0, 1).reshape(128, 64),
                         np.asarray(inp["norm_mlp_g"], np.float32).reshape(4, 16, 128).transpose(2, 0, 1).reshape(128, 64),
                         np.asarray(inp["final_norm_g"], np.float32).reshape(16, 128).T], 1)
    shared["gvec"] = np.ascontiguousarray(gv)
    sinks = np.asarray(inp["even_sinks"], np.float32)
    shared["sinksb"] = np.ascontiguousarray(np.broadcast_to(sinks.reshape(1, 16), (128, 16)))
    shared["fbias"] = np.ascontiguousarray(np.broadcast_to(np.asarray(inp["fox_forget_b"], np.float32).reshape(1, 16), (128, 16)))
    pos = [np.asarray(inp["nsa_k_pos"], np.float32), np.asarray(inp["nsa_v_pos"], np.float32)]
    shared["posT"] = np.ascontiguousarray(np.concatenate([pos[t][j2].T for j2 in range(2) for t in range(2)], 1))
    for j2 in range(2):
        shared[f"w1_{j2}_0"] = np.asarray(inp["nsa_k_w1"][j2], np.float32)
        shared[f"w1_{j2}_1"] = np.asarray(inp["nsa_v_w1"][j2], np.float32)
        shared[f"w2_{j2}"] = np.ascontiguousarray(np.concatenate([np.asarray(inp["nsa_k_w2"][j2], np.float32), np.asarray(inp["nsa_v_w2"][j2], np.float32)], 1))
    for l in range(nlayers):
        j = l // 2
        if l % 2 == 0:
            fm, tm = host_w_layout(np.asarray(inp["even_w_in"][j], np.float32), even_spec())
            shared[f"wout{l}"] = np.asarray(inp["even_w_out"][j], np.float32)
        else:
            w = np.asarray(inp["odd_w_in"][j], np.float32)
            fm, tm = host_w_layout(w, odd_spec())
            shared[f"wsm{l}"] = np.ascontiguousarray(np.concatenate([w[:, 3072:3080], w[:, 5640:5664]], 1))
            shared[f"wout{l}"] = np.asarray(inp["odd_w_out"][j], np.float32)
        shared[f"wfm{l}"] = fm
        shared[f"wtm{l}"] = tm
        if need_mlp_last or l < nlayers - 1:
            shared[f"wup{l}"] = np.asarray(inp["mlp_up"][l], np.float32)
            shared[f"wdown{l}"] = np.asarray(inp["mlp_down"][l], np.float32)
    ada_w = np.asarray(inp["ada_w"], np.float32)
    ada_b = np.asarray(inp["ada_b"], np.float32)
    in_maps = []
    for c in range(NCORE):
        m = dict(shared)
        tok = tok_index(c)
        m["xT"] = np.ascontiguousarray(x[tok].T.reshape(16, 128, TL))
        m.update(host_consts(c))
        m["adaw"] = np.ascontiguousarray(ada_w[:, :, c * 1536:(c + 1) * 1536])
        m["adabT"] = np.ascontiguousarray(ada_b[:, c * 1536:(c + 1) * 1536].reshape(4, 12, 128).transpose(2, 0, 1).reshape(128, 48))
        in_maps.append(m)
    return in_maps


def _run_phase(ph, in_maps, extra):
    b = Builder(DEPTH, False, True, phase=ph)
    nc = b.build()
    names = set(b._ins.keys())
    maps = []
    for c in range(NCORE):
        m = dict(in_maps[c])
        m.update(extra[c])
        missing = names - set(m.keys())
        assert not missing, missing
        maps.append({k: m[k] for k in names})
    res = run_bass_kernel_spmd(nc, maps, core_ids=list(range(NCORE)))
    return res.results


def _assemble(results):
    out = np.zeros((SEQ, D), np.float32)
    for c in range(NCORE):
        oT = np.asarray(results[c]["outT"]).reshape(D, TL)
        out[tok_index(c)] = oT.T
    return out[None]


def kernel_multi(debug_cb=None, **inputs):
    in_maps = make_in_maps(inputs, DEPTH)
    r = _run_phase("M", in_maps, [{} for _ in range(NCORE)])
    modgath = np.concatenate([np.asarray(r[c]["modsend_out"]) for c in range(NCORE)], 0)
    extra = [{"modgath_in": modgath} for _ in range(NCORE)]
    for ph in range(DEPTH + 1):
        r = _run_phase(ph, in_maps, extra)
        if ph == DEPTH:
            return _assemble(r)
        gath = np.concatenate([np.asarray(r[c][f"send{ph}_out"]) for c in range(NCORE)], 0)
        sgath = np.concatenate([np.asarray(r[c][f"ssend{ph}_out"]) for c in range(NCORE)], 0)
        if debug_cb is not None:
            debug_cb(ph, r)
        extra = [{"modgath_in": modgath, "xstate_in": np.asarray(r[c]["xstate_out"]), "qstate_in": np.asarray(r[c]["qstate_out"]),
                  "gstate_in": np.asarray(r[c]["gstate_out"]), "pstate_in": np.asarray(r[c]["pstate_out"]),
                  f"gath{ph}_in": gath, f"sgath{ph}_in": sgath} for c in range(NCORE)]


def kernel_fused(**inputs):
    b = Builder(DEPTH, False, True)
    in_maps = make_in_maps(inputs, DEPTH)
    nc = b.build()
    names = set(b._ins.keys())
    in_maps = [{k: m[k] for k in names} for m in in_maps]
    res = run_bass_kernel_spmd(nc, in_maps, core_ids=list(range(NCORE)))
    return _assemble(res.results)


def kernel(**inputs):
    return kernel_multi(**inputs)
```
